# Optimizing a Trainium2 kernel written in Bass

```python
import jax, jax.numpy as jnp
from jax import lax
import numpy as np

D_MODEL = 4096
BATCH = 2
SEQ = 8192
DEPTH = 1

NORM_EPS = 1e-6
DN_HEADS = 16
DN_HEAD_DIM = D_MODEL // 32
DN_WIDTH = DN_HEADS * DN_HEAD_DIM
CONV_WIDTH = 4
CHUNK = 64
SWA_Q_HEADS = 16
SWA_KV_HEADS = 4
SWA_HEAD_DIM = D_MODEL // 32
SWA_Q_WIDTH = SWA_Q_HEADS * SWA_HEAD_DIM
SWA_KV_WIDTH = SWA_KV_HEADS * SWA_HEAD_DIM
WINDOW = 128
BLOCK = 128
MIX_WIDTH = DN_WIDTH + SWA_Q_WIDTH
IN_PROJ_WIDTH = 4 * DN_WIDTH + 2 * DN_HEADS + SWA_Q_WIDTH + 2 * SWA_KV_WIDTH
FFN_HIDDEN = -(-8 * D_MODEL // (3 * 256)) * 256

kernel_name = "hybrid_gdn_swa_parallel_heads"


def rms_norm(x, w):
    xf = x.astype(jnp.float32)
    y = xf * lax.rsqrt(jnp.mean(xf * xf, axis=-1, keepdims=True) + NORM_EPS)
    return y * w.astype(jnp.float32)


def l2_norm(x):
    return x * lax.rsqrt(jnp.sum(x * x, axis=-1, keepdims=True) + NORM_EPS)


def alibi_slopes(n_heads):
    return jnp.exp2(-8.0 * jnp.arange(1, n_heads + 1, dtype=jnp.float32) / n_heads)


def causal_conv_silu(x, w):
    k = w.shape[0]
    y = lax.conv_general_dilated(
        x, w[:, None, :].astype(x.dtype), window_strides=(1,), padding=[(k - 1, 0)],
        dimension_numbers=("NWC", "WIO", "NWC"), feature_group_count=x.shape[-1])
    return jax.nn.silu(y)


def chunk_gated_delta_rule(q, k, v, g, beta):
    b, t, h, d = q.shape
    n = t // CHUNK

    def to_chunks(a):
        a = a.reshape((b, n, CHUNK, h) + a.shape[3:])
        return jnp.moveaxis(a, 3, 1)

    q, k, v, g, beta = (to_chunks(a) for a in (q, k, v, g, beta))
    gc = jnp.cumsum(g, axis=-1)
    idx = jnp.arange(CHUNK)
    incl = idx[:, None] >= idx[None, :]
    strict = idx[:, None] > idx[None, :]
    decay = jnp.exp(jnp.where(incl, gc[..., :, None] - gc[..., None, :], -jnp.inf))

    kb = k * beta[..., None]
    lower = jnp.where(strict, jnp.einsum("bhncd,bhnsd->bhncs", kb, k) * decay, 0.0)
    a_mat = lower + jnp.eye(CHUNK, dtype=jnp.float32)
    solve = lambda rhs: lax.linalg.triangular_solve(
        a_mat, rhs, left_side=True, lower=True, unit_diagonal=True)
    u = solve(v * beta[..., None])
    w = solve(kb * jnp.exp(gc)[..., None])

    qk = jnp.einsum("bhncd,bhnsd->bhncs", q, k) * decay
    q_dec = q * jnp.exp(gc)[..., None]
    k_tail = k * jnp.exp(gc[..., -1:] - gc)[..., None]
    g_last = jnp.exp(gc[..., -1])

    xs = tuple(jnp.moveaxis(a, 2, 0) for a in (q_dec, qk, u, w, k_tail, g_last))

    def step(state, inp):
        qd, qk_i, u_i, w_i, kt, gl = inp
        v_new = u_i - jnp.einsum("bhcd,bhde->bhce", w_i, state)
        o = jnp.einsum("bhcd,bhde->bhce", qd, state) + jnp.einsum("bhcs,bhse->bhce", qk_i, v_new)
        state = state * gl[..., None, None] + jnp.einsum("bhcd,bhce->bhde", kt, v_new)
        return state, o

    s0 = jnp.zeros((b, h, d, d), jnp.float32)
    _, o = lax.scan(step, s0, xs)
    return o.transpose(1, 0, 3, 2, 4).reshape(b, t, h, d)


def gated_deltanet(q, k, v, z, a, bt, conv_w, a_log, dt_bias, norm_w):
    bsz, t, _ = q.shape
    qkv = causal_conv_silu(jnp.concatenate([q, k, v], axis=-1), conv_w)
    q, k, v = jnp.split(qkv.astype(jnp.float32), [DN_WIDTH, 2 * DN_WIDTH], axis=-1)
    shp = (bsz, t, DN_HEADS, DN_HEAD_DIM)
    q = l2_norm(q.reshape(shp)) * (DN_HEAD_DIM ** -0.5)
    k = l2_norm(k.reshape(shp))
    v = v.reshape(shp)
    beta = jax.nn.sigmoid(bt.astype(jnp.float32))
    g = -jnp.exp(a_log.astype(jnp.float32)) * jax.nn.softplus(
        a.astype(jnp.float32) + dt_bias.astype(jnp.float32))
    o = chunk_gated_delta_rule(q, k, v, g, beta)
    o = rms_norm(o, norm_w) * jax.nn.silu(z.astype(jnp.float32).reshape(shp))
    return o.reshape(bsz, t, DN_WIDTH)


def sliding_window_gqa(q, k, v, q_norm_w, k_norm_w, sinks):
    bsz, t, _ = q.shape
    grp = SWA_Q_HEADS // SWA_KV_HEADS
    nb = t // BLOCK
    q = rms_norm(q.reshape(bsz, t, SWA_KV_HEADS, grp, SWA_HEAD_DIM), q_norm_w)
    k = rms_norm(k.reshape(bsz, t, SWA_KV_HEADS, SWA_HEAD_DIM), k_norm_w)
    v = v.astype(jnp.float32).reshape(bsz, t, SWA_KV_HEADS, SWA_HEAD_DIM)

    qb = q.reshape(bsz, nb, BLOCK, SWA_KV_HEADS, grp, SWA_HEAD_DIM)

    def with_prev(a):
        a = a.reshape(bsz, nb, BLOCK, SWA_KV_HEADS, SWA_HEAD_DIM)
        prev = jnp.concatenate([jnp.zeros_like(a[:, :1]), a[:, :-1]], axis=1)
        return jnp.concatenate([prev, a], axis=2)

    kk, vv = with_prev(k), with_prev(v)
    s = jnp.einsum("bnqhgd,bnkhd->bnhgqk", qb, kk) * (SWA_HEAD_DIM ** -0.5)

    qpos = jnp.arange(BLOCK) + BLOCK
    kpos = jnp.arange(2 * BLOCK)
    dist = (qpos[:, None] - kpos[None, :])
    key_global = jnp.arange(nb)[:, None] * BLOCK + kpos[None, :] - BLOCK
    valid = ((dist >= 0) & (dist < WINDOW))[None] & (key_global >= 0)[:, None, :]

    slopes = alibi_slopes(SWA_Q_HEADS).reshape(SWA_KV_HEADS, grp)
    s = s - slopes[:, :, None, None] * dist.astype(jnp.float32)
    s = jnp.where(valid[None, :, None, None], s, -jnp.inf)
    sink = jnp.broadcast_to(
        sinks.astype(jnp.float32).reshape(SWA_KV_HEADS, grp)[:, :, None, None],
        s.shape[:-1] + (1,))
    p = jax.nn.softmax(jnp.concatenate([s, sink], axis=-1), axis=-1)[..., :-1]
    o = jnp.einsum("bnhgqk,bnkhd->bnqhgd", p, vv)
    return o.reshape(bsz, t, SWA_Q_WIDTH)


def in_proj_cuts():
    sizes = [DN_WIDTH] * 4 + [DN_HEADS] * 2 + [SWA_Q_WIDTH, SWA_KV_WIDTH, SWA_KV_WIDTH]
    return np.cumsum(sizes)[:-1].tolist()


def setup_inputs(seed: int = 0) -> dict:
    key = jax.random.key(seed)
    ks = jax.random.split(key, 16)
    f32 = jnp.float32
    nrm = lambda k, shape, scale: jax.random.normal(k, shape, f32) * scale
    x = jax.random.normal(ks[0], (BATCH, SEQ, D_MODEL), f32)
    attn_norm_w = 1.0 + nrm(ks[1], (DEPTH, D_MODEL), 0.02)
    w_in = nrm(ks[2], (DEPTH, D_MODEL, IN_PROJ_WIDTH), D_MODEL ** -0.5)
    conv_w = nrm(ks[3], (DEPTH, CONV_WIDTH, 3 * DN_WIDTH), CONV_WIDTH ** -0.5)
    a_log = jnp.log(jax.random.uniform(ks[4], (DEPTH, DN_HEADS), f32, 1.0, 16.0))
    dt = jnp.exp(jax.random.uniform(ks[5], (DEPTH, DN_HEADS), f32, np.log(1e-3), np.log(1e-1)))
    dt_bias = dt + jnp.log(-jnp.expm1(-dt))
    dn_norm_w = 1.0 + nrm(ks[6], (DEPTH, DN_HEAD_DIM), 0.02)
    q_norm_w = 1.0 + nrm(ks[7], (DEPTH, SWA_HEAD_DIM), 0.02)
    k_norm_w = 1.0 + nrm(ks[8], (DEPTH, SWA_HEAD_DIM), 0.02)
    sinks = nrm(ks[9], (DEPTH, SWA_Q_HEADS), 0.5)
    w_out = nrm(ks[10], (DEPTH, MIX_WIDTH, D_MODEL), MIX_WIDTH ** -0.5)
    ffn_norm_w = 1.0 + nrm(ks[11], (DEPTH, D_MODEL), 0.02)
    w_gate = nrm(ks[12], (DEPTH, D_MODEL, FFN_HIDDEN), D_MODEL ** -0.5)
    w_up = nrm(ks[13], (DEPTH, D_MODEL, FFN_HIDDEN), D_MODEL ** -0.5)
    w_down = nrm(ks[14], (DEPTH, FFN_HIDDEN, D_MODEL), FFN_HIDDEN ** -0.5)
    return {"x": x, "attn_norm_w": attn_norm_w, "w_in": w_in, "conv_w": conv_w,
            "a_log": a_log, "dt_bias": dt_bias, "dn_norm_w": dn_norm_w,
            "q_norm_w": q_norm_w, "k_norm_w": k_norm_w, "sinks": sinks, "w_out": w_out,
            "ffn_norm_w": ffn_norm_w, "w_gate": w_gate, "w_up": w_up, "w_down": w_down}


def reference(x, attn_norm_w, w_in, conv_w, a_log, dt_bias, dn_norm_w, q_norm_w, k_norm_w,
              sinks, w_out, ffn_norm_w, w_gate, w_up, w_down):
    dtype = x.dtype
    cuts = in_proj_cuts()
    for l in range(DEPTH):
        h = rms_norm(x, attn_norm_w[l]).astype(dtype)
        proj = jnp.einsum("btd,de->bte", h, w_in[l])
        dq, dk, dv, dz, da, db, sq, sk, sv = jnp.split(proj, cuts, axis=-1)
        o_dn = gated_deltanet(dq, dk, dv, dz, da, db, conv_w[l], a_log[l], dt_bias[l], dn_norm_w[l])
        o_sw = sliding_window_gqa(sq, sk, sv, q_norm_w[l], k_norm_w[l], sinks[l])
        mixed = jnp.concatenate([o_dn, o_sw], axis=-1).astype(dtype)
        x = x + jnp.einsum("bte,ed->btd", mixed, w_out[l])
        h = rms_norm(x, ffn_norm_w[l]).astype(dtype)
        gate = jnp.einsum("btd,df->btf", h, w_gate[l])
        up = jnp.einsum("btd,df->btf", h, w_up[l])
        x = x + jnp.einsum("btf,fd->btd", jax.nn.silu(gate) * up, w_down[l])
    return x
```

```python
import contextlib
import numpy as np
import concourse.bass as bass
import concourse.mybir as mybir
from concourse.bass_utils import run_bass_kernel_spmd

F32 = mybir.dt.float32
BF16 = mybir.dt.bfloat16
ALU = mybir.AluOpType
AF = mybir.ActivationFunctionType
ENGS = ("pe", "act", "dve", "pool", "sp")
EPS = 1e-6
D = 4096
FF = 11008
NFB = FF // 128
NCB = 22


class Buf:
    __slots__ = ("name", "writes", "reads", "sem", "semval", "excl")

    def __init__(self, name, excl=False):
        self.name = name
        self.excl = excl
        self.writes = {}
        self.reads = {}
        self.sem = None
        self.semval = 0


class K:
    def __init__(self, nc):
        self.nc = nc
        self.q = {e: [] for e in ENGS}
        self.cnt = {e: 0 for e in ENGS}
        self.waited = {e: {} for e in ENGS}
        self.semkeys = ["E_" + e for e in ENGS]
        self.relay_sem = "S_relay"
        self.semkeys.append(self.relay_sem)
        self.relay_cnt = 0
        self.tok_out = None
        self.tok_in = None

    def newsem(self, name):
        key = "S%d_%s" % (len(self.semkeys), name)
        self.semkeys.append(key)
        return key

    def _wait(self, eng, evs):
        if eng == "pool":
            comp = {sk: v for sk, v in evs.items()
                    if sk.startswith("E_") and v > 0 and self.waited["pool"].get(sk, 0) < v}
            if comp:
                for sk, v in comp.items():
                    self.waited["pool"][sk] = v
                self._wait("sp", comp)
                self.relay_cnt += 16
                self.q["sp"].append(("dma", (self.tok_out, self.tok_in, self.relay_sem)))
                self.q["pool"].append(("wait", (self.relay_sem, self.relay_cnt)))
            evs = {sk: v for sk, v in evs.items() if not sk.startswith("E_")}
        for sk, v in evs.items():
            if v <= 0 or (eng == "pe" and sk == "E_pe"):
                continue
            if self.waited[eng].get(sk, 0) >= v:
                continue
            self.waited[eng][sk] = v
            self.q[eng].append(("wait", (sk, v)))

    def _deps(self, eng, r, w):
        evs = {}
        for b in r:
            for sk, v in b.writes.items():
                if evs.get(sk, 0) < v:
                    evs[sk] = v
        for b in w:
            for d in (b.writes, b.reads):
                for sk, v in d.items():
                    if evs.get(sk, 0) < v:
                        evs[sk] = v
        self._wait(eng, evs)

    def op(self, eng, meth, r=(), w=(), sig=True, **kw):
        if eng != "pe":
            ex = [b for b in r if b.excl]
            if ex:
                r = [b for b in r if not b.excl]
                w = list(w) + ex
        self._deps(eng, r, w)
        sk = "E_" + eng
        if eng == "pe" and not sig:
            ev = self.cnt[eng] + 1
            inc = False
        else:
            self.cnt[eng] += 1
            ev = self.cnt[eng]
            inc = True
        self.q[eng].append(("op", (meth, kw, sk if inc else None)))
        for b in r:
            if b.reads.get(sk, 0) < ev:
                b.reads[sk] = ev
        for b in w:
            b.writes = {sk: ev}
            b.reads = {}

    def dma(self, eng, out, in_, r=(), w=(), sem_from=None):
        self._deps(eng, r, w)
        tgt = sem_from if sem_from is not None else (w[0] if len(w) else r[0])
        if tgt.sem is None:
            tgt.sem = self.newsem(tgt.name)
        tgt.semval += 16
        sk, v = tgt.sem, tgt.semval
        self.q[eng].append(("dma", (out, in_, sk)))
        for b in r:
            if b.reads.get(sk, 0) < v:
                b.reads[sk] = v
        for b in w:
            b.writes = {sk: v}
            b.reads = {}

    def wait_all(self, eng, bufs):
        evs = {}
        for b in bufs:
            for d in (b.writes, b.reads):
                for sk, v in d.items():
                    if evs.get(sk, 0) < v:
                        evs[sk] = v
        self._wait(eng, evs)

    def raw(self, eng, fn):
        self.q[eng].append(("raw", fn))

    def emit(self):
        nc = self.nc
        with contextlib.ExitStack() as st:
            sems = {}
            for sk in self.semkeys:
                sems[sk] = st.enter_context(nc.semaphore(sk))
            block = st.enter_context(nc.Block())
            q = self.q

            def run(engname, e):
                for kind, p in q[engname]:
                    if kind == "wait":
                        e.wait_ge(sems[p[0]], p[1])
                    elif kind == "op":
                        meth, kw, sk = p
                        ins = getattr(e, meth)(**kw)
                        if sk is not None:
                            ins.then_inc(sems[sk], 1)
                    elif kind == "dma":
                        out, in_, sk = p
                        e.dma_start(out=out, in_=in_).then_inc(sems[sk], 16)
                    else:
                        p(e, sems)

            @block.tensor
            def _(e):
                run("pe", e)

            @block.scalar
            def _(e):
                run("act", e)

            @block.vector
            def _(e):
                run("dve", e)

            @block.gpsimd
            def _(e):
                run("pool", e)

            @block.sync
            def _(e):
                run("sp", e)


class Arena:
    def __init__(self, nc, st, nbytes):
        self.n = nbytes // 2
        self.t = st.enter_context(nc.sbuf_tensor("arena", [128, self.n], BF16))
        self.off = 0

    def alloc(self, nbytes):
        nb_ = (nbytes + 31) // 32 * 32
        o = self.off
        self.off += nb_ // 2
        assert self.off <= self.n, "SBUF arena overflow: %d > %d" % (self.off * 2, self.n * 2)
        return o


class Tl:
    def __init__(self, arena, name, shape, dt):
        nel = 1
        for s in shape[1:]:
            nel *= s
        esz = 4 if dt == F32 else 2
        o = arena.alloc(nel * esz)
        ap = arena.t[:, o:o + nel * esz // 2]
        if dt != BF16:
            ap = ap.bitcast(dt)
        if len(shape) == 3:
            ap = ap.rearrange("p (a b) -> p a b", a=shape[1])
        self.ap = ap
        self.b = Buf(name)

    def __getitem__(self, idx):
        return self.ap[idx]


C_ID, C_ONE, C_UIN, C_LST, C_MST, C_MBD, C_MNB, C_AL = [i * 128 for i in range(8)]
NCONST = C_AL + 8 * 128
P_CW, P_ALOG, P_DTB, P_DNW, P_QNW, P_KNW, P_SNK = 0, 48, 52, 56, 57, 58, 59
NPRM = 64


def build(SEQ, dbg=None, skip=(), np1=None):
    nc = bass.Bass("TRN2", target_bir_lowering=False)
    NP1 = SEQ // 512 if np1 is None else np1
    TOK2 = SEQ // 4
    NP2 = TOK2 // 512
    dram = lambda n, s, d, kind=None: (nc.dram_tensor(n, s, d, kind=kind) if kind else nc.dram_tensor(n, s, d))
    xb = dram("xb", [SEQ, D], F32, "ExternalInput").ap()
    consts = dram("consts", [128, NCONST], F32, "ExternalInput").ap()
    prm = dram("prm", [128, NPRM], F32, "ExternalInput").ap()
    anwbc_d = dram("anwbc", [128, D], F32, "ExternalInput").ap()
    fnwbc_d = dram("fnwbc", [128, D], F32, "ExternalInput").ap()
    w1t = dram("w1t", [NCB, 128, D], F32, "ExternalInput").ap()
    wab_d = dram("wab", [128, 256], F32, "ExternalInput").ap()
    wo_t = dram("wo_t", [8, 128, 32 * 512], F32, "ExternalInput").ap()
    wg_t = dram("wg_t", [NFB, 128, D], F32, "ExternalInput").ap()
    wu_t = dram("wu_t", [NFB, 128, D], F32, "ExternalInput").ap()
    wd_t = dram("wd_t", [4, 128, NFB * 1024], F32, "ExternalInput").ap()
    out_d = dram("out", [TOK2, D], F32, "ExternalOutput").ap()
    x2tok = dram("x2tok", [TOK2, D], F32, "ExternalInput").ap()
    sel_d = dram("sel", [128, 4], F32, "ExternalInput").ap()
    NPT = SEQ // 512
    mix_in_l = [dram("mix_in%d" % p_, [1024, 512], BF16) for p_ in range(NPT)]
    mix_all_l = [dram("mix_all%d" % p_, [4096, 512], BF16) for p_ in range(NPT)]
    dbg_d = {}
    if dbg:
        for name, shape, dt in dbg:
            dbg_d[name] = dram("dbg_" + name, shape, dt, "ExternalOutput").ap()

    k = K(nc)
    tok_d = dram("tok_d", [1, 16], F32)
    k.tok_out = tok_d.ap()
    k.tok_in = consts[0:1, 0:16]
    b_mixin_l = [Buf("mixin%d" % p_) for p_ in range(NPT)]
    b_mixall_l = [Buf("mixall%d" % p_) for p_ in range(NPT)]
    b_out = Buf("outd")

    with contextlib.ExitStack() as st0:
        banks = []
        for i in range(8):
            t = st0.enter_context(nc.psum_tensor("bank%d" % i, [128, 512], F32))
            banks.append((t, Buf("bank%d" % i, excl=True)))
        rot = [0]

        def nb():
            i = rot[0] % 8
            rot[0] += 1
            return banks[i]

        arena = Arena(nc, st0, 206 * 1024)
        cst = Tl(arena, "cst", [128, NCONST], F32)
        prt = Tl(arena, "prt", [128, NPRM], F32)
        idb = Tl(arena, "idb", [128, 128], BF16)
        oneb = Tl(arena, "oneb", [128, 128], BF16)
        arena_mark = arena.off
        k.dma("sp", cst[:], consts, w=[cst.b])
        k.dma("sp", prt[:], prm, w=[prt.b])
        k.op("dve", "tensor_copy", r=[cst.b], w=[idb.b], out=idb[:], in_=cst[:, C_ID:C_ID + 128])
        k.op("dve", "tensor_copy", r=[cst.b], w=[oneb.b], out=oneb[:], in_=cst[:, C_ONE:C_ONE + 128])
        ident = cst[:, C_ID:C_ID + 128]
        ones_f = cst[:, C_ONE:C_ONE + 128]

        def dbg_store(name, ap_dram_idx, src_ap, srcbuf):
            if name in dbg_d:
                k.dma("sp", dbg_d[name][ap_dram_idx], src_ap, r=[srcbuf])
                dbg_bufs.append(srcbuf)
        dbg_bufs = []

        with contextlib.ExitStack() as st:
            p1_tiles = []

            def T(n, s, d):
                t = Tl(arena, n, s, d)
                p1_tiles.append(t)
                return t
            xin = [T("xin%d" % i, [128, D], F32) for i in range(2)]
            xs = T("xs", [128, D], BF16)
            anwbc = T("anwbc_s", [128, D], F32)
            hT = T("hT", [128, 32, 512], BF16)
            hTb = [Buf("hT%d" % m) for m in range(4)]
            NW = 3
            wsl = [T("wsl%d" % i, [128, D], BF16) for i in range(NW)]
            wabf = T("wabf", [128, 256], F32)
            wabb = T("wabb", [128, 256], BF16)
            ssq = T("ssq", [128, 1], F32)
            rstd = T("rstd", [128, 1], F32)
            nA = T("nA", [128, 4], F32)
            dnw2 = T("dnw2", [128, 1], F32)
            knw2 = T("knw2", [128, 1], F32)
            esink = T("esink", [128, 4], F32)
            k.dma("sp", anwbc[:], anwbc_d, w=[anwbc.b])
            k.dma("sp", wabf[:], wab_d, w=[wabf.b])
            k.op("dve", "tensor_copy", r=[wabf.b], w=[wabb.b], out=wabb[:], in_=wabf[:])
            k.op("act", "activation", r=[prt.b], w=[nA.b], out=nA[:], in_=prt[:, P_ALOG:P_ALOG + 4], func=AF.Exp)
            k.op("dve", "tensor_scalar", r=[nA.b], w=[nA.b], out=nA[:], in0=nA[:], scalar1=-1.0, scalar2=None,
                 op0=ALU.mult)
            k.op("act", "activation", r=[prt.b], w=[esink.b], out=esink[:], in_=prt[:, P_SNK:P_SNK + 4], func=AF.Exp)
            k.op("dve", "tensor_scalar", r=[prt.b], w=[dnw2.b], out=dnw2[:], in0=prt[:, P_DNW:P_DNW + 1],
                 scalar1=float(np.sqrt(128.0)), scalar2=None, op0=ALU.mult)
            k.op("dve", "tensor_scalar", r=[prt.b], w=[knw2.b], out=knw2[:], in0=prt[:, P_KNW:P_KNW + 1],
                 scalar1=float(np.sqrt(128.0)), scalar2=None, op0=ALU.mult)

            cb_order = []
            for i in range(4):
                cb_order += [i, 4 + i, 8 + i, 12 + i]
            cb_order += [16, 17, 18, 19, 20, 21]
            wseq = [(p, cb) for p in range(NP1) for cb in cb_order]
            wpos = {pc: n for n, pc in enumerate(wseq)}
            wstate = {"issued": 0}

            def wprefetch(upto):
                while wstate["issued"] <= min(upto, len(wseq) - 1):
                    i = wstate["issued"]
                    s = wsl[i % NW]
                    k.dma("pool", s[:], w1t[wseq[i][1]], w=[s.b])
                    wstate["issued"] += 1

            def inproj(p, cb):
                i = wpos[(p, cb)]
                wprefetch(i + NW - 1)
                s = wsl[i % NW]
                bt, bb = nb()
                for kk in range(32):
                    k.op("pe", "matmul", r=[s.b] + hTb, w=[bb], sig=(kk == 31), out=bt[:, 0:512],
                         lhsT=s[:, kk * 128:(kk + 1) * 128], rhs=hT[:, kk, :], start=(kk == 0), stop=(kk == 31))
                return bt, bb

            raw = T("raw", [128, 515], F32)
            halo = [[T("halo%d_%d" % (t_, i), [128, 3], F32) for i in range(4)] for t_ in range(3)]
            for t_ in range(3):
                for i in range(4):
                    k.op("dve", "memset", w=[halo[t_][i].b], ap=halo[t_][i][:], constant=0.0)
            acc = T("acc", [128, 512], F32)
            th = T("th", [128, 512], F32)
            sqb = T("sqb", [128, 512], BF16)
            rr = T("rr", [128, 512], F32)
            qn = [T("qn%d" % i, [128, 512], F32) for i in range(2)]
            qnb = [T("qnb%d" % i, [128, 512], BF16) for i in range(2)]
            knb = [T("knb%d" % i, [128, 512], BF16) for i in range(2)]
            v2b = T("v2b", [128, 512], BF16)
            vtok = [T("vtok%d" % i, [128, 512], BF16) for i in range(2)]
            z2 = [T("z2_%d" % i, [128, 512], F32) for i in range(2)]
            oT = [T("oT%d" % i, [128, 512], F32) for i in range(2)]
            mixt = [T("mixt%d" % i, [128, 512], BF16) for i in range(2)]
            Sf = [T("Sf%d" % i, [128, 128], F32) for i in range(4)]
            Sb = [T("Sb%d" % i, [128, 128], BF16) for i in range(4)]
            for i in range(4):
                k.op("dve", "memset", w=[Sf[i].b], ap=Sf[i][:], constant=0.0)
                k.op("dve", "memset", w=[Sb[i].b], ap=Sb[i][:], constant=0.0)
            gnames = ["tA", "eA", "sp", "g", "tb", "beta", "nbeta", "gcc", "egc", "tail"]
            gt = [{n: T("g_%s%d" % (n, j), [128, 4], F32) for n in gnames} for j in range(4)]
            f32names = ["Gh", "Dt", "E", "egb", "EMi", "EMs", "Nf", "A0", "A0T", "Nof", "NofT", "P0", "P1",
                        "A1", "A1T", "XT", "Y"]
            bfnames = ["PT", "T2T", "qd", "ktl", "r2n", "vnw"]
            hb = [dict([(n, T("hb_%s%d" % (n, par), [128, 128], F32)) for n in f32names] +
                       [(n, T("hb_%s%d" % (n, par), [128, 128], BF16)) for n in bfnames]) for par in range(2)]
            sraw = T("sraw", [128, 512], F32)
            sqn = [T("sqn%d" % h, [128, 512], BF16) for h in range(4)]
            skn = T("skn", [128, 640], BF16)
            svb = T("svb", [128, 512], BF16)
            svt = T("svt", [128, 5, 128], BF16)
            smix = [T("smix%d" % i, [128, 512], BF16) for i in range(2)]
            sw = [dict(tc=T("sw_tc%d" % par, [128, 128], F32), tp=T("sw_tp%d" % par, [128, 128], F32),
                       pc=T("sw_pc%d" % par, [128, 128], BF16), pp=T("sw_pp%d" % par, [128, 128], BF16),
                       rd=T("sw_rd%d" % par, [128, 128], F32)) for par in range(2)]

            cwcol = lambda t_, i, j: prt[:, P_CW + (t_ * 4 + i) * 4 + j:P_CW + (t_ * 4 + i) * 4 + j + 1]
            hbcount = [0]
            swcount = [0]

            for p in range(NP1):
                for m in range(4):
                    xi = xin[(4 * p + m) % 2]
                    r0 = p * 512 + m * 128
                    k.dma("sp", xi[:], xb[r0:r0 + 128, :], w=[xi.b])
                    k.op("act", "activation", r=[xi.b], w=[xs.b, ssq.b], out=xs[:], in_=xi[:], func=AF.Square,
                         accum_out=ssq[:, 0:1])
                    k.op("act", "activation", r=[ssq.b], w=[rstd.b], out=rstd[:], in_=ssq[:], func=AF.Ln,
                         scale=1.0 / D, bias=EPS)
                    k.op("act", "activation", r=[rstd.b], w=[rstd.b], out=rstd[:], in_=rstd[:], func=AF.Exp,
                         scale=-0.5)
                    k.op("dve", "scalar_tensor_tensor", r=[xi.b, rstd.b, anwbc.b], w=[xs.b], out=xs[:], in0=xi[:],
                         scalar=rstd[:, 0:1], in1=anwbc[:], op0=ALU.mult, op1=ALU.mult)
                    for kq in range(8):
                        bt, bb = nb()
                        btb = bt[:].bitcast(BF16)
                        for j in range(4):
                            kk = kq * 4 + j
                            k.op("pe", "transpose", r=[xs.b, idb.b], w=[bb], sig=(j == 3),
                                 out=btb[:, j * 128:(j + 1) * 128], in_=xs[:, kk * 128:(kk + 1) * 128],
                                 identity=idb[:])
                        src = btb[:, 0:512].rearrange("p (a b) -> p a b", a=4)
                        dst = hT[:, kq * 4:(kq + 1) * 4, m * 128:(m + 1) * 128]
                        if kq % 2 == 0:
                            k.op("act", "activation", r=[bb], w=[hTb[m]], out=dst, in_=src, func=AF.Copy)
                        else:
                            k.op("dve", "tensor_copy", r=[bb], w=[hTb[m]], out=dst, in_=src)
                if p == 0 and "hT" in dbg_d:
                    k.dma("sp", dbg_d["hT"], hT[:], r=hTb)
                    dbg_bufs.extend(hTb)

                for j in range(4):
                    G = gt[j]
                    bt, bb = nb()
                    for kk in range(32):
                        k.op("pe", "matmul", r=[wabb.b] + hTb, w=[bb], sig=(kk == 31), out=bt[:, 0:8],
                             lhsT=hT[:, kk, j * 128:(j + 1) * 128], rhs=wabb[:, kk * 8:(kk + 1) * 8],
                             start=(kk == 0), stop=(kk == 31))
                    k.op("dve", "tensor_tensor", r=[bb, prt.b], w=[G["tA"].b], out=G["tA"][:], in0=bt[:, 0:4],
                         in1=prt[:, P_DTB:P_DTB + 4], op=ALU.add)
                    k.op("act", "activation", r=[bb], w=[G["tb"].b], out=G["tb"][:], in_=bt[:, 4:8], func=AF.Exp,
                         scale=-1.0)
                    k.op("act", "activation", r=[G["tb"].b], w=[G["tb"].b], out=G["tb"][:], in_=G["tb"][:],
                         func=AF.Ln, bias=1.0)
                    k.op("act", "activation", r=[G["tb"].b], w=[G["beta"].b], out=G["beta"][:], in_=G["tb"][:],
                         func=AF.Exp, scale=-1.0)
                    k.op("act", "activation", r=[G["tA"].b], w=[G["eA"].b], out=G["eA"][:], in_=G["tA"][:],
                         func=AF.Exp)
                    k.op("act", "activation", r=[G["eA"].b], w=[G["sp"].b], out=G["sp"][:], in_=G["eA"][:],
                         func=AF.Ln, bias=1.0)
                    k.op("dve", "tensor_tensor", r=[G["sp"].b, nA.b], w=[G["g"].b], out=G["g"][:], in0=G["sp"][:],
                         in1=nA[:], op=ALU.mult)
                    k.op("dve", "tensor_scalar", r=[G["beta"].b], w=[G["nbeta"].b], out=G["nbeta"][:],
                         in0=G["beta"][:], scalar1=-1.0, scalar2=None, op0=ALU.mult)
                    bt2, bb2 = nb()
                    k.op("pe", "matmul", r=[cst.b, G["g"].b], w=[bb2], out=bt2[:, 0:4],
                         lhsT=cst[:, C_UIN:C_UIN + 128], rhs=G["g"][:], start=True, stop=True)
                    k.op("act", "activation", r=[bb2], w=[G["gcc"].b], out=G["gcc"][:], in_=bt2[:, 0:4],
                         func=AF.Copy)
                    k.op("act", "activation", r=[bb2], w=[G["egc"].b], out=G["egc"][:], in_=bt2[:, 0:4],
                         func=AF.Exp)
                    bt3, bb3 = nb()
                    k.op("pe", "matmul", r=[cst.b, G["g"].b], w=[bb3], out=bt3[:, 0:4],
                         lhsT=cst[:, C_LST:C_LST + 128], rhs=G["g"][:], start=True, stop=True)
                    k.op("act", "activation", r=[bb3], w=[G["tail"].b], out=G["tail"][:], in_=bt3[:, 0:4],
                         func=AF.Exp)

                for i in range(0 if "gdn" in skip else 4):
                    par = i % 2
                    for t_ in range(3):
                        bt, bb = inproj(p, t_ * 4 + i)
                        k.op("dve", "tensor_copy", r=[halo[t_][i].b], w=[raw.b], out=raw[:, 0:3],
                             in_=halo[t_][i][:])
                        k.op("act", "activation", r=[bb], w=[raw.b], out=raw[:, 3:515], in_=bt[:, 0:512],
                             func=AF.Copy)
                        k.op("dve", "tensor_scalar", r=[raw.b, prt.b], w=[acc.b], out=acc[:], in0=raw[:, 0:512],
                             scalar1=cwcol(t_, i, 0), scalar2=None, op0=ALU.mult)
                        for j in range(1, 4):
                            k.op("dve", "scalar_tensor_tensor", r=[raw.b, prt.b, acc.b], w=[acc.b], out=acc[:],
                                 in0=raw[:, j:j + 512], scalar=cwcol(t_, i, j), in1=acc[:], op0=ALU.mult,
                                 op1=ALU.add)
                        k.op("dve", "tensor_copy", r=[raw.b], w=[halo[t_][i].b], out=halo[t_][i][:],
                             in_=raw[:, 512:515])
                        k.op("act", "activation", r=[acc.b], w=[th.b], out=th[:], in_=acc[:], func=AF.Exp,
                             scale=-1.0)
                        k.op("act", "activation", r=[th.b], w=[th.b], out=th[:], in_=th[:], func=AF.Ln, bias=1.0)
                        k.op("act", "activation", r=[th.b], w=[th.b], out=th[:], in_=th[:], func=AF.Exp, scale=-1.0)
                        k.op("dve", "tensor_tensor", r=[th.b, acc.b], w=[th.b], out=th[:], in0=th[:], in1=acc[:],
                             op=ALU.mult)
                        if t_ < 2:
                            k.op("act", "activation", r=[th.b], w=[sqb.b], out=sqb[:], in_=th[:], func=AF.Square)
                            bt2, bb2 = nb()
                            k.op("pe", "matmul", r=[oneb.b, sqb.b], w=[bb2], out=bt2[:, 0:512], lhsT=oneb[:],
                                 rhs=sqb[:], start=True, stop=True)
                            k.op("act", "activation", r=[bb2], w=[rr.b], out=rr[:], in_=bt2[:, 0:512], func=AF.Ln,
                                 bias=EPS)
                            k.op("act", "activation", r=[rr.b], w=[rr.b], out=rr[:], in_=rr[:], func=AF.Exp,
                                 scale=-0.5)
                            if t_ == 0:
                                k.op("dve", "scalar_tensor_tensor", r=[th.b, rr.b], w=[qn[par].b], out=qn[par][:],
                                     in0=th[:], scalar=float(128.0 ** -0.5), in1=rr[:], op0=ALU.mult,
                                     op1=ALU.mult)
                                k.op("act", "activation", r=[qn[par].b], w=[qnb[par].b], out=qnb[par][:],
                                     in_=qn[par][:], func=AF.Copy)
                            else:
                                k.op("dve", "tensor_tensor", r=[th.b, rr.b], w=[knb[par].b], out=knb[par][:],
                                     in0=th[:], in1=rr[:], op=ALU.mult)
                        else:
                            k.op("act", "activation", r=[th.b], w=[v2b.b], out=v2b[:], in_=th[:], func=AF.Copy)
                            bt2, bb2 = nb()
                            btb = bt2[:].bitcast(BF16)
                            for j in range(4):
                                k.op("pe", "transpose", r=[v2b.b, idb.b], w=[bb2], sig=(j == 3),
                                     out=btb[:, j * 128:(j + 1) * 128], in_=v2b[:, j * 128:(j + 1) * 128],
                                     identity=idb[:])
                            k.op("act", "activation", r=[bb2], w=[vtok[par].b], out=vtok[par][:],
                                 in_=btb[:, 0:512], func=AF.Copy)
                    bt, bb = inproj(p, 12 + i)
                    k.op("act", "activation", r=[bb], w=[th.b], out=th[:], in_=bt[:, 0:512], func=AF.Exp, scale=-1.0)
                    k.op("act", "activation", r=[th.b], w=[th.b], out=th[:], in_=th[:], func=AF.Ln, bias=1.0)
                    k.op("act", "activation", r=[th.b], w=[th.b], out=th[:], in_=th[:], func=AF.Exp, scale=-1.0)
                    k.op("dve", "tensor_tensor", r=[th.b, bb], w=[z2[par].b], out=z2[par][:], in0=th[:],
                         in1=bt[:, 0:512], op=ALU.mult)

                    for j in range(4):
                        H = hb[hbcount[0] % 2]
                        hbcount[0] += 1
                        G = gt[j]
                        cs = slice(j * 128, (j + 1) * 128)
                        gcol = G["g"][:, i:i + 1]
                        k.op("dve", "tensor_scalar", r=[cst.b, G["g"].b], w=[H["Gh"].b], out=H["Gh"][:], in0=ones_f,
                             scalar1=gcol, scalar2=None, op0=ALU.mult)
                        btg, bbg = nb()
                        k.op("pe", "matmul", r=[H["Gh"].b, cst.b], w=[bbg], out=btg[:, 0:128], lhsT=H["Gh"][:],
                             rhs=cst[:, C_UIN:C_UIN + 128], start=True, stop=True)
                        k.op("dve", "tensor_scalar", r=[bbg, G["gcc"].b], w=[H["Dt"].b], out=H["Dt"][:],
                             in0=btg[:, 0:128], scalar1=G["gcc"][:, i:i + 1], scalar2=0.0, op0=ALU.subtract,
                             op1=ALU.min)
                        k.op("act", "activation", r=[bbg], w=[H["egb"].b], out=H["egb"][:], in_=btg[:, 0:128],
                             func=AF.Exp)
                        k.op("act", "activation", r=[H["Dt"].b], w=[H["E"].b], out=H["E"][:], in_=H["Dt"][:],
                             func=AF.Exp)
                        k.op("dve", "tensor_tensor", r=[H["E"].b, cst.b], w=[H["EMi"].b], out=H["EMi"][:],
                             in0=H["E"][:], in1=cst[:, C_UIN:C_UIN + 128], op=ALU.mult)
                        k.op("dve", "tensor_tensor", r=[H["E"].b, cst.b], w=[H["EMs"].b], out=H["EMs"][:],
                             in0=H["E"][:], in1=cst[:, C_MST:C_MST + 128], op=ALU.mult)
                        k.op("dve", "tensor_tensor", r=[qn[par].b, H["egb"].b], w=[H["qd"].b], out=H["qd"][:],
                             in0=qn[par][:, cs], in1=H["egb"][:], op=ALU.mult)
                        btk, bbk = nb()
                        k.op("pe", "matmul", r=[knb[par].b], w=[bbk], out=btk[:, 0:128], lhsT=knb[par][:, cs],
                             rhs=knb[par][:, cs], start=True, stop=True)
                        k.op("dve", "scalar_tensor_tensor", r=[bbk, G["beta"].b, H["EMs"].b], w=[H["Nf"].b],
                             out=H["Nf"][:], in0=btk[:, 0:128], scalar=G["beta"][:, i:i + 1], in1=H["EMs"][:],
                             op0=ALU.mult, op1=ALU.mult)
                        btq, bbq = nb()
                        k.op("pe", "matmul", r=[knb[par].b, qnb[par].b], w=[bbq], out=btq[:, 0:128],
                             lhsT=knb[par][:, cs], rhs=qnb[par][:, cs], start=True, stop=True)
                        k.op("dve", "tensor_tensor", r=[bbq, H["EMi"].b], w=[H["PT"].b], out=H["PT"][:],
                             in0=btq[:, 0:128], in1=H["EMi"][:], op=ALU.mult)
                        btt, bbt = nb()
                        bttb = btt[:].bitcast(BF16)
                        k.op("pe", "transpose", r=[knb[par].b, idb.b], w=[bbt], out=bttb[:, 0:128],
                             in_=knb[par][:, cs], identity=idb[:])
                        k.op("act", "activation", r=[bbt, G["tail"].b], w=[H["ktl"].b], out=H["ktl"][:],
                             in_=bttb[:, 0:128], func=AF.Copy, scale=G["tail"][:, i:i + 1])
                        k.op("dve", "tensor_tensor", r=[H["Nf"].b, cst.b], w=[H["A0"].b], out=H["A0"][:],
                             in0=H["Nf"][:], in1=cst[:, C_MBD:C_MBD + 128], op=ALU.mult)
                        k.op("dve", "tensor_tensor", r=[H["Nf"].b, cst.b], w=[H["Nof"].b], out=H["Nof"][:],
                             in0=H["Nf"][:], in1=cst[:, C_MNB:C_MNB + 128], op=ALU.mult)
                        btn, bbn = nb()
                        k.op("pe", "transpose", r=[H["Nf"].b, cst.b], w=[bbn], out=btn[:, 0:128], in_=H["Nf"][:],
                             identity=ident)
                        k.op("dve", "tensor_tensor", r=[bbn, cst.b], w=[H["A0T"].b], out=H["A0T"][:],
                             in0=btn[:, 0:128], in1=cst[:, C_MBD:C_MBD + 128], op=ALU.mult)
                        k.op("dve", "tensor_tensor", r=[bbn, cst.b], w=[H["NofT"].b], out=H["NofT"][:],
                             in0=btn[:, 0:128], in1=cst[:, C_MNB:C_MNB + 128], op=ALU.mult)
                        k.op("dve", "tensor_tensor", r=[cst.b, H["A0"].b], w=[H["P0"].b], out=H["P0"][:], in0=ident,
                             in1=H["A0"][:], op=ALU.subtract)
                        A, AT, Pc = H["A0"], H["A0T"], H["P0"]
                        An, ATn, Pn = H["A1"], H["A1T"], H["P1"]
                        for lev in range(5):
                            last = (lev == 4)
                            b1, bb1 = nb()
                            k.op("pe", "matmul", r=[A.b, AT.b], w=[bb1], out=b1[:, 0:128], lhsT=A[:], rhs=AT[:],
                                 start=True, stop=True)
                            if not last:
                                b2, bb2 = nb()
                                k.op("pe", "matmul", r=[A.b, AT.b], w=[bb2], out=b2[:, 0:128], lhsT=AT[:], rhs=A[:],
                                     start=True, stop=True)
                            k.op("act", "activation", r=[bb1], w=[ATn.b], out=ATn[:], in_=b1[:, 0:128], func=AF.Copy)
                            if not last:
                                k.op("act", "activation", r=[bb2], w=[An.b], out=An[:], in_=b2[:, 0:128],
                                     func=AF.Copy)
                            b3, bb3 = nb()
                            k.op("pe", "matmul", r=[ATn.b, Pc.b], w=[bb3], out=b3[:, 0:128], lhsT=ATn[:], rhs=Pc[:],
                                 start=True, stop=True)
                            k.op("dve", "tensor_tensor", r=[bb3, Pc.b], w=[Pn.b], out=Pn[:], in0=b3[:, 0:128],
                                 in1=Pc[:], op=ALU.add)
                            A, An = An, A
                            AT, ATn = ATn, AT
                            Pc, Pn = Pn, Pc
                        X = Pc
                        b1, bb1 = nb()
                        k.op("pe", "transpose", r=[X.b, cst.b], w=[bb1], out=b1[:, 0:128], in_=X[:], identity=ident)
                        k.op("act", "activation", r=[bb1], w=[H["XT"].b], out=H["XT"][:], in_=b1[:, 0:128],
                             func=AF.Copy)
                        b2, bb2 = nb()
                        k.op("pe", "matmul", r=[H["NofT"].b, X.b], w=[bb2], out=b2[:, 0:128], lhsT=H["NofT"][:],
                             rhs=X[:], start=True, stop=True)
                        k.op("act", "activation", r=[bb2], w=[H["Y"].b], out=H["Y"][:], in_=b2[:, 0:128],
                             func=AF.Copy)
                        b3, bb3 = nb()
                        k.op("pe", "matmul", r=[H["XT"].b, H["Y"].b], w=[bb3], out=b3[:, 0:128], lhsT=H["XT"][:],
                             rhs=H["Y"][:], start=True, stop=True)
                        k.op("dve", "tensor_tensor", r=[X.b, bb3], w=[H["T2T"].b], out=H["T2T"][:], in0=X[:],
                             in1=b3[:, 0:128], op=ALU.subtract)

                        b4, bb4 = nb()
                        k.op("pe", "matmul", r=[knb[par].b, Sb[i].b], w=[bb4], out=b4[:, 0:128], lhsT=knb[par][:, cs],
                             rhs=Sb[i][:], start=True, stop=True)
                        k.op("dve", "scalar_tensor_tensor", r=[bb4, G["egc"].b, vtok[par].b], w=[H["r2n"].b],
                             out=H["r2n"][:], in0=b4[:, 0:128], scalar=G["egc"][:, i:i + 1], in1=vtok[par][:, cs],
                             op0=ALU.mult, op1=ALU.subtract)
                        b5, bb5 = nb()
                        k.op("pe", "matmul", r=[H["T2T"].b, H["r2n"].b], w=[bb5], out=b5[:, 0:128], lhsT=H["T2T"][:],
                             rhs=H["r2n"][:], start=True, stop=True)
                        k.op("act", "activation", r=[bb5, G["nbeta"].b], w=[H["vnw"].b], out=H["vnw"][:],
                             in_=b5[:, 0:128], func=AF.Copy, scale=G["nbeta"][:, i:i + 1])
                        b6, bb6 = nb()
                        k.op("pe", "matmul", r=[Sb[i].b, H["qd"].b], w=[bb6], sig=False, out=b6[:, 0:128],
                             lhsT=Sb[i][:], rhs=H["qd"][:], start=True, stop=False)
                        k.op("pe", "matmul", r=[H["vnw"].b, H["PT"].b], w=[bb6], out=b6[:, 0:128], lhsT=H["vnw"][:],
                             rhs=H["PT"][:], start=False, stop=True)
                        k.op("act", "activation", r=[bb6], w=[oT[par].b], out=oT[par][:, cs], in_=b6[:, 0:128],
                             func=AF.Copy)
                        b7, bb7 = nb()
                        k.op("pe", "matmul", r=[H["ktl"].b, H["vnw"].b], w=[bb7], out=b7[:, 0:128], lhsT=H["ktl"][:],
                             rhs=H["vnw"][:], start=True, stop=True)
                        k.op("dve", "scalar_tensor_tensor", r=[Sf[i].b, H["egb"].b, bb7], w=[Sf[i].b], out=Sf[i][:],
                             in0=Sf[i][:], scalar=H["egb"][:, 127:128], in1=b7[:, 0:128], op0=ALU.mult, op1=ALU.add)
                        k.op("act", "activation", r=[Sf[i].b], w=[Sb[i].b], out=Sb[i][:], in_=Sf[i][:], func=AF.Copy)

                    k.op("act", "activation", r=[oT[par].b], w=[sqb.b], out=sqb[:], in_=oT[par][:], func=AF.Square)
                    bt2, bb2 = nb()
                    k.op("pe", "matmul", r=[oneb.b, sqb.b], w=[bb2], out=bt2[:, 0:512], lhsT=oneb[:], rhs=sqb[:],
                         start=True, stop=True)
                    k.op("act", "activation", r=[bb2], w=[rr.b], out=rr[:], in_=bt2[:, 0:512], func=AF.Ln, bias=128.0 * EPS)
                    k.op("act", "activation", r=[rr.b], w=[rr.b], out=rr[:], in_=rr[:], func=AF.Exp, scale=-0.5)
                    k.op("dve", "tensor_tensor", r=[oT[par].b, rr.b], w=[rr.b], out=rr[:], in0=oT[par][:], in1=rr[:],
                         op=ALU.mult)
                    k.op("dve", "scalar_tensor_tensor", r=[rr.b, dnw2.b, z2[par].b], w=[mixt[par].b],
                         out=mixt[par][:], in0=rr[:], scalar=dnw2[:, 0:1], in1=z2[par][:], op0=ALU.mult,
                         op1=ALU.mult)
                    k.dma("sp", mix_in_l[p].ap()[i * 128:(i + 1) * 128, :], mixt[par][:],
                          r=[mixt[par].b], w=[b_mixin_l[p]], sem_from=mixt[par].b)

                if "swa" in skip:
                    continue
                for h in range(4):
                    bt, bb = inproj(p, 16 + h)
                    k.op("act", "activation", r=[bb], w=[sraw.b], out=sraw[:], in_=bt[:, 0:512], func=AF.Copy)
                    k.op("act", "activation", r=[sraw.b], w=[sqb.b], out=sqb[:], in_=sraw[:], func=AF.Square)
                    bt2, bb2 = nb()
                    k.op("pe", "matmul", r=[oneb.b, sqb.b], w=[bb2], out=bt2[:, 0:512], lhsT=oneb[:], rhs=sqb[:],
                         start=True, stop=True)
                    k.op("act", "activation", r=[bb2], w=[rr.b], out=rr[:], in_=bt2[:, 0:512], func=AF.Ln, bias=128.0 * EPS)
                    k.op("act", "activation", r=[rr.b], w=[rr.b], out=rr[:], in_=rr[:], func=AF.Exp, scale=-0.5)
                    k.op("dve", "scalar_tensor_tensor", r=[sraw.b, prt.b, rr.b], w=[sqn[h].b], out=sqn[h][:],
                         in0=sraw[:], scalar=prt[:, P_QNW:P_QNW + 1], in1=rr[:], op0=ALU.mult, op1=ALU.mult)
                bt, bb = inproj(p, 20)
                k.op("act", "activation", r=[bb], w=[sraw.b], out=sraw[:], in_=bt[:, 0:512], func=AF.Copy)
                k.op("act", "activation", r=[sraw.b], w=[sqb.b], out=sqb[:], in_=sraw[:], func=AF.Square)
                bt2, bb2 = nb()
                k.op("pe", "matmul", r=[oneb.b, sqb.b], w=[bb2], out=bt2[:, 0:512], lhsT=oneb[:], rhs=sqb[:],
                     start=True, stop=True)
                k.op("act", "activation", r=[bb2], w=[rr.b], out=rr[:], in_=bt2[:, 0:512], func=AF.Ln, bias=128.0 * EPS)
                k.op("act", "activation", r=[rr.b], w=[rr.b], out=rr[:], in_=rr[:], func=AF.Exp, scale=-0.5)
                if p > 0:
                    k.op("dve", "tensor_copy", r=[skn.b], w=[skn.b], out=skn[:, 0:128], in_=skn[:, 512:640])
                    k.op("dve", "tensor_copy", r=[svt.b], w=[svt.b], out=svt[:, 0, :], in_=svt[:, 4, :])
                k.op("dve", "scalar_tensor_tensor", r=[sraw.b, knw2.b, rr.b], w=[skn.b], out=skn[:, 128:640],
                     in0=sraw[:], scalar=knw2[:, 0:1], in1=rr[:], op0=ALU.mult, op1=ALU.mult)
                bt, bb = inproj(p, 21)
                k.op("act", "activation", r=[bb], w=[svb.b], out=svb[:], in_=bt[:, 0:512], func=AF.Copy)
                bt2, bb2 = nb()
                btb = bt2[:].bitcast(BF16)
                for j in range(4):
                    k.op("pe", "transpose", r=[svb.b, idb.b], w=[bb2], sig=(j == 3),
                         out=btb[:, j * 128:(j + 1) * 128], in_=svb[:, j * 128:(j + 1) * 128], identity=idb[:])
                k.op("act", "activation", r=[bb2], w=[svt.b], out=svt[:, 1:5, :],
                     in_=btb[:, 0:512].rearrange("p (a b) -> p a b", a=4), func=AF.Copy)
                for h in range(4):
                    sm = smix[h % 2]
                    for j in range(4):
                        W = sw[swcount[0] % 2]
                        swcount[0] += 1
                        gblk = p * 4 + j
                        cs = slice(j * 128, (j + 1) * 128)
                        has_prev = gblk > 0
                        b1, bb1 = nb()
                        k.op("pe", "matmul", r=[skn.b, sqn[h].b], w=[bb1], out=b1[:, 0:128],
                             lhsT=skn[:, 128 + j * 128:256 + j * 128], rhs=sqn[h][:, cs], start=True, stop=True)
                        k.op("dve", "tensor_tensor", r=[bb1, cst.b], w=[W["tc"].b], out=W["tc"][:], in0=b1[:, 0:128],
                             in1=cst[:, C_AL + (2 * h) * 128:C_AL + (2 * h + 1) * 128], op=ALU.add)
                        k.op("act", "activation", r=[W["tc"].b], w=[W["pc"].b], out=W["pc"][:], in_=W["tc"][:],
                             func=AF.Exp)
                        if has_prev:
                            b2, bb2 = nb()
                            k.op("pe", "matmul", r=[skn.b, sqn[h].b], w=[bb2], out=b2[:, 0:128],
                                 lhsT=skn[:, j * 128:128 + j * 128], rhs=sqn[h][:, cs], start=True, stop=True)
                            k.op("dve", "tensor_tensor", r=[bb2, cst.b], w=[W["tp"].b], out=W["tp"][:],
                                 in0=b2[:, 0:128], in1=cst[:, C_AL + (2 * h + 1) * 128:C_AL + (2 * h + 2) * 128],
                                 op=ALU.add)
                            k.op("act", "activation", r=[W["tp"].b], w=[W["pp"].b], out=W["pp"][:], in_=W["tp"][:],
                                 func=AF.Exp)
                        b3, bb3 = nb()
                        if has_prev:
                            k.op("pe", "matmul", r=[svt.b, W["pp"].b], w=[bb3], sig=False, out=b3[:, 0:128],
                                 lhsT=svt[:, j, :], rhs=W["pp"][:], start=True, stop=False)
                        k.op("pe", "matmul", r=[svt.b, W["pc"].b], w=[bb3], out=b3[:, 0:128], lhsT=svt[:, j + 1, :],
                             rhs=W["pc"][:], start=(not has_prev), stop=True)
                        b4, bb4 = nb()
                        if has_prev:
                            k.op("pe", "matmul", r=[oneb.b, W["pp"].b], w=[bb4], sig=False, out=b4[:, 0:128],
                                 lhsT=oneb[:], rhs=W["pp"][:], start=True, stop=False)
                        k.op("pe", "matmul", r=[oneb.b, W["pc"].b], w=[bb4], out=b4[:, 0:128], lhsT=oneb[:],
                             rhs=W["pc"][:], start=(not has_prev), stop=True)
                        k.op("act", "activation", r=[bb4, esink.b], w=[W["rd"].b], out=W["rd"][:], in_=b4[:, 0:128],
                             func=AF.Ln, bias=esink[:, h:h + 1])
                        k.op("act", "activation", r=[W["rd"].b], w=[W["rd"].b], out=W["rd"][:], in_=W["rd"][:],
                             func=AF.Exp, scale=-1.0)
                        k.op("dve", "tensor_tensor", r=[bb3, W["rd"].b], w=[sm.b], out=sm[:, cs], in0=b3[:, 0:128],
                             in1=W["rd"][:], op=ALU.mult)
                    k.dma("sp", mix_in_l[p].ap()[512 + h * 128:512 + (h + 1) * 128, :], sm[:],
                          r=[sm.b], w=[b_mixin_l[p]], sem_from=sm.b)
                if "cc" not in skip:
                    k.wait_all("pool", [b_mixin_l[p]])
                    ccs = k.newsem("cc%d" % p)

                    def cc(e, sems, p=p, ccs=ccs):
                        e.collective_compute("AllGather", ALU.bypass, replica_groups=[[0, 1, 2, 3], [4, 5, 6, 7]],
                                             ins=[mix_in_l[p].ap().opt()],
                                             outs=[mix_all_l[p].ap().opt()]).then_inc(sems[ccs])
                    k.raw("pool", cc)
                    b_mixall_l[p].writes = {ccs: 1}

            allb = [t.b for t in p1_tiles] + hTb
            allb += [bb for _, bb in banks] + dbg_bufs
            for e in ("pe", "act", "dve", "pool", "sp"):
                k.wait_all(e, allb)
            p1_bufs = allb

        with contextlib.ExitStack() as st:
            T = lambda n, s, d: Tl(arena, n, s, d)
            arena.off = arena_mark
            fnwbc = T("fnwbc_s", [128, D], F32)
            selt = T("selt", [128, 4], F32)
            h2T = T("h2T", [128, 32, 512], BF16)
            h2b = [Buf("h2T%d" % m) for m in range(4)]
            big = T("big", [128, NFB * 512], BF16)
            bigb = [Buf("big%d" % f) for f in range(NFB)]
            NS = 5
            ws = [T("ws%d" % i, [128, D], BF16) for i in range(NS)]
            cand = [T("cand%d" % i, [128, 2, 512], BF16) for i in range(2)]
            xsl = [T("xsl%d" % i, [128, 512], F32) for i in range(3)]
            x2f = [T("x2f%d" % i, [128, 512], F32) for i in range(3)]
            sg = [T("sg%d" % i, [128, 512], F32) for i in range(2)]
            ssq8 = [T("ssq8_%d" % m, [128, 8], F32) for m in range(4)]
            rs2 = [T("rs2_%d" % m, [128, 1], F32) for m in range(4)]
            junk = T("junk", [128, 512], BF16)
            for t_ in [fnwbc, selt, h2T, big] + ws + cand + xsl + x2f + sg + ssq8 + rs2 + [junk]:
                t_.b.reads = {}
            k.dma("sp", fnwbc[:], fnwbc_d, w=[fnwbc.b])
            k.dma("sp", selt[:], sel_d, w=[selt.b])
            wcount = [0]
            xcount = [0]

            def wslot():
                s = ws[wcount[0] % NS]
                wcount[0] += 1
                return s

            mixT = big[:, 0:32 * 512].rearrange("p (a b) -> p a b", a=32)
            mixb = bigb[0:32]
            x2b = big[:, 32 * 512:64 * 512].rearrange("p (m c) -> p m c", m=4)
            outb = {}

            for p in range(0 if "p2" in skip else NP2):
                t0 = p * 512
                for a in range(16):
                    dst = mixT[:, a * 2:(a + 1) * 2, :]
                    dbufs = mixb[a * 2:(a + 1) * 2]
                    for j in range(4):
                        c = cand[(a * 4 + j) % 2]
                        P_ = j * NP2 + p
                        k.dma("sp", c[:], mix_all_l[P_].ap().rearrange("(a q) t -> q a t", q=128)[:, a * 2:(a + 1) * 2, :],
                              r=[b_mixall_l[P_]], w=[c.b])
                        if j == 0:
                            k.op("dve", "tensor_scalar", r=[c.b, selt.b], w=dbufs, out=dst, in0=c[:],
                                 scalar1=selt[:, 0:1], scalar2=None, op0=ALU.mult)
                        else:
                            k.op("dve", "scalar_tensor_tensor", r=[c.b, selt.b] + dbufs, w=dbufs, out=dst, in0=c[:],
                                 scalar=selt[:, j:j + 1], in1=dst, op0=ALU.mult, op1=ALU.add)
                for n in range(8):
                    bk = [nb() for _ in range(4)]
                    for a in range(4):
                        s = wslot()
                        k.dma("pool", s[:], wo_t[n][:, a * 4096:(a + 1) * 4096], w=[s.b])
                        for m in range(4):
                            for kk in range(8):
                                kc = a * 8 + kk
                                k.op("pe", "matmul", r=[s.b, mixb[kc]], w=[bk[m][1]],
                                     sig=(kk == 7 and (a == 3 or m == 3)), out=bk[m][0][:, 0:512],
                                     lhsT=mixT[:, kc, m * 128:(m + 1) * 128], rhs=s[:, kk * 512:(kk + 1) * 512],
                                     start=(a == 0 and kk == 0), stop=(a == 3 and kk == 7))
                    for m in range(4):
                        xi = xsl[xcount[0] % 3]
                        xo = x2f[xcount[0] % 3]
                        xcount[0] += 1
                        rs = slice(t0 + m * 128, t0 + (m + 1) * 128)
                        cs = slice(n * 512, (n + 1) * 512)
                        k.dma("sp", xi[:], x2tok[rs, cs], w=[xi.b])
                        k.op("dve", "tensor_tensor", r=[bk[m][1], xi.b], w=[xo.b], out=xo[:], in0=bk[m][0][:, 0:512],
                             in1=xi[:], op=ALU.add)
                        k.op("act", "activation", r=[xo.b], w=[junk.b, ssq8[m].b], out=junk[:], in_=xo[:],
                             func=AF.Square, accum_out=ssq8[m][:, n:n + 1])
                        k.op("act", "activation", r=[xo.b], w=[bigb[32 + m * 8 + n]], out=x2b[:, m, cs], in_=xo[:], func=AF.Copy)
                        ob = Buf("o_%d_%d" % (m, n))
                        outb[(m, n)] = ob
                        k.dma("sp", out_d[rs, cs], xo[:], r=[xo.b], w=[ob], sem_from=xo.b)
                for m in range(4):
                    xb_bufs = bigb[32 + m * 8:40 + m * 8]
                    k.op("dve", "tensor_reduce", r=[ssq8[m].b], w=[rs2[m].b], out=rs2[m][:], in_=ssq8[m][:],
                         axis=mybir.AxisListType.X, op=ALU.add)
                    k.op("act", "activation", r=[rs2[m].b], w=[rs2[m].b], out=rs2[m][:], in_=rs2[m][:], func=AF.Ln,
                         scale=1.0 / D, bias=EPS)
                    k.op("act", "activation", r=[rs2[m].b], w=[rs2[m].b], out=rs2[m][:], in_=rs2[m][:], func=AF.Exp,
                         scale=-0.5)
                    k.op("dve", "scalar_tensor_tensor", r=xb_bufs + [rs2[m].b, fnwbc.b], w=xb_bufs, out=x2b[:, m, :],
                         in0=x2b[:, m, :], scalar=rs2[m][:, 0:1], in1=fnwbc[:], op0=ALU.mult, op1=ALU.mult)
                    for kq in range(8):
                        bt, bb = nb()
                        btb = bt[:].bitcast(BF16)
                        for j in range(4):
                            kk = kq * 4 + j
                            k.op("pe", "transpose", r=xb_bufs + [idb.b], w=[bb], sig=(j == 3),
                                 out=btb[:, j * 128:(j + 1) * 128], in_=x2b[:, m, kk * 128:(kk + 1) * 128],
                                 identity=idb[:])
                        src = btb[:, 0:512].rearrange("p (a b) -> p a b", a=4)
                        dst = h2T[:, kq * 4:(kq + 1) * 4, m * 128:(m + 1) * 128]
                        if kq % 2 == 0:
                            k.op("act", "activation", r=[bb], w=[h2b[m]], out=dst, in_=src, func=AF.Copy)
                        else:
                            k.op("dve", "tensor_copy", r=[bb], w=[h2b[m]], out=dst, in_=src)
                for fb in range(NFB):
                    s_g = wslot()
                    k.dma("pool", s_g[:], wg_t[fb], w=[s_g.b])
                    s_u = wslot()
                    k.dma("pool", s_u[:], wu_t[fb], w=[s_u.b])
                    bg, bbg = nb()
                    for kk in range(32):
                        k.op("pe", "matmul", r=[s_g.b] + h2b, w=[bbg], sig=(kk == 31), out=bg[:, 0:512],
                             lhsT=s_g[:, kk * 128:(kk + 1) * 128], rhs=h2T[:, kk, :], start=(kk == 0),
                             stop=(kk == 31))
                    bu, bbu = nb()
                    for kk in range(32):
                        k.op("pe", "matmul", r=[s_u.b] + h2b, w=[bbu], sig=(kk == 31), out=bu[:, 0:512],
                             lhsT=s_u[:, kk * 128:(kk + 1) * 128], rhs=h2T[:, kk, :], start=(kk == 0),
                             stop=(kk == 31))
                    sgt = sg[fb % 2]
                    k.op("act", "activation", r=[bbg], w=[sgt.b], out=sgt[:], in_=bg[:, 0:512], func=AF.Silu)
                    k.op("dve", "tensor_tensor", r=[sgt.b, bbu], w=[bigb[fb]], out=big[:, fb * 512:(fb + 1) * 512],
                         in0=sgt[:], in1=bu[:, 0:512], op=ALU.mult)
                for q in range(4):
                    bk = [[nb() for _ in range(2)] for _ in range(4)]
                    for fg in range((NFB + 3) // 4):
                        nf = min(4, NFB - fg * 4)
                        s = wslot()
                        k.dma("pool", s[:, 0:nf * 1024], wd_t[q][:, fg * 4096:fg * 4096 + nf * 1024], w=[s.b])
                        for f in range(nf):
                            fb = fg * 4 + f
                            for m in range(4):
                                for n2 in range(2):
                                    k.op("pe", "matmul", r=[s.b, bigb[fb]], w=[bk[m][n2][1]],
                                         sig=(fb == NFB - 1 or (f == nf - 1 and m == 3 and n2 == 1)),
                                         out=bk[m][n2][0][:, 0:512], lhsT=big[:, fb * 512 + m * 128:fb * 512 + (m + 1) * 128],
                                         rhs=s[:, f * 1024 + n2 * 512:f * 1024 + (n2 + 1) * 512],
                                         start=(fb == 0), stop=(fb == NFB - 1))
                    for m in range(4):
                        for n2 in range(2):
                            n = q * 2 + n2
                            xi = xsl[xcount[0] % 3]
                            xo = x2f[xcount[0] % 3]
                            xcount[0] += 1
                            rs = slice(t0 + m * 128, t0 + (m + 1) * 128)
                            cs = slice(n * 512, (n + 1) * 512)
                            ob = outb[(m, n)]
                            k.dma("sp", xi[:], out_d[rs, cs], r=[ob], w=[xi.b])
                            k.op("dve", "tensor_tensor", r=[bk[m][n2][1], xi.b], w=[xo.b], out=xo[:],
                                 in0=bk[m][n2][0][:, 0:512], in1=xi[:], op=ALU.add)
                            k.dma("sp", out_d[rs, cs], xo[:], r=[xo.b], w=[ob], sem_from=xo.b)
            k.wait_all("sp", list(outb.values()) + [t_.b for t_ in x2f] + dbg_bufs)
        k.emit()
    return nc


_CACHE = {}


def _consts(g):
    c = np.zeros((128, NCONST), np.float32)
    i = np.arange(128)
    S, Cc = np.meshgrid(i, i, indexing="ij")
    c[:, C_ID:C_ID + 128] = (S == Cc)
    c[:, C_ONE:C_ONE + 128] = 1.0
    c[:, C_UIN:C_UIN + 128] = (S <= Cc)
    c[:, C_LST:C_LST + 128] = (S > Cc)
    c[:, C_MST:C_MST + 128] = (Cc > S)
    bd = (S // 64 == Cc // 64)
    c[:, C_MBD:C_MBD + 128] = bd
    c[:, C_MNB:C_MNB + 128] = ~bd
    for h in range(4):
        slope = 2.0 ** (-8.0 * (4 * g + h + 1) / 16.0)
        kk, qq = S, Cc
        cur = np.where(qq >= kk, -slope * (qq - kk), -30000.0)
        prv = np.where(kk > qq, -slope * (qq + 128 - kk), -30000.0)
        c[:, C_AL + (2 * h) * 128:C_AL + (2 * h + 1) * 128] = cur
        c[:, C_AL + (2 * h + 1) * 128:C_AL + (2 * h + 2) * 128] = prv
    return c


def _prep_shared(inp):
    w_out, w_gate, w_up, w_down = inp["w_out"][0], inp["w_gate"][0], inp["w_up"][0], inp["w_down"][0]
    perm = np.zeros(4096, np.int64)
    for r in range(4):
        for loc in range(1024):
            hh, d = (loc % 512) // 128, loc % 128
            perm[r * 1024 + loc] = (0 if loc < 512 else 2048) + (4 * r + hh) * 128 + d
    wo = w_out[perm, :]
    sh = {}
    sh["wo_t"] = np.ascontiguousarray(wo.reshape(32, 128, 8, 512).transpose(2, 1, 0, 3)).reshape(8, 128, 32 * 512)
    sh["wg_t"] = np.ascontiguousarray(w_gate.reshape(32, 128, NFB, 128).transpose(2, 1, 0, 3)).reshape(NFB, 128, D)
    sh["wu_t"] = np.ascontiguousarray(w_up.reshape(32, 128, NFB, 128).transpose(2, 1, 0, 3)).reshape(NFB, 128, D)
    sh["wd_t"] = np.ascontiguousarray(w_down.reshape(NFB, 128, 4, 1024).transpose(2, 1, 0, 3)).reshape(4, 128, NFB * 1024)
    sh["anwbc"] = np.ascontiguousarray(np.broadcast_to(inp["attn_norm_w"][0][None, :], (128, D)))
    sh["fnwbc"] = np.ascontiguousarray(np.broadcast_to(inp["ffn_norm_w"][0][None, :], (128, D)))
    return sh


def _prep_group(inp, g):
    w_in = inp["w_in"][0]
    cols = []
    for t_ in range(4):
        for i in range(4):
            cols.append(t_ * 2048 + (4 * g + i) * 128)
    for h in range(4):
        cols.append(8224 + (4 * g + h) * 128)
    cols.append(10272 + g * 128)
    cols.append(10784 + g * 128)
    w1t = np.empty((NCB, 128, D), np.float32)
    for cb, c0 in enumerate(cols):
        w1t[cb] = w_in[:, c0:c0 + 128].reshape(32, 128, 128).transpose(1, 0, 2).reshape(128, D)
    abcols = [8192 + 4 * g + i for i in range(4)] + [8208 + 4 * g + i for i in range(4)]
    wab = np.ascontiguousarray(w_in[:, abcols].reshape(32, 128, 8).transpose(1, 0, 2)).reshape(128, 256)
    prm = np.zeros((128, NPRM), np.float32)
    cw = inp["conv_w"][0]
    for t_ in range(3):
        for i in range(4):
            ch = t_ * 2048 + (4 * g + i) * 128
            for j in range(4):
                prm[:, P_CW + (t_ * 4 + i) * 4 + j] = cw[j, ch:ch + 128]
    prm[:, P_ALOG:P_ALOG + 4] = inp["a_log"][0][4 * g:4 * g + 4][None, :]
    prm[:, P_DTB:P_DTB + 4] = inp["dt_bias"][0][4 * g:4 * g + 4][None, :]
    prm[:, P_DNW] = inp["dn_norm_w"][0]
    prm[:, P_QNW] = inp["q_norm_w"][0]
    prm[:, P_KNW] = inp["k_norm_w"][0]
    prm[:, P_SNK:P_SNK + 4] = inp["sinks"][0][4 * g:4 * g + 4][None, :]
    sel = np.zeros((128, 4), np.float32)
    sel[:, g] = 1.0
    return {"w1t": w1t, "wab": wab, "prm": prm, "consts": _consts(g), "sel": sel}


def make_in_maps(inp, SEQ):
    x = inp["x"]
    sh = _prep_shared(inp)
    TOK2 = SEQ // 4
    maps = []
    grp = [_prep_group(inp, g) for g in range(4)]
    for c in range(8):
        b, g = c // 4, c % 4
        m = dict(sh)
        m.update(grp[g])
        m["xb"] = np.ascontiguousarray(x[b, :SEQ])
        m["x2tok"] = np.ascontiguousarray(x[b, g * TOK2:(g + 1) * TOK2])
        maps.append(m)
    return maps


def kernel(**inputs):
    inp = {k_: np.asarray(v) for k_, v in inputs.items()}
    SEQ = inp["x"].shape[1]
    if SEQ not in _CACHE:
        _CACHE[SEQ] = build(SEQ)
    nc = _CACHE[SEQ]
    maps = make_in_maps(inp, SEQ)
    res = run_bass_kernel_spmd(nc, maps, core_ids=list(range(8)))
    TOK2 = SEQ // 4
    out = np.empty((2, SEQ, D), np.float32)
    for c in range(8):
        b, g = c // 4, c % 4
        out[b, g * TOK2:(g + 1) * TOK2] = np.asarray(res.results[c]["out"])
    return out
```

```python
import contextlib
import numpy as np
import concourse.bass as bass
import concourse.mybir as mybir
from concourse.bass_utils import run_bass_kernel_spmd

F32 = mybir.dt.float32
BF16 = mybir.dt.bfloat16
ALU = mybir.AluOpType
AF = mybir.ActivationFunctionType
ENGS = ("pe", "act", "dve", "pool", "sp")
EPS = 1e-6
D = 4096
FF = 11008
NFB = FF // 128
NCB = 22


class Buf:
    __slots__ = ("name", "writes", "reads", "sem", "semval", "excl")

    def __init__(self, name, excl=False):
        self.name = name
        self.excl = excl
        self.writes = {}
        self.reads = {}
        self.sem = None
        self.semval = 0


class K:
    def __init__(self, nc):
        self.nc = nc
        self.q = {e: [] for e in ENGS}
        self.cnt = {e: 0 for e in ENGS}
        self.waited = {e: {} for e in ENGS}
        self.semkeys = ["E_" + e for e in ENGS]
        self.relay_sem = "S_relay"
        self.semkeys.append(self.relay_sem)
        self.relay_cnt = 0
        self.tok_out = None
        self.tok_rows = 8192
        self.tok_in = None

    def newsem(self, name):
        key = "S%d_%s" % (len(self.semkeys), name)
        self.semkeys.append(key)
        return key

    def _wait(self, eng, evs):
        if eng == "pool":
            comp = {sk: v for sk, v in evs.items()
                    if sk.startswith("E_") and v > 0 and self.waited["pool"].get(sk, 0) < v}
            if comp:
                for sk, v in comp.items():
                    self.waited["pool"][sk] = v
                self._wait("sp", comp)
                self.relay_cnt += 16
                ti = self.relay_cnt // 16 - 1
                assert ti < self.tok_rows
                self.q["sp"].append(("dma", (self.tok_out[ti:ti + 1, :], self.tok_in, self.relay_sem)))
                self.q["pool"].append(("wait", (self.relay_sem, self.relay_cnt)))
            evs = {sk: v for sk, v in evs.items() if not sk.startswith("E_")}
        for sk, v in evs.items():
            if v <= 0 or (eng == "pe" and sk == "E_pe"):
                continue
            if self.waited[eng].get(sk, 0) >= v:
                continue
            self.waited[eng][sk] = v
            self.q[eng].append(("wait", (sk, v)))

    def _deps(self, eng, r, w):
        evs = {}
        for b in r:
            for sk, v in b.writes.items():
                if evs.get(sk, 0) < v:
                    evs[sk] = v
        for b in w:
            for d in (b.writes, b.reads):
                for sk, v in d.items():
                    if evs.get(sk, 0) < v:
                        evs[sk] = v
        self._wait(eng, evs)

    def op(self, eng, meth, r=(), w=(), sig=True, **kw):
        if eng != "pe":
            ex = [b for b in r if b.excl]
            if ex:
                r = [b for b in r if not b.excl]
                w = list(w) + ex
        self._deps(eng, r, w)
        sk = "E_" + eng
        if eng == "pe" and not sig:
            ev = self.cnt[eng] + 1
            inc = False
        else:
            self.cnt[eng] += 1
            ev = self.cnt[eng]
            inc = True
        self.q[eng].append(("op", (meth, kw, sk if inc else None)))
        for b in r:
            if b.reads.get(sk, 0) < ev:
                b.reads[sk] = ev
        for b in w:
            b.writes = {sk: ev}
            b.reads = {}

    def dma(self, eng, out, in_, r=(), w=(), sem_from=None):
        self._deps(eng, r, w)
        tgt = sem_from if sem_from is not None else (w[0] if len(w) else r[0])
        if tgt.sem is None:
            tgt.sem = self.newsem(tgt.name)
        tgt.semval += 16
        sk, v = tgt.sem, tgt.semval
        self.q[eng].append(("dma", (out, in_, sk)))
        for b in r:
            if b.reads.get(sk, 0) < v:
                b.reads[sk] = v
        for b in w:
            b.writes = {sk: v}
            b.reads = {}

    def wait_all(self, eng, bufs):
        evs = {}
        for b in bufs:
            for d in (b.writes, b.reads):
                for sk, v in d.items():
                    if evs.get(sk, 0) < v:
                        evs[sk] = v
        self._wait(eng, evs)

    def raw(self, eng, fn):
        self.q[eng].append(("raw", fn))

    def emit(self):
        nc = self.nc
        with contextlib.ExitStack() as st:
            sems = {}
            for sk in self.semkeys:
                sems[sk] = st.enter_context(nc.semaphore(sk))
            block = st.enter_context(nc.Block())
            q = self.q

            def run(engname, e):
                for kind, p in q[engname]:
                    if kind == "wait":
                        e.wait_ge(sems[p[0]], p[1])
                    elif kind == "op":
                        meth, kw, sk = p
                        ins = getattr(e, meth)(**kw)
                        if sk is not None:
                            ins.then_inc(sems[sk], 1)
                    elif kind == "dma":
                        out, in_, sk = p
                        e.dma_start(out=out, in_=in_).then_inc(sems[sk], 16)
                    else:
                        p(e, sems)

            @block.tensor
            def _(e):
                run("pe", e)

            @block.scalar
            def _(e):
                run("act", e)

            @block.vector
            def _(e):
                run("dve", e)

            @block.gpsimd
            def _(e):
                run("pool", e)

            @block.sync
            def _(e):
                run("sp", e)


class Arena:
    def __init__(self, nc, st, nbytes):
        self.n = nbytes // 2
        self.t = st.enter_context(nc.sbuf_tensor("arena", [128, self.n], BF16))
        self.off = 0

    def alloc(self, nbytes):
        nb_ = (nbytes + 31) // 32 * 32
        o = self.off
        self.off += nb_ // 2
        assert self.off <= self.n, "SBUF arena overflow: %d > %d" % (self.off * 2, self.n * 2)
        return o


class Tl:
    def __init__(self, arena, name, shape, dt):
        nel = 1
        for s in shape[1:]:
            nel *= s
        esz = 4 if dt == F32 else 2
        o = arena.alloc(nel * esz)
        ap = arena.t[:, o:o + nel * esz // 2]
        if dt != BF16:
            ap = ap.bitcast(dt)
        if len(shape) == 3:
            ap = ap.rearrange("p (a b) -> p a b", a=shape[1])
        self.ap = ap
        self.b = Buf(name)

    def __getitem__(self, idx):
        return self.ap[idx]


C_ID, C_ONE, C_UIN, C_LST, C_MST, C_MBD, C_MNB, C_AL = [i * 128 for i in range(8)]
NCONST = C_AL + 8 * 128
P_CW, P_ALOG, P_DTB, P_DNW, P_QNW, P_KNW, P_SNK = 0, 48, 52, 56, 57, 58, 59
NPRM = 64


def build(SEQ, dbg=None, skip=(), np1=None):
    nc = bass.Bass("TRN2", target_bir_lowering=False)
    NP1 = SEQ // 512 if np1 is None else np1
    TOK2 = SEQ // 4
    NP2 = TOK2 // 512
    dram = lambda n, s, d, kind=None: (nc.dram_tensor(n, s, d, kind=kind) if kind else nc.dram_tensor(n, s, d))
    xb = dram("xb", [SEQ, D], F32, "ExternalInput").ap()
    consts = dram("consts", [128, NCONST], F32, "ExternalInput").ap()
    prm = dram("prm", [128, NPRM], F32, "ExternalInput").ap()
    anwbc_d = dram("anwbc", [128, D], F32, "ExternalInput").ap()
    fnwbc_d = dram("fnwbc", [128, D], F32, "ExternalInput").ap()
    w1t = dram("w1t", [NCB, 128, D], F32, "ExternalInput").ap()
    wab_d = dram("wab", [128, 256], F32, "ExternalInput").ap()
    wo_t = dram("wo_t", [8, 128, 32 * 512], F32, "ExternalInput").ap()
    wg_t = dram("wg_t", [NFB, 128, D], F32, "ExternalInput").ap()
    wu_t = dram("wu_t", [NFB, 128, D], F32, "ExternalInput").ap()
    wd_t = dram("wd_t", [4, 128, NFB * 1024], F32, "ExternalInput").ap()
    out_d = dram("out", [TOK2, D], F32, "ExternalOutput").ap()
    x2tok = dram("x2tok", [TOK2, D], F32, "ExternalInput").ap()
    sel_d = dram("sel", [128, 4], F32, "ExternalInput").ap()
    NPT = SEQ // 512
    wo_b = dram("wo_b", [8, 128, 32 * 512], BF16)
    wg_b = dram("wg_b", [NFB, 128, D], BF16)
    wu_b = dram("wu_b", [NFB, 128, D], BF16)
    wd_b = dram("wd_b", [4, 128, NFB * 1024], BF16)
    mix_in_l = [dram("mix_in%d" % p_, [1024, 512], BF16) for p_ in range(NPT)]
    mix_all_l = [dram("mix_all%d" % p_, [4096, 512], BF16) for p_ in range(NPT)]
    dbg_d = {}
    if dbg:
        for name, shape, dt in dbg:
            dbg_d[name] = dram("dbg_" + name, shape, dt, "ExternalOutput").ap()

    k = K(nc)
    tok_d = dram("tok_d", [8192, 16], F32)
    k.tok_out = tok_d.ap()
    k.tok_in = consts[0:1, 0:16]
    b_mixin_l = [Buf("mixin%d" % p_) for p_ in range(NPT)]
    b_wconv = Buf("wconv")
    convsem = k.newsem("conv")
    conv_jobs = []
    for n_ in range(8):
        for a_ in range(4):
            conv_jobs.append((wo_b.ap()[n_][:, a_ * 4096:(a_ + 1) * 4096], wo_t[n_][:, a_ * 4096:(a_ + 1) * 4096]))
    for fb_ in range(NFB):
        conv_jobs.append((wg_b.ap()[fb_], wg_t[fb_]))
        conv_jobs.append((wu_b.ap()[fb_], wu_t[fb_]))
    for q_ in range(4):
        for fg_ in range((NFB + 3) // 4):
            nf_ = min(4, NFB - fg_ * 4)
            conv_jobs.append((wd_b.ap()[q_][:, fg_ * 4096:fg_ * 4096 + nf_ * 1024],
                              wd_t[q_][:, fg_ * 4096:fg_ * 4096 + nf_ * 1024]))
    conv_state = {"i": 0, "cnt": 0}

    def conv_issue(n):
        while n > 0 and conv_state["i"] < len(conv_jobs):
            o_, i_ = conv_jobs[conv_state["i"]]
            k.q["pool"].append(("dma", (o_, i_, convsem)))
            conv_state["i"] += 1
            conv_state["cnt"] += 16
            n -= 1
    b_mixall_l = [Buf("mixall%d" % p_) for p_ in range(NPT)]
    b_out = Buf("outd")

    with contextlib.ExitStack() as st0:
        banks = []
        for i in range(8):
            t = st0.enter_context(nc.psum_tensor("bank%d" % i, [128, 512], F32))
            banks.append((t, Buf("bank%d" % i, excl=True)))
        rot = [0]

        def nb():
            i = rot[0] % 8
            rot[0] += 1
            return banks[i]

        arena = Arena(nc, st0, 206 * 1024)
        cst = Tl(arena, "cst", [128, NCONST], F32)
        prt = Tl(arena, "prt", [128, NPRM], F32)
        idb = Tl(arena, "idb", [128, 128], BF16)
        oneb = Tl(arena, "oneb", [128, 128], BF16)
        arena_mark = arena.off
        k.dma("sp", cst[:], consts, w=[cst.b])
        k.dma("sp", prt[:], prm, w=[prt.b])
        k.op("dve", "tensor_copy", r=[cst.b], w=[idb.b], out=idb[:], in_=cst[:, C_ID:C_ID + 128])
        k.op("dve", "tensor_copy", r=[cst.b], w=[oneb.b], out=oneb[:], in_=cst[:, C_ONE:C_ONE + 128])
        ident = cst[:, C_ID:C_ID + 128]
        ones_f = cst[:, C_ONE:C_ONE + 128]

        def dbg_store(name, ap_dram_idx, src_ap, srcbuf):
            if name in dbg_d:
                k.dma("sp", dbg_d[name][ap_dram_idx], src_ap, r=[srcbuf])
                dbg_bufs.append(srcbuf)
        dbg_bufs = []

        with contextlib.ExitStack() as st:
            p1_tiles = []

            def T(n, s, d):
                t = Tl(arena, n, s, d)
                p1_tiles.append(t)
                return t
            xin = [T("xin%d" % i, [128, D], F32) for i in range(2)]
            xs = T("xs", [128, D], BF16)
            anwbc = T("anwbc_s", [128, D], F32)
            hT = T("hT", [128, 32, 512], BF16)
            hTb = [Buf("hT%d" % m) for m in range(4)]
            NW = 3
            wsl = [T("wsl%d" % i, [128, D], BF16) for i in range(NW)]
            wabf = T("wabf", [128, 256], F32)
            wabb = T("wabb", [128, 256], BF16)
            ssq = T("ssq", [128, 1], F32)
            rstd = T("rstd", [128, 1], F32)
            nA = T("nA", [128, 4], F32)
            dnw2 = T("dnw2", [128, 1], F32)
            knw2 = T("knw2", [128, 1], F32)
            esink = T("esink", [128, 4], F32)
            k.dma("sp", anwbc[:], anwbc_d, w=[anwbc.b])
            k.dma("sp", wabf[:], wab_d, w=[wabf.b])
            k.op("dve", "tensor_copy", r=[wabf.b], w=[wabb.b], out=wabb[:], in_=wabf[:])
            k.op("act", "activation", r=[prt.b], w=[nA.b], out=nA[:], in_=prt[:, P_ALOG:P_ALOG + 4], func=AF.Exp)
            k.op("dve", "tensor_scalar", r=[nA.b], w=[nA.b], out=nA[:], in0=nA[:], scalar1=-1.0, scalar2=None,
                 op0=ALU.mult)
            k.op("act", "activation", r=[prt.b], w=[esink.b], out=esink[:], in_=prt[:, P_SNK:P_SNK + 4], func=AF.Exp)
            k.op("dve", "tensor_scalar", r=[prt.b], w=[dnw2.b], out=dnw2[:], in0=prt[:, P_DNW:P_DNW + 1],
                 scalar1=float(np.sqrt(128.0)), scalar2=None, op0=ALU.mult)
            k.op("dve", "tensor_scalar", r=[prt.b], w=[knw2.b], out=knw2[:], in0=prt[:, P_KNW:P_KNW + 1],
                 scalar1=float(np.sqrt(128.0)), scalar2=None, op0=ALU.mult)

            cb_order = []
            for i in range(4):
                cb_order += [i, 4 + i, 8 + i, 12 + i]
            cb_order += [16, 17, 18, 19, 20, 21]
            wseq = [(p, cb) for p in range(NP1) for cb in cb_order]
            wpos = {pc: n for n, pc in enumerate(wseq)}
            wstate = {"issued": 0}
            conv_per = -(-len(conv_jobs) // max(1, len(wseq)))

            def wprefetch(upto):
                while wstate["issued"] <= min(upto, len(wseq) - 1):
                    i = wstate["issued"]
                    s = wsl[i % NW]
                    k.dma("pool", s[:], w1t[wseq[i][1]], w=[s.b])
                    wstate["issued"] += 1
                    conv_issue(conv_per)

            def inproj(p, cb):
                i = wpos[(p, cb)]
                wprefetch(i + NW - 1)
                s = wsl[i % NW]
                bt, bb = nb()
                for kk in range(32):
                    k.op("pe", "matmul", r=[s.b] + hTb, w=[bb], sig=(kk == 31), out=bt[:, 0:512],
                         lhsT=s[:, kk * 128:(kk + 1) * 128], rhs=hT[:, kk, :], start=(kk == 0), stop=(kk == 31))
                return bt, bb

            raw = T("raw", [128, 515], F32)
            halo = [[T("halo%d_%d" % (t_, i), [128, 3], F32) for i in range(4)] for t_ in range(3)]
            for t_ in range(3):
                for i in range(4):
                    k.op("dve", "memset", w=[halo[t_][i].b], ap=halo[t_][i][:], constant=0.0)
            acc = T("acc", [128, 512], F32)
            th = T("th", [128, 512], F32)
            sqb = T("sqb", [128, 512], BF16)
            rr = T("rr", [128, 512], F32)
            qn = [T("qn%d" % i, [128, 512], F32) for i in range(2)]
            qnb = [T("qnb%d" % i, [128, 512], BF16) for i in range(2)]
            knb = [T("knb%d" % i, [128, 512], BF16) for i in range(2)]
            v2b = T("v2b", [128, 512], BF16)
            vtok = [T("vtok%d" % i, [128, 512], BF16) for i in range(2)]
            z2 = [T("z2_%d" % i, [128, 512], F32) for i in range(2)]
            oT = [T("oT%d" % i, [128, 512], F32) for i in range(2)]
            mixt = [T("mixt%d" % i, [128, 512], BF16) for i in range(2)]
            Sf = [T("Sf%d" % i, [128, 128], F32) for i in range(4)]
            Sb = [T("Sb%d" % i, [128, 128], BF16) for i in range(4)]
            for i in range(4):
                k.op("dve", "memset", w=[Sf[i].b], ap=Sf[i][:], constant=0.0)
                k.op("dve", "memset", w=[Sb[i].b], ap=Sb[i][:], constant=0.0)
            gnames = ["tA", "eA", "sp", "g", "tb", "beta", "nbeta", "gcc", "egc", "tail"]
            gt = [{n: T("g_%s%d" % (n, j), [128, 4], F32) for n in gnames} for j in range(4)]
            f32names = ["Gh", "Dt", "E", "egb", "EMi", "EMs", "Nf", "A0", "A0T", "Nof", "NofT", "P0", "P1",
                        "A1", "A1T", "XT", "Y"]
            bfnames = ["PT", "T2T", "qd", "ktl", "r2n", "vnw"]
            hb = [dict([(n, T("hb_%s%d" % (n, par), [128, 128], F32)) for n in f32names] +
                       [(n, T("hb_%s%d" % (n, par), [128, 128], BF16)) for n in bfnames]) for par in range(2)]
            sraw = T("sraw", [128, 512], F32)
            sqn = [T("sqn%d" % h, [128, 512], BF16) for h in range(4)]
            skn = T("skn", [128, 640], BF16)
            svb = T("svb", [128, 512], BF16)
            svt = T("svt", [128, 5, 128], BF16)
            smix = [T("smix%d" % i, [128, 512], BF16) for i in range(2)]
            sw = [dict(tc=T("sw_tc%d" % par, [128, 128], F32), tp=T("sw_tp%d" % par, [128, 128], F32),
                       pc=T("sw_pc%d" % par, [128, 128], BF16), pp=T("sw_pp%d" % par, [128, 128], BF16),
                       rd=T("sw_rd%d" % par, [128, 128], F32)) for par in range(2)]

            cwcol = lambda t_, i, j: prt[:, P_CW + (t_ * 4 + i) * 4 + j:P_CW + (t_ * 4 + i) * 4 + j + 1]
            hbcount = [0]
            swcount = [0]

            for p in range(NP1):
                for m in range(4):
                    xi = xin[(4 * p + m) % 2]
                    r0 = p * 512 + m * 128
                    k.dma("sp", xi[:], xb[r0:r0 + 128, :], w=[xi.b])
                    k.op("act", "activation", r=[xi.b], w=[xs.b, ssq.b], out=xs[:], in_=xi[:], func=AF.Square,
                         accum_out=ssq[:, 0:1])
                    k.op("act", "activation", r=[ssq.b], w=[rstd.b], out=rstd[:], in_=ssq[:], func=AF.Ln,
                         scale=1.0 / D, bias=EPS)
                    k.op("act", "activation", r=[rstd.b], w=[rstd.b], out=rstd[:], in_=rstd[:], func=AF.Exp,
                         scale=-0.5)
                    k.op("dve", "scalar_tensor_tensor", r=[xi.b, rstd.b, anwbc.b], w=[xs.b], out=xs[:], in0=xi[:],
                         scalar=rstd[:, 0:1], in1=anwbc[:], op0=ALU.mult, op1=ALU.mult)
                    for kq in range(8):
                        bt, bb = nb()
                        btb = bt[:].bitcast(BF16)
                        for j in range(4):
                            kk = kq * 4 + j
                            k.op("pe", "transpose", r=[xs.b, idb.b], w=[bb], sig=(j == 3),
                                 out=btb[:, j * 128:(j + 1) * 128], in_=xs[:, kk * 128:(kk + 1) * 128],
                                 identity=idb[:])
                        src = btb[:, 0:512].rearrange("p (a b) -> p a b", a=4)
                        dst = hT[:, kq * 4:(kq + 1) * 4, m * 128:(m + 1) * 128]
                        if kq % 2 == 0:
                            k.op("act", "activation", r=[bb], w=[hTb[m]], out=dst, in_=src, func=AF.Copy)
                        else:
                            k.op("dve", "tensor_copy", r=[bb], w=[hTb[m]], out=dst, in_=src)
                if p == 0 and "hT" in dbg_d:
                    k.dma("sp", dbg_d["hT"], hT[:], r=hTb)
                    dbg_bufs.extend(hTb)

                for j in range(4):
                    G = gt[j]
                    bt, bb = nb()
                    for kk in range(32):
                        k.op("pe", "matmul", r=[wabb.b] + hTb, w=[bb], sig=(kk == 31), out=bt[:, 0:8],
                             lhsT=hT[:, kk, j * 128:(j + 1) * 128], rhs=wabb[:, kk * 8:(kk + 1) * 8],
                             start=(kk == 0), stop=(kk == 31))
                    k.op("dve", "tensor_tensor", r=[bb, prt.b], w=[G["tA"].b], out=G["tA"][:], in0=bt[:, 0:4],
                         in1=prt[:, P_DTB:P_DTB + 4], op=ALU.add)
                    k.op("act", "activation", r=[bb], w=[G["tb"].b], out=G["tb"][:], in_=bt[:, 4:8], func=AF.Exp,
                         scale=-1.0)
                    k.op("act", "activation", r=[G["tb"].b], w=[G["tb"].b], out=G["tb"][:], in_=G["tb"][:],
                         func=AF.Ln, bias=1.0)
                    k.op("act", "activation", r=[G["tb"].b], w=[G["beta"].b], out=G["beta"][:], in_=G["tb"][:],
                         func=AF.Exp, scale=-1.0)
                    k.op("act", "activation", r=[G["tA"].b], w=[G["eA"].b], out=G["eA"][:], in_=G["tA"][:],
                         func=AF.Exp)
                    k.op("act", "activation", r=[G["eA"].b], w=[G["sp"].b], out=G["sp"][:], in_=G["eA"][:],
                         func=AF.Ln, bias=1.0)
                    k.op("dve", "tensor_tensor", r=[G["sp"].b, nA.b], w=[G["g"].b], out=G["g"][:], in0=G["sp"][:],
                         in1=nA[:], op=ALU.mult)
                    k.op("dve", "tensor_scalar", r=[G["beta"].b], w=[G["nbeta"].b], out=G["nbeta"][:],
                         in0=G["beta"][:], scalar1=-1.0, scalar2=None, op0=ALU.mult)
                    bt2, bb2 = nb()
                    k.op("pe", "matmul", r=[cst.b, G["g"].b], w=[bb2], out=bt2[:, 0:4],
                         lhsT=cst[:, C_UIN:C_UIN + 128], rhs=G["g"][:], start=True, stop=True)
                    k.op("act", "activation", r=[bb2], w=[G["gcc"].b], out=G["gcc"][:], in_=bt2[:, 0:4],
                         func=AF.Copy)
                    k.op("act", "activation", r=[bb2], w=[G["egc"].b], out=G["egc"][:], in_=bt2[:, 0:4],
                         func=AF.Exp)
                    bt3, bb3 = nb()
                    k.op("pe", "matmul", r=[cst.b, G["g"].b], w=[bb3], out=bt3[:, 0:4],
                         lhsT=cst[:, C_LST:C_LST + 128], rhs=G["g"][:], start=True, stop=True)
                    k.op("act", "activation", r=[bb3], w=[G["tail"].b], out=G["tail"][:], in_=bt3[:, 0:4],
                         func=AF.Exp)

                for i in range(0 if "gdn" in skip else 4):
                    par = i % 2
                    for t_ in range(3):
                        bt, bb = inproj(p, t_ * 4 + i)
                        k.op("dve", "tensor_copy", r=[halo[t_][i].b], w=[raw.b], out=raw[:, 0:3],
                             in_=halo[t_][i][:])
                        k.op("act", "activation", r=[bb], w=[raw.b], out=raw[:, 3:515], in_=bt[:, 0:512],
                             func=AF.Copy)
                        k.op("dve", "tensor_scalar", r=[raw.b, prt.b], w=[acc.b], out=acc[:], in0=raw[:, 0:512],
                             scalar1=cwcol(t_, i, 0), scalar2=None, op0=ALU.mult)
                        for j in range(1, 4):
                            k.op("dve", "scalar_tensor_tensor", r=[raw.b, prt.b, acc.b], w=[acc.b], out=acc[:],
                                 in0=raw[:, j:j + 512], scalar=cwcol(t_, i, j), in1=acc[:], op0=ALU.mult,
                                 op1=ALU.add)
                        k.op("dve", "tensor_copy", r=[raw.b], w=[halo[t_][i].b], out=halo[t_][i][:],
                             in_=raw[:, 512:515])
                        k.op("act", "activation", r=[acc.b], w=[th.b], out=th[:], in_=acc[:], func=AF.Exp,
                             scale=-1.0)
                        k.op("act", "activation", r=[th.b], w=[th.b], out=th[:], in_=th[:], func=AF.Ln, bias=1.0)
                        k.op("act", "activation", r=[th.b], w=[th.b], out=th[:], in_=th[:], func=AF.Exp, scale=-1.0)
                        k.op("dve", "tensor_tensor", r=[th.b, acc.b], w=[th.b], out=th[:], in0=th[:], in1=acc[:],
                             op=ALU.mult)
                        if t_ < 2:
                            k.op("act", "activation", r=[th.b], w=[sqb.b], out=sqb[:], in_=th[:], func=AF.Square)
                            bt2, bb2 = nb()
                            k.op("pe", "matmul", r=[oneb.b, sqb.b], w=[bb2], out=bt2[:, 0:512], lhsT=oneb[:],
                                 rhs=sqb[:], start=True, stop=True)
                            k.op("act", "activation", r=[bb2], w=[rr.b], out=rr[:], in_=bt2[:, 0:512], func=AF.Ln,
                                 bias=EPS)
                            k.op("act", "activation", r=[rr.b], w=[rr.b], out=rr[:], in_=rr[:], func=AF.Exp,
                                 scale=-0.5)
                            if t_ == 0:
                                k.op("dve", "scalar_tensor_tensor", r=[th.b, rr.b], w=[qn[par].b], out=qn[par][:],
                                     in0=th[:], scalar=float(128.0 ** -0.5), in1=rr[:], op0=ALU.mult,
                                     op1=ALU.mult)
                                k.op("act", "activation", r=[qn[par].b], w=[qnb[par].b], out=qnb[par][:],
                                     in_=qn[par][:], func=AF.Copy)
                            else:
                                k.op("dve", "tensor_tensor", r=[th.b, rr.b], w=[knb[par].b], out=knb[par][:],
                                     in0=th[:], in1=rr[:], op=ALU.mult)
                        else:
                            k.op("act", "activation", r=[th.b], w=[v2b.b], out=v2b[:], in_=th[:], func=AF.Copy)
                            bt2, bb2 = nb()
                            btb = bt2[:].bitcast(BF16)
                            for j in range(4):
                                k.op("pe", "transpose", r=[v2b.b, idb.b], w=[bb2], sig=(j == 3),
                                     out=btb[:, j * 128:(j + 1) * 128], in_=v2b[:, j * 128:(j + 1) * 128],
                                     identity=idb[:])
                            k.op("act", "activation", r=[bb2], w=[vtok[par].b], out=vtok[par][:],
                                 in_=btb[:, 0:512], func=AF.Copy)
                    bt, bb = inproj(p, 12 + i)
                    k.op("act", "activation", r=[bb], w=[th.b], out=th[:], in_=bt[:, 0:512], func=AF.Exp, scale=-1.0)
                    k.op("act", "activation", r=[th.b], w=[th.b], out=th[:], in_=th[:], func=AF.Ln, bias=1.0)
                    k.op("act", "activation", r=[th.b], w=[th.b], out=th[:], in_=th[:], func=AF.Exp, scale=-1.0)
                    k.op("dve", "tensor_tensor", r=[th.b, bb], w=[z2[par].b], out=z2[par][:], in0=th[:],
                         in1=bt[:, 0:512], op=ALU.mult)

                    for j in range(4):
                        H = hb[hbcount[0] % 2]
                        hbcount[0] += 1
                        G = gt[j]
                        cs = slice(j * 128, (j + 1) * 128)
                        gcol = G["g"][:, i:i + 1]
                        k.op("dve", "tensor_scalar", r=[cst.b, G["g"].b], w=[H["Gh"].b], out=H["Gh"][:], in0=ones_f,
                             scalar1=gcol, scalar2=None, op0=ALU.mult)
                        btg, bbg = nb()
                        k.op("pe", "matmul", r=[H["Gh"].b, cst.b], w=[bbg], out=btg[:, 0:128], lhsT=H["Gh"][:],
                             rhs=cst[:, C_UIN:C_UIN + 128], start=True, stop=True)
                        k.op("dve", "tensor_scalar", r=[bbg, G["gcc"].b], w=[H["Dt"].b], out=H["Dt"][:],
                             in0=btg[:, 0:128], scalar1=G["gcc"][:, i:i + 1], scalar2=0.0, op0=ALU.subtract,
                             op1=ALU.min)
                        k.op("act", "activation", r=[bbg], w=[H["egb"].b], out=H["egb"][:], in_=btg[:, 0:128],
                             func=AF.Exp)
                        k.op("act", "activation", r=[H["Dt"].b], w=[H["E"].b], out=H["E"][:], in_=H["Dt"][:],
                             func=AF.Exp)
                        k.op("dve", "tensor_tensor", r=[H["E"].b, cst.b], w=[H["EMi"].b], out=H["EMi"][:],
                             in0=H["E"][:], in1=cst[:, C_UIN:C_UIN + 128], op=ALU.mult)
                        k.op("dve", "tensor_tensor", r=[H["E"].b, cst.b], w=[H["EMs"].b], out=H["EMs"][:],
                             in0=H["E"][:], in1=cst[:, C_MST:C_MST + 128], op=ALU.mult)
                        k.op("dve", "tensor_tensor", r=[qn[par].b, H["egb"].b], w=[H["qd"].b], out=H["qd"][:],
                             in0=qn[par][:, cs], in1=H["egb"][:], op=ALU.mult)
                        btk, bbk = nb()
                        k.op("pe", "matmul", r=[knb[par].b], w=[bbk], out=btk[:, 0:128], lhsT=knb[par][:, cs],
                             rhs=knb[par][:, cs], start=True, stop=True)
                        k.op("dve", "scalar_tensor_tensor", r=[bbk, G["beta"].b, H["EMs"].b], w=[H["Nf"].b],
                             out=H["Nf"][:], in0=btk[:, 0:128], scalar=G["beta"][:, i:i + 1], in1=H["EMs"][:],
                             op0=ALU.mult, op1=ALU.mult)
                        btq, bbq = nb()
                        k.op("pe", "matmul", r=[knb[par].b, qnb[par].b], w=[bbq], out=btq[:, 0:128],
                             lhsT=knb[par][:, cs], rhs=qnb[par][:, cs], start=True, stop=True)
                        k.op("dve", "tensor_tensor", r=[bbq, H["EMi"].b], w=[H["PT"].b], out=H["PT"][:],
                             in0=btq[:, 0:128], in1=H["EMi"][:], op=ALU.mult)
                        btt, bbt = nb()
                        bttb = btt[:].bitcast(BF16)
                        k.op("pe", "transpose", r=[knb[par].b, idb.b], w=[bbt], out=bttb[:, 0:128],
                             in_=knb[par][:, cs], identity=idb[:])
                        k.op("act", "activation", r=[bbt, G["tail"].b], w=[H["ktl"].b], out=H["ktl"][:],
                             in_=bttb[:, 0:128], func=AF.Copy, scale=G["tail"][:, i:i + 1])
                        k.op("dve", "tensor_tensor", r=[H["Nf"].b, cst.b], w=[H["A0"].b], out=H["A0"][:],
                             in0=H["Nf"][:], in1=cst[:, C_MBD:C_MBD + 128], op=ALU.mult)
                        k.op("dve", "tensor_tensor", r=[H["Nf"].b, cst.b], w=[H["Nof"].b], out=H["Nof"][:],
                             in0=H["Nf"][:], in1=cst[:, C_MNB:C_MNB + 128], op=ALU.mult)
                        btn, bbn = nb()
                        k.op("pe", "transpose", r=[H["Nf"].b, cst.b], w=[bbn], out=btn[:, 0:128], in_=H["Nf"][:],
                             identity=ident)
                        k.op("dve", "tensor_tensor", r=[bbn, cst.b], w=[H["A0T"].b], out=H["A0T"][:],
                             in0=btn[:, 0:128], in1=cst[:, C_MBD:C_MBD + 128], op=ALU.mult)
                        k.op("dve", "tensor_tensor", r=[bbn, cst.b], w=[H["NofT"].b], out=H["NofT"][:],
                             in0=btn[:, 0:128], in1=cst[:, C_MNB:C_MNB + 128], op=ALU.mult)
                        k.op("dve", "tensor_tensor", r=[cst.b, H["A0"].b], w=[H["P0"].b], out=H["P0"][:], in0=ident,
                             in1=H["A0"][:], op=ALU.subtract)
                        A, AT, Pc = H["A0"], H["A0T"], H["P0"]
                        An, ATn, Pn = H["A1"], H["A1T"], H["P1"]
                        for lev in range(5):
                            last = (lev == 4)
                            b1, bb1 = nb()
                            k.op("pe", "matmul", r=[A.b, AT.b], w=[bb1], out=b1[:, 0:128], lhsT=A[:], rhs=AT[:],
                                 start=True, stop=True)
                            if not last:
                                b2, bb2 = nb()
                                k.op("pe", "matmul", r=[A.b, AT.b], w=[bb2], out=b2[:, 0:128], lhsT=AT[:], rhs=A[:],
                                     start=True, stop=True)
                            k.op("act", "activation", r=[bb1], w=[ATn.b], out=ATn[:], in_=b1[:, 0:128], func=AF.Copy)
                            if not last:
                                k.op("act", "activation", r=[bb2], w=[An.b], out=An[:], in_=b2[:, 0:128],
                                     func=AF.Copy)
                            b3, bb3 = nb()
                            k.op("pe", "matmul", r=[ATn.b, Pc.b], w=[bb3], out=b3[:, 0:128], lhsT=ATn[:], rhs=Pc[:],
                                 start=True, stop=True)
                            k.op("dve", "tensor_tensor", r=[bb3, Pc.b], w=[Pn.b], out=Pn[:], in0=b3[:, 0:128],
                                 in1=Pc[:], op=ALU.add)
                            A, An = An, A
                            AT, ATn = ATn, AT
                            Pc, Pn = Pn, Pc
                        X = Pc
                        b1, bb1 = nb()
                        k.op("pe", "transpose", r=[X.b, cst.b], w=[bb1], out=b1[:, 0:128], in_=X[:], identity=ident)
                        k.op("act", "activation", r=[bb1], w=[H["XT"].b], out=H["XT"][:], in_=b1[:, 0:128],
                             func=AF.Copy)
                        b2, bb2 = nb()
                        k.op("pe", "matmul", r=[H["NofT"].b, X.b], w=[bb2], out=b2[:, 0:128], lhsT=H["NofT"][:],
                             rhs=X[:], start=True, stop=True)
                        k.op("act", "activation", r=[bb2], w=[H["Y"].b], out=H["Y"][:], in_=b2[:, 0:128],
                             func=AF.Copy)
                        b3, bb3 = nb()
                        k.op("pe", "matmul", r=[H["XT"].b, H["Y"].b], w=[bb3], out=b3[:, 0:128], lhsT=H["XT"][:],
                             rhs=H["Y"][:], start=True, stop=True)
                        k.op("dve", "tensor_tensor", r=[X.b, bb3], w=[H["T2T"].b], out=H["T2T"][:], in0=X[:],
                             in1=b3[:, 0:128], op=ALU.subtract)

                        b4, bb4 = nb()
                        k.op("pe", "matmul", r=[knb[par].b, Sb[i].b], w=[bb4], out=b4[:, 0:128], lhsT=knb[par][:, cs],
                             rhs=Sb[i][:], start=True, stop=True)
                        k.op("dve", "scalar_tensor_tensor", r=[bb4, G["egc"].b, vtok[par].b], w=[H["r2n"].b],
                             out=H["r2n"][:], in0=b4[:, 0:128], scalar=G["egc"][:, i:i + 1], in1=vtok[par][:, cs],
                             op0=ALU.mult, op1=ALU.subtract)
                        b5, bb5 = nb()
                        k.op("pe", "matmul", r=[H["T2T"].b, H["r2n"].b], w=[bb5], out=b5[:, 0:128], lhsT=H["T2T"][:],
                             rhs=H["r2n"][:], start=True, stop=True)
                        k.op("act", "activation", r=[bb5, G["nbeta"].b], w=[H["vnw"].b], out=H["vnw"][:],
                             in_=b5[:, 0:128], func=AF.Copy, scale=G["nbeta"][:, i:i + 1])
                        b6, bb6 = nb()
                        k.op("pe", "matmul", r=[Sb[i].b, H["qd"].b], w=[bb6], sig=False, out=b6[:, 0:128],
                             lhsT=Sb[i][:], rhs=H["qd"][:], start=True, stop=False)
                        k.op("pe", "matmul", r=[H["vnw"].b, H["PT"].b], w=[bb6], out=b6[:, 0:128], lhsT=H["vnw"][:],
                             rhs=H["PT"][:], start=False, stop=True)
                        k.op("act", "activation", r=[bb6], w=[oT[par].b], out=oT[par][:, cs], in_=b6[:, 0:128],
                             func=AF.Copy)
                        b7, bb7 = nb()
                        k.op("pe", "matmul", r=[H["ktl"].b, H["vnw"].b], w=[bb7], out=b7[:, 0:128], lhsT=H["ktl"][:],
                             rhs=H["vnw"][:], start=True, stop=True)
                        k.op("dve", "scalar_tensor_tensor", r=[Sf[i].b, H["egb"].b, bb7], w=[Sf[i].b], out=Sf[i][:],
                             in0=Sf[i][:], scalar=H["egb"][:, 127:128], in1=b7[:, 0:128], op0=ALU.mult, op1=ALU.add)
                        k.op("act", "activation", r=[Sf[i].b], w=[Sb[i].b], out=Sb[i][:], in_=Sf[i][:], func=AF.Copy)

                    k.op("act", "activation", r=[oT[par].b], w=[sqb.b], out=sqb[:], in_=oT[par][:], func=AF.Square)
                    bt2, bb2 = nb()
                    k.op("pe", "matmul", r=[oneb.b, sqb.b], w=[bb2], out=bt2[:, 0:512], lhsT=oneb[:], rhs=sqb[:],
                         start=True, stop=True)
                    k.op("act", "activation", r=[bb2], w=[rr.b], out=rr[:], in_=bt2[:, 0:512], func=AF.Ln, bias=128.0 * EPS)
                    k.op("act", "activation", r=[rr.b], w=[rr.b], out=rr[:], in_=rr[:], func=AF.Exp, scale=-0.5)
                    k.op("dve", "tensor_tensor", r=[oT[par].b, rr.b], w=[rr.b], out=rr[:], in0=oT[par][:], in1=rr[:],
                         op=ALU.mult)
                    k.op("dve", "scalar_tensor_tensor", r=[rr.b, dnw2.b, z2[par].b], w=[mixt[par].b],
                         out=mixt[par][:], in0=rr[:], scalar=dnw2[:, 0:1], in1=z2[par][:], op0=ALU.mult,
                         op1=ALU.mult)
                    k.dma("sp", mix_in_l[p].ap()[i * 128:(i + 1) * 128, :], mixt[par][:],
                          r=[mixt[par].b], w=[b_mixin_l[p]], sem_from=mixt[par].b)

                if "swa" in skip:
                    continue
                for h in range(4):
                    bt, bb = inproj(p, 16 + h)
                    k.op("act", "activation", r=[bb], w=[sraw.b], out=sraw[:], in_=bt[:, 0:512], func=AF.Copy)
                    k.op("act", "activation", r=[sraw.b], w=[sqb.b], out=sqb[:], in_=sraw[:], func=AF.Square)
                    bt2, bb2 = nb()
                    k.op("pe", "matmul", r=[oneb.b, sqb.b], w=[bb2], out=bt2[:, 0:512], lhsT=oneb[:], rhs=sqb[:],
                         start=True, stop=True)
                    k.op("act", "activation", r=[bb2], w=[rr.b], out=rr[:], in_=bt2[:, 0:512], func=AF.Ln, bias=128.0 * EPS)
                    k.op("act", "activation", r=[rr.b], w=[rr.b], out=rr[:], in_=rr[:], func=AF.Exp, scale=-0.5)
                    k.op("dve", "scalar_tensor_tensor", r=[sraw.b, prt.b, rr.b], w=[sqn[h].b], out=sqn[h][:],
                         in0=sraw[:], scalar=prt[:, P_QNW:P_QNW + 1], in1=rr[:], op0=ALU.mult, op1=ALU.mult)
                bt, bb = inproj(p, 20)
                k.op("act", "activation", r=[bb], w=[sraw.b], out=sraw[:], in_=bt[:, 0:512], func=AF.Copy)
                k.op("act", "activation", r=[sraw.b], w=[sqb.b], out=sqb[:], in_=sraw[:], func=AF.Square)
                bt2, bb2 = nb()
                k.op("pe", "matmul", r=[oneb.b, sqb.b], w=[bb2], out=bt2[:, 0:512], lhsT=oneb[:], rhs=sqb[:],
                     start=True, stop=True)
                k.op("act", "activation", r=[bb2], w=[rr.b], out=rr[:], in_=bt2[:, 0:512], func=AF.Ln, bias=128.0 * EPS)
                k.op("act", "activation", r=[rr.b], w=[rr.b], out=rr[:], in_=rr[:], func=AF.Exp, scale=-0.5)
                if p > 0:
                    k.op("dve", "tensor_copy", r=[skn.b], w=[skn.b], out=skn[:, 0:128], in_=skn[:, 512:640])
                    k.op("dve", "tensor_copy", r=[svt.b], w=[svt.b], out=svt[:, 0, :], in_=svt[:, 4, :])
                k.op("dve", "scalar_tensor_tensor", r=[sraw.b, knw2.b, rr.b], w=[skn.b], out=skn[:, 128:640],
                     in0=sraw[:], scalar=knw2[:, 0:1], in1=rr[:], op0=ALU.mult, op1=ALU.mult)
                bt, bb = inproj(p, 21)
                k.op("act", "activation", r=[bb], w=[svb.b], out=svb[:], in_=bt[:, 0:512], func=AF.Copy)
                bt2, bb2 = nb()
                btb = bt2[:].bitcast(BF16)
                for j in range(4):
                    k.op("pe", "transpose", r=[svb.b, idb.b], w=[bb2], sig=(j == 3),
                         out=btb[:, j * 128:(j + 1) * 128], in_=svb[:, j * 128:(j + 1) * 128], identity=idb[:])
                k.op("act", "activation", r=[bb2], w=[svt.b], out=svt[:, 1:5, :],
                     in_=btb[:, 0:512].rearrange("p (a b) -> p a b", a=4), func=AF.Copy)
                for h in range(4):
                    sm = smix[h % 2]
                    for j in range(4):
                        W = sw[swcount[0] % 2]
                        swcount[0] += 1
                        gblk = p * 4 + j
                        cs = slice(j * 128, (j + 1) * 128)
                        has_prev = gblk > 0
                        b1, bb1 = nb()
                        k.op("pe", "matmul", r=[skn.b, sqn[h].b], w=[bb1], out=b1[:, 0:128],
                             lhsT=skn[:, 128 + j * 128:256 + j * 128], rhs=sqn[h][:, cs], start=True, stop=True)
                        k.op("dve", "tensor_tensor", r=[bb1, cst.b], w=[W["tc"].b], out=W["tc"][:], in0=b1[:, 0:128],
                             in1=cst[:, C_AL + (2 * h) * 128:C_AL + (2 * h + 1) * 128], op=ALU.add)
                        k.op("act", "activation", r=[W["tc"].b], w=[W["pc"].b], out=W["pc"][:], in_=W["tc"][:],
                             func=AF.Exp)
                        if has_prev:
                            b2, bb2 = nb()
                            k.op("pe", "matmul", r=[skn.b, sqn[h].b], w=[bb2], out=b2[:, 0:128],
                                 lhsT=skn[:, j * 128:128 + j * 128], rhs=sqn[h][:, cs], start=True, stop=True)
                            k.op("dve", "tensor_tensor", r=[bb2, cst.b], w=[W["tp"].b], out=W["tp"][:],
                                 in0=b2[:, 0:128], in1=cst[:, C_AL + (2 * h + 1) * 128:C_AL + (2 * h + 2) * 128],
                                 op=ALU.add)
                            k.op("act", "activation", r=[W["tp"].b], w=[W["pp"].b], out=W["pp"][:], in_=W["tp"][:],
                                 func=AF.Exp)
                        b3, bb3 = nb()
                        if has_prev:
                            k.op("pe", "matmul", r=[svt.b, W["pp"].b], w=[bb3], sig=False, out=b3[:, 0:128],
                                 lhsT=svt[:, j, :], rhs=W["pp"][:], start=True, stop=False)
                        k.op("pe", "matmul", r=[svt.b, W["pc"].b], w=[bb3], out=b3[:, 0:128], lhsT=svt[:, j + 1, :],
                             rhs=W["pc"][:], start=(not has_prev), stop=True)
                        b4, bb4 = nb()
                        if has_prev:
                            k.op("pe", "matmul", r=[oneb.b, W["pp"].b], w=[bb4], sig=False, out=b4[:, 0:128],
                                 lhsT=oneb[:], rhs=W["pp"][:], start=True, stop=False)
                        k.op("pe", "matmul", r=[oneb.b, W["pc"].b], w=[bb4], out=b4[:, 0:128], lhsT=oneb[:],
                             rhs=W["pc"][:], start=(not has_prev), stop=True)
                        k.op("act", "activation", r=[bb4, esink.b], w=[W["rd"].b], out=W["rd"][:], in_=b4[:, 0:128],
                             func=AF.Ln, bias=esink[:, h:h + 1])
                        k.op("act", "activation", r=[W["rd"].b], w=[W["rd"].b], out=W["rd"][:], in_=W["rd"][:],
                             func=AF.Exp, scale=-1.0)
                        k.op("dve", "tensor_tensor", r=[bb3, W["rd"].b], w=[sm.b], out=sm[:, cs], in0=b3[:, 0:128],
                             in1=W["rd"][:], op=ALU.mult)
                    k.dma("sp", mix_in_l[p].ap()[512 + h * 128:512 + (h + 1) * 128, :], sm[:],
                          r=[sm.b], w=[b_mixin_l[p]], sem_from=sm.b)
                if "cc" not in skip:
                    k.wait_all("pool", [b_mixin_l[p]])
                    ccs = k.newsem("cc%d" % p)

                    def cc(e, sems, p=p, ccs=ccs):
                        e.collective_compute("AllGather", ALU.bypass, replica_groups=[[0, 1, 2, 3], [4, 5, 6, 7]],
                                             ins=[mix_in_l[p].ap().opt()],
                                             outs=[mix_all_l[p].ap().opt()]).then_inc(sems[ccs])
                    k.raw("pool", cc)
                    b_mixall_l[p].writes = {ccs: 1}

            conv_issue(len(conv_jobs))
            b_wconv.writes = {convsem: conv_state["cnt"]}
            allb = [t.b for t in p1_tiles] + hTb
            allb += [bb for _, bb in banks] + dbg_bufs
            for e in ("pe", "act", "dve", "pool", "sp"):
                k.wait_all(e, allb)
            p1_bufs = allb

        with contextlib.ExitStack() as st:
            T = lambda n, s, d: Tl(arena, n, s, d)
            arena.off = arena_mark
            fnwbc = T("fnwbc_s", [128, D], F32)
            selt = T("selt", [128, 4], F32)
            h2T = T("h2T", [128, 32, 512], BF16)
            h2b = [Buf("h2T%d" % m) for m in range(4)]
            big = T("big", [128, NFB * 512], BF16)
            bigb = [Buf("big%d" % f) for f in range(NFB)]
            NS = 5
            ws = [T("ws%d" % i, [128, D], BF16) for i in range(NS)]
            cand = [T("cand%d" % i, [128, 2, 512], BF16) for i in range(2)]
            xsl = [T("xsl%d" % i, [128, 512], F32) for i in range(3)]
            x2f = [T("x2f%d" % i, [128, 512], F32) for i in range(3)]
            sg = [T("sg%d" % i, [128, 512], F32) for i in range(2)]
            ssq8 = [T("ssq8_%d" % m, [128, 8], F32) for m in range(4)]
            rs2 = [T("rs2_%d" % m, [128, 1], F32) for m in range(4)]
            junk = T("junk", [128, 512], BF16)
            for t_ in [fnwbc, selt, h2T, big] + ws + cand + xsl + x2f + sg + ssq8 + rs2 + [junk]:
                t_.b.reads = {}
            k.dma("sp", fnwbc[:], fnwbc_d, w=[fnwbc.b])
            k.dma("sp", selt[:], sel_d, w=[selt.b])
            wcount = [0]
            xcount = [0]

            def wslot():
                s = ws[wcount[0] % NS]
                wcount[0] += 1
                return s

            mixT = big[:, 0:32 * 512].rearrange("p (a b) -> p a b", a=32)
            mixb = bigb[0:32]
            x2b = big[:, 32 * 512:64 * 512].rearrange("p (m c) -> p m c", m=4)
            outb = {}

            for p in range(0 if "p2" in skip else NP2):
                t0 = p * 512
                for a in range(16):
                    dst = mixT[:, a * 2:(a + 1) * 2, :]
                    dbufs = mixb[a * 2:(a + 1) * 2]
                    for j in range(4):
                        c = cand[(a * 4 + j) % 2]
                        P_ = j * NP2 + p
                        k.dma("sp", c[:], mix_all_l[P_].ap().rearrange("(a q) t -> q a t", q=128)[:, a * 2:(a + 1) * 2, :],
                              r=[b_mixall_l[P_]], w=[c.b])
                        if j == 0:
                            k.op("dve", "tensor_scalar", r=[c.b, selt.b], w=dbufs, out=dst, in0=c[:],
                                 scalar1=selt[:, 0:1], scalar2=None, op0=ALU.mult)
                        else:
                            k.op("dve", "scalar_tensor_tensor", r=[c.b, selt.b] + dbufs, w=dbufs, out=dst, in0=c[:],
                                 scalar=selt[:, j:j + 1], in1=dst, op0=ALU.mult, op1=ALU.add)
                for n in range(8):
                    bk = [nb() for _ in range(4)]
                    for a in range(4):
                        s = wslot()
                        k.dma("pool", s[:], wo_b.ap()[n][:, a * 4096:(a + 1) * 4096], r=[b_wconv], w=[s.b])
                        for m in range(4):
                            for kk in range(8):
                                kc = a * 8 + kk
                                k.op("pe", "matmul", r=[s.b, mixb[kc]], w=[bk[m][1]],
                                     sig=(kk == 7 and (a == 3 or m == 3)), out=bk[m][0][:, 0:512],
                                     lhsT=mixT[:, kc, m * 128:(m + 1) * 128], rhs=s[:, kk * 512:(kk + 1) * 512],
                                     start=(a == 0 and kk == 0), stop=(a == 3 and kk == 7))
                    for m in range(4):
                        xi = xsl[xcount[0] % 3]
                        xo = x2f[xcount[0] % 3]
                        xcount[0] += 1
                        rs = slice(t0 + m * 128, t0 + (m + 1) * 128)
                        cs = slice(n * 512, (n + 1) * 512)
                        k.dma("sp", xi[:], x2tok[rs, cs], w=[xi.b])
                        k.op("dve", "tensor_tensor", r=[bk[m][1], xi.b], w=[xo.b], out=xo[:], in0=bk[m][0][:, 0:512],
                             in1=xi[:], op=ALU.add)
                        k.op("act", "activation", r=[xo.b], w=[junk.b, ssq8[m].b], out=junk[:], in_=xo[:],
                             func=AF.Square, accum_out=ssq8[m][:, n:n + 1])
                        k.op("act", "activation", r=[xo.b], w=[bigb[32 + m * 8 + n]], out=x2b[:, m, cs], in_=xo[:], func=AF.Copy)
                        ob = Buf("o_%d_%d" % (m, n))
                        outb[(m, n)] = ob
                        k.dma("sp", out_d[rs, cs], xo[:], r=[xo.b], w=[ob], sem_from=xo.b)
                for m in range(4):
                    xb_bufs = bigb[32 + m * 8:40 + m * 8]
                    k.op("dve", "tensor_reduce", r=[ssq8[m].b], w=[rs2[m].b], out=rs2[m][:], in_=ssq8[m][:],
                         axis=mybir.AxisListType.X, op=ALU.add)
                    k.op("act", "activation", r=[rs2[m].b], w=[rs2[m].b], out=rs2[m][:], in_=rs2[m][:], func=AF.Ln,
                         scale=1.0 / D, bias=EPS)
                    k.op("act", "activation", r=[rs2[m].b], w=[rs2[m].b], out=rs2[m][:], in_=rs2[m][:], func=AF.Exp,
                         scale=-0.5)
                    k.op("dve", "scalar_tensor_tensor", r=xb_bufs + [rs2[m].b, fnwbc.b], w=xb_bufs, out=x2b[:, m, :],
                         in0=x2b[:, m, :], scalar=rs2[m][:, 0:1], in1=fnwbc[:], op0=ALU.mult, op1=ALU.mult)
                    for kq in range(8):
                        bt, bb = nb()
                        btb = bt[:].bitcast(BF16)
                        for j in range(4):
                            kk = kq * 4 + j
                            k.op("pe", "transpose", r=xb_bufs + [idb.b], w=[bb], sig=(j == 3),
                                 out=btb[:, j * 128:(j + 1) * 128], in_=x2b[:, m, kk * 128:(kk + 1) * 128],
                                 identity=idb[:])
                        src = btb[:, 0:512].rearrange("p (a b) -> p a b", a=4)
                        dst = h2T[:, kq * 4:(kq + 1) * 4, m * 128:(m + 1) * 128]
                        if kq % 2 == 0:
                            k.op("act", "activation", r=[bb], w=[h2b[m]], out=dst, in_=src, func=AF.Copy)
                        else:
                            k.op("dve", "tensor_copy", r=[bb], w=[h2b[m]], out=dst, in_=src)
                for fb in range(NFB):
                    s_g = wslot()
                    k.dma("pool", s_g[:], wg_b.ap()[fb], r=[b_wconv], w=[s_g.b])
                    s_u = wslot()
                    k.dma("pool", s_u[:], wu_b.ap()[fb], r=[b_wconv], w=[s_u.b])
                    bg, bbg = nb()
                    for kk in range(32):
                        k.op("pe", "matmul", r=[s_g.b] + h2b, w=[bbg], sig=(kk == 31), out=bg[:, 0:512],
                             lhsT=s_g[:, kk * 128:(kk + 1) * 128], rhs=h2T[:, kk, :], start=(kk == 0),
                             stop=(kk == 31))
                    bu, bbu = nb()
                    for kk in range(32):
                        k.op("pe", "matmul", r=[s_u.b] + h2b, w=[bbu], sig=(kk == 31), out=bu[:, 0:512],
                             lhsT=s_u[:, kk * 128:(kk + 1) * 128], rhs=h2T[:, kk, :], start=(kk == 0),
                             stop=(kk == 31))
                    sgt = sg[fb % 2]
                    k.op("act", "activation", r=[bbg], w=[sgt.b], out=sgt[:], in_=bg[:, 0:512], func=AF.Silu)
                    k.op("dve", "tensor_tensor", r=[sgt.b, bbu], w=[bigb[fb]], out=big[:, fb * 512:(fb + 1) * 512],
                         in0=sgt[:], in1=bu[:, 0:512], op=ALU.mult)
                for q in range(4):
                    bk = [[nb() for _ in range(2)] for _ in range(4)]
                    for fg in range((NFB + 3) // 4):
                        nf = min(4, NFB - fg * 4)
                        s = wslot()
                        k.dma("pool", s[:, 0:nf * 1024], wd_b.ap()[q][:, fg * 4096:fg * 4096 + nf * 1024], r=[b_wconv], w=[s.b])
                        for f in range(nf):
                            fb = fg * 4 + f
                            for m in range(4):
                                for n2 in range(2):
                                    k.op("pe", "matmul", r=[s.b, bigb[fb]], w=[bk[m][n2][1]],
                                         sig=(fb == NFB - 1 or (f == nf - 1 and m == 3 and n2 == 1)),
                                         out=bk[m][n2][0][:, 0:512], lhsT=big[:, fb * 512 + m * 128:fb * 512 + (m + 1) * 128],
                                         rhs=s[:, f * 1024 + n2 * 512:f * 1024 + (n2 + 1) * 512],
                                         start=(fb == 0), stop=(fb == NFB - 1))
                    for m in range(4):
                        for n2 in range(2):
                            n = q * 2 + n2
                            xi = xsl[xcount[0] % 3]
                            xo = x2f[xcount[0] % 3]
                            xcount[0] += 1
                            rs = slice(t0 + m * 128, t0 + (m + 1) * 128)
                            cs = slice(n * 512, (n + 1) * 512)
                            ob = outb[(m, n)]
                            k.dma("sp", xi[:], out_d[rs, cs], r=[ob], w=[xi.b])
                            k.op("dve", "tensor_tensor", r=[bk[m][n2][1], xi.b], w=[xo.b], out=xo[:],
                                 in0=bk[m][n2][0][:, 0:512], in1=xi[:], op=ALU.add)
                            k.dma("sp", out_d[rs, cs], xo[:], r=[xo.b], w=[ob], sem_from=xo.b)
            k.wait_all("sp", list(outb.values()) + [t_.b for t_ in x2f] + dbg_bufs)
        k.emit()
    return nc


_CACHE = {}


def _consts(g):
    c = np.zeros((128, NCONST), np.float32)
    i = np.arange(128)
    S, Cc = np.meshgrid(i, i, indexing="ij")
    c[:, C_ID:C_ID + 128] = (S == Cc)
    c[:, C_ONE:C_ONE + 128] = 1.0
    c[:, C_UIN:C_UIN + 128] = (S <= Cc)
    c[:, C_LST:C_LST + 128] = (S > Cc)
    c[:, C_MST:C_MST + 128] = (Cc > S)
    bd = (S // 64 == Cc // 64)
    c[:, C_MBD:C_MBD + 128] = bd
    c[:, C_MNB:C_MNB + 128] = ~bd
    for h in range(4):
        slope = 2.0 ** (-8.0 * (4 * g + h + 1) / 16.0)
        kk, qq = S, Cc
        cur = np.where(qq >= kk, -slope * (qq - kk), -30000.0)
        prv = np.where(kk > qq, -slope * (qq + 128 - kk), -30000.0)
        c[:, C_AL + (2 * h) * 128:C_AL + (2 * h + 1) * 128] = cur
        c[:, C_AL + (2 * h + 1) * 128:C_AL + (2 * h + 2) * 128] = prv
    return c


def _prep_shared(inp):
    w_out, w_gate, w_up, w_down = inp["w_out"][0], inp["w_gate"][0], inp["w_up"][0], inp["w_down"][0]
    perm = np.zeros(4096, np.int64)
    for r in range(4):
        for loc in range(1024):
            hh, d = (loc % 512) // 128, loc % 128
            perm[r * 1024 + loc] = (0 if loc < 512 else 2048) + (4 * r + hh) * 128 + d
    wo = w_out[perm, :]
    sh = {}
    sh["wo_t"] = np.ascontiguousarray(wo.reshape(32, 128, 8, 512).transpose(2, 1, 0, 3)).reshape(8, 128, 32 * 512)
    sh["wg_t"] = np.ascontiguousarray(w_gate.reshape(32, 128, NFB, 128).transpose(2, 1, 0, 3)).reshape(NFB, 128, D)
    sh["wu_t"] = np.ascontiguousarray(w_up.reshape(32, 128, NFB, 128).transpose(2, 1, 0, 3)).reshape(NFB, 128, D)
    sh["wd_t"] = np.ascontiguousarray(w_down.reshape(NFB, 128, 4, 1024).transpose(2, 1, 0, 3)).reshape(4, 128, NFB * 1024)
    sh["anwbc"] = np.ascontiguousarray(np.broadcast_to(inp["attn_norm_w"][0][None, :], (128, D)))
    sh["fnwbc"] = np.ascontiguousarray(np.broadcast_to(inp["ffn_norm_w"][0][None, :], (128, D)))
    return sh


def _prep_group(inp, g):
    w_in = inp["w_in"][0]
    cols = []
    for t_ in range(4):
        for i in range(4):
            cols.append(t_ * 2048 + (4 * g + i) * 128)
    for h in range(4):
        cols.append(8224 + (4 * g + h) * 128)
    cols.append(10272 + g * 128)
    cols.append(10784 + g * 128)
    w1t = np.empty((NCB, 128, D), np.float32)
    for cb, c0 in enumerate(cols):
        w1t[cb] = w_in[:, c0:c0 + 128].reshape(32, 128, 128).transpose(1, 0, 2).reshape(128, D)
    abcols = [8192 + 4 * g + i for i in range(4)] + [8208 + 4 * g + i for i in range(4)]
    wab = np.ascontiguousarray(w_in[:, abcols].reshape(32, 128, 8).transpose(1, 0, 2)).reshape(128, 256)
    prm = np.zeros((128, NPRM), np.float32)
    cw = inp["conv_w"][0]
    for t_ in range(3):
        for i in range(4):
            ch = t_ * 2048 + (4 * g + i) * 128
            for j in range(4):
                prm[:, P_CW + (t_ * 4 + i) * 4 + j] = cw[j, ch:ch + 128]
    prm[:, P_ALOG:P_ALOG + 4] = inp["a_log"][0][4 * g:4 * g + 4][None, :]
    prm[:, P_DTB:P_DTB + 4] = inp["dt_bias"][0][4 * g:4 * g + 4][None, :]
    prm[:, P_DNW] = inp["dn_norm_w"][0]
    prm[:, P_QNW] = inp["q_norm_w"][0]
    prm[:, P_KNW] = inp["k_norm_w"][0]
    prm[:, P_SNK:P_SNK + 4] = inp["sinks"][0][4 * g:4 * g + 4][None, :]
    sel = np.zeros((128, 4), np.float32)
    sel[:, g] = 1.0
    return {"w1t": w1t, "wab": wab, "prm": prm, "consts": _consts(g), "sel": sel}


def make_in_maps(inp, SEQ):
    x = inp["x"]
    sh = _prep_shared(inp)
    TOK2 = SEQ // 4
    maps = []
    grp = [_prep_group(inp, g) for g in range(4)]
    for c in range(8):
        b, g = c // 4, c % 4
        m = dict(sh)
        m.update(grp[g])
        m["xb"] = np.ascontiguousarray(x[b, :SEQ])
        m["x2tok"] = np.ascontiguousarray(x[b, g * TOK2:(g + 1) * TOK2])
        maps.append(m)
    return maps


def kernel(**inputs):
    inp = {k_: np.asarray(v) for k_, v in inputs.items()}
    SEQ = inp["x"].shape[1]
    if SEQ not in _CACHE:
        _CACHE[SEQ] = build(SEQ)
    nc = _CACHE[SEQ]
    maps = make_in_maps(inp, SEQ)
    res = run_bass_kernel_spmd(nc, maps, core_ids=list(range(8)))
    TOK2 = SEQ // 4
    out = np.empty((2, SEQ, D), np.float32)
    for c in range(8):
        b, g = c // 4, c % 4
        out[b, g * TOK2:(g + 1) * TOK2] = np.asarray(res.results[c]["out"])
    return out
```

```python
import contextlib
import numpy as np
import concourse.bass as bass
import concourse.mybir as mybir
from concourse.bass_utils import run_bass_kernel_spmd

F32 = mybir.dt.float32
BF16 = mybir.dt.bfloat16
ALU = mybir.AluOpType
AF = mybir.ActivationFunctionType
ENGS = ("pe", "act", "dve", "pool", "sp")
EPS = 1e-6
D = 4096
FF = 11008
NFB = FF // 128
NCB = 22


class Buf:
    __slots__ = ("name", "writes", "reads", "sem", "semval", "excl")

    def __init__(self, name, excl=False):
        self.name = name
        self.excl = excl
        self.writes = {}
        self.reads = {}
        self.sem = None
        self.semval = 0


class K:
    RELAY = True

    def __init__(self, nc):
        self.nc = nc
        self.q = {e: [] for e in ENGS}
        self.cnt = {e: 0 for e in ENGS}
        self.waited = {e: {} for e in ENGS}
        self.semkeys = ["E_" + e for e in ENGS]
        self.relay_sem = "S_relay"
        self.semkeys.append(self.relay_sem)
        self.relay_cnt = 0
        self.tok_out = None
        self.tok_rows = 8192
        self.tok_in = None

    def newsem(self, name):
        key = "S%d_%s" % (len(self.semkeys), name)
        self.semkeys.append(key)
        return key

    def _wait(self, eng, evs):
        if eng == "pool" and K.RELAY:
            comp = {sk: v for sk, v in evs.items()
                    if sk.startswith("E_") and v > 0 and self.waited["pool"].get(sk, 0) < v}
            if comp:
                for sk, v in comp.items():
                    self.waited["pool"][sk] = v
                self._wait("sp", comp)
                self.relay_cnt += 16
                ti = self.relay_cnt // 16 - 1
                assert ti < self.tok_rows
                self.q["sp"].append(("dma", (self.tok_out[ti:ti + 1, :], self.tok_in, self.relay_sem)))
                self.q["pool"].append(("wait", (self.relay_sem, self.relay_cnt)))
            evs = {sk: v for sk, v in evs.items() if not sk.startswith("E_")}
        for sk, v in evs.items():
            if v <= 0 or (eng == "pe" and sk == "E_pe"):
                continue
            if self.waited[eng].get(sk, 0) >= v:
                continue
            self.waited[eng][sk] = v
            self.q[eng].append(("wait", (sk, v)))

    def _deps(self, eng, r, w):
        evs = {}
        for b in r:
            for sk, v in b.writes.items():
                if evs.get(sk, 0) < v:
                    evs[sk] = v
        for b in w:
            for d in (b.writes, b.reads):
                for sk, v in d.items():
                    if evs.get(sk, 0) < v:
                        evs[sk] = v
        self._wait(eng, evs)

    def op(self, eng, meth, r=(), w=(), sig=True, **kw):
        if eng != "pe":
            ex = [b for b in r if b.excl]
            if ex:
                r = [b for b in r if not b.excl]
                w = list(w) + ex
        self._deps(eng, r, w)
        sk = "E_" + eng
        if eng == "pe" and not sig:
            ev = self.cnt[eng] + 1
            inc = False
        else:
            self.cnt[eng] += 1
            ev = self.cnt[eng]
            inc = True
        self.q[eng].append(("op", (meth, kw, sk if inc else None)))
        for b in r:
            if b.reads.get(sk, 0) < ev:
                b.reads[sk] = ev
        for b in w:
            b.writes = {sk: ev}
            b.reads = {}

    def dma(self, eng, out, in_, r=(), w=(), sem_from=None):
        self._deps(eng, r, w)
        tgt = sem_from if sem_from is not None else (w[0] if len(w) else r[0])
        if tgt.sem is None:
            tgt.sem = self.newsem(tgt.name)
        tgt.semval += 16
        sk, v = tgt.sem, tgt.semval
        self.q[eng].append(("dma", (out, in_, sk)))
        for b in r:
            if b.reads.get(sk, 0) < v:
                b.reads[sk] = v
        for b in w:
            b.writes = {sk: v}
            b.reads = {}

    def wait_all(self, eng, bufs):
        evs = {}
        for b in bufs:
            for d in (b.writes, b.reads):
                for sk, v in d.items():
                    if evs.get(sk, 0) < v:
                        evs[sk] = v
        self._wait(eng, evs)

    def raw(self, eng, fn):
        self.q[eng].append(("raw", fn))

    def emit(self):
        nc = self.nc
        with contextlib.ExitStack() as st:
            sems = {}
            for sk in self.semkeys:
                sems[sk] = st.enter_context(nc.semaphore(sk))
            block = st.enter_context(nc.Block())
            q = self.q

            def run(engname, e):
                for kind, p in q[engname]:
                    if kind == "wait":
                        e.wait_ge(sems[p[0]], p[1])
                    elif kind == "op":
                        meth, kw, sk = p
                        ins = getattr(e, meth)(**kw)
                        if sk is not None:
                            ins.then_inc(sems[sk], 1)
                    elif kind == "dma":
                        out, in_, sk = p
                        e.dma_start(out=out, in_=in_).then_inc(sems[sk], 16)
                    else:
                        p(e, sems)

            @block.tensor
            def _(e):
                run("pe", e)

            @block.scalar
            def _(e):
                run("act", e)

            @block.vector
            def _(e):
                run("dve", e)

            @block.gpsimd
            def _(e):
                run("pool", e)

            @block.sync
            def _(e):
                run("sp", e)


class Arena:
    def __init__(self, nc, st, nbytes):
        self.n = nbytes // 2
        self.t = st.enter_context(nc.sbuf_tensor("arena", [128, self.n], BF16))
        self.off = 0

    def alloc(self, nbytes):
        nb_ = (nbytes + 31) // 32 * 32
        o = self.off
        self.off += nb_ // 2
        assert self.off <= self.n, "SBUF arena overflow: %d > %d" % (self.off * 2, self.n * 2)
        return o


class Tl:
    def __init__(self, arena, name, shape, dt):
        nel = 1
        for s in shape[1:]:
            nel *= s
        esz = 4 if dt == F32 else 2
        o = arena.alloc(nel * esz)
        ap = arena.t[:, o:o + nel * esz // 2]
        if dt != BF16:
            ap = ap.bitcast(dt)
        if len(shape) == 3:
            ap = ap.rearrange("p (a b) -> p a b", a=shape[1])
        self.ap = ap
        self.b = Buf(name)

    def __getitem__(self, idx):
        return self.ap[idx]


C_ID, C_ONE, C_UIN, C_LST, C_MST, C_MBD, C_MNB, C_AL = [i * 128 for i in range(8)]
NCONST = C_AL + 8 * 128
P_CW, P_ALOG, P_DTB, P_DNW, P_QNW, P_KNW, P_SNK = 0, 48, 52, 56, 57, 58, 59
NPRM = 64


def build(SEQ, dbg=None, skip=(), np1=None):
    nc = bass.Bass("TRN2", target_bir_lowering=False)
    NP1 = SEQ // 512 if np1 is None else np1
    TOK2 = SEQ // 4
    NP2 = TOK2 // 512
    dram = lambda n, s, d, kind=None: (nc.dram_tensor(n, s, d, kind=kind) if kind else nc.dram_tensor(n, s, d))
    xb = dram("xb", [SEQ, D], F32, "ExternalInput").ap()
    consts = dram("consts", [128, NCONST], F32, "ExternalInput").ap()
    prm = dram("prm", [128, NPRM], F32, "ExternalInput").ap()
    anwbc_d = dram("anwbc", [128, D], F32, "ExternalInput").ap()
    fnwbc_d = dram("fnwbc", [128, D], F32, "ExternalInput").ap()
    w1t = dram("w1t", [NCB, 128, D], F32, "ExternalInput").ap()
    wab_d = dram("wab", [128, 256], F32, "ExternalInput").ap()
    wo_t = dram("wo_t", [8, 128, 32 * 512], F32, "ExternalInput").ap()
    wg_t = dram("wg_t", [NFB, 128, D], F32, "ExternalInput").ap()
    wu_t = dram("wu_t", [NFB, 128, D], F32, "ExternalInput").ap()
    wd_t = dram("wd_t", [4, 128, NFB * 1024], F32, "ExternalInput").ap()
    out_d = dram("out", [TOK2, D], F32, "ExternalOutput").ap()
    x2tok = dram("x2tok", [TOK2, D], F32, "ExternalInput").ap()
    sel_d = dram("sel", [128, 4], F32, "ExternalInput").ap()
    NPT = SEQ // 512
    wo_b = dram("wo_b", [8, 128, 32 * 512], BF16)
    wg_b = dram("wg_b", [NFB, 128, D], BF16)
    wu_b = dram("wu_b", [NFB, 128, D], BF16)
    wd_b = dram("wd_b", [4, 128, NFB * 1024], BF16)
    mix_in_l = [dram("mix_in%d" % p_, [1024, 512], BF16) for p_ in range(NPT)]
    mix_all_l = [dram("mix_all%d" % p_, [4096, 512], BF16) for p_ in range(NPT)]
    dbg_d = {}
    if dbg:
        for name, shape, dt in dbg:
            dbg_d[name] = dram("dbg_" + name, shape, dt, "ExternalOutput").ap()

    k = K(nc)
    tok_d = dram("tok_d", [8192, 16], F32)
    k.tok_out = tok_d.ap()
    k.tok_in = consts[0:1, 0:16]
    b_mixin_l = [Buf("mixin%d" % p_) for p_ in range(NPT)]
    b_wconv = Buf("wconv")
    convsem = k.newsem("conv")
    conv_jobs = []
    for n_ in range(8):
        for a_ in range(4):
            conv_jobs.append((wo_b.ap()[n_][:, a_ * 4096:(a_ + 1) * 4096], wo_t[n_][:, a_ * 4096:(a_ + 1) * 4096]))
    for fb_ in range(NFB):
        conv_jobs.append((wg_b.ap()[fb_], wg_t[fb_]))
        conv_jobs.append((wu_b.ap()[fb_], wu_t[fb_]))
    for q_ in range(4):
        for fg_ in range((NFB + 3) // 4):
            nf_ = min(4, NFB - fg_ * 4)
            conv_jobs.append((wd_b.ap()[q_][:, fg_ * 4096:fg_ * 4096 + nf_ * 1024],
                              wd_t[q_][:, fg_ * 4096:fg_ * 4096 + nf_ * 1024]))
    conv_state = {"i": 0, "cnt": 0}

    def conv_issue(n):
        while n > 0 and conv_state["i"] < len(conv_jobs):
            o_, i_ = conv_jobs[conv_state["i"]]
            k.q["pool"].append(("dma", (o_, i_, convsem)))
            conv_state["i"] += 1
            conv_state["cnt"] += 16
            n -= 1
    b_mixall_l = [Buf("mixall%d" % p_) for p_ in range(NPT)]
    b_out = Buf("outd")

    with contextlib.ExitStack() as st0:
        banks = []
        for i in range(8):
            t = st0.enter_context(nc.psum_tensor("bank%d" % i, [128, 512], F32))
            banks.append((t, Buf("bank%d" % i, excl=True)))
        rot = [0]

        def nb():
            i = rot[0] % 8
            rot[0] += 1
            return banks[i]

        arena = Arena(nc, st0, 206 * 1024)
        cst = Tl(arena, "cst", [128, NCONST], F32)
        prt = Tl(arena, "prt", [128, NPRM], F32)
        idb = Tl(arena, "idb", [128, 128], BF16)
        oneb = Tl(arena, "oneb", [128, 128], BF16)
        arena_mark = arena.off
        k.dma("sp", cst[:], consts, w=[cst.b])
        k.dma("sp", prt[:], prm, w=[prt.b])
        k.op("dve", "tensor_copy", r=[cst.b], w=[idb.b], out=idb[:], in_=cst[:, C_ID:C_ID + 128])
        k.op("dve", "tensor_copy", r=[cst.b], w=[oneb.b], out=oneb[:], in_=cst[:, C_ONE:C_ONE + 128])
        ident = cst[:, C_ID:C_ID + 128]
        ones_f = cst[:, C_ONE:C_ONE + 128]

        def dbg_store(name, ap_dram_idx, src_ap, srcbuf):
            if name in dbg_d:
                k.dma("sp", dbg_d[name][ap_dram_idx], src_ap, r=[srcbuf])
                dbg_bufs.append(srcbuf)
        dbg_bufs = []

        with contextlib.ExitStack() as st:
            p1_tiles = []

            def T(n, s, d):
                t = Tl(arena, n, s, d)
                p1_tiles.append(t)
                return t
            xin = [T("xin%d" % i, [128, D], F32) for i in range(2)]
            xs = T("xs", [128, D], BF16)
            anwbc = T("anwbc_s", [128, D], F32)
            hT = T("hT", [128, 32, 512], BF16)
            hTb = [Buf("hT%d" % m) for m in range(4)]
            NW = 3
            wsl = [T("wsl%d" % i, [128, D], BF16) for i in range(NW)]
            wabf = T("wabf", [128, 256], F32)
            wabb = T("wabb", [128, 256], BF16)
            ssq = T("ssq", [128, 1], F32)
            rstd = T("rstd", [128, 1], F32)
            nA = T("nA", [128, 4], F32)
            dnw2 = T("dnw2", [128, 1], F32)
            knw2 = T("knw2", [128, 1], F32)
            esink = T("esink", [128, 4], F32)
            k.dma("sp", anwbc[:], anwbc_d, w=[anwbc.b])
            k.dma("sp", wabf[:], wab_d, w=[wabf.b])
            k.op("dve", "tensor_copy", r=[wabf.b], w=[wabb.b], out=wabb[:], in_=wabf[:])
            k.op("act", "activation", r=[prt.b], w=[nA.b], out=nA[:], in_=prt[:, P_ALOG:P_ALOG + 4], func=AF.Exp)
            k.op("dve", "tensor_scalar", r=[nA.b], w=[nA.b], out=nA[:], in0=nA[:], scalar1=-1.0, scalar2=None,
                 op0=ALU.mult)
            k.op("act", "activation", r=[prt.b], w=[esink.b], out=esink[:], in_=prt[:, P_SNK:P_SNK + 4], func=AF.Exp)
            k.op("dve", "tensor_scalar", r=[prt.b], w=[dnw2.b], out=dnw2[:], in0=prt[:, P_DNW:P_DNW + 1],
                 scalar1=float(np.sqrt(128.0)), scalar2=None, op0=ALU.mult)
            k.op("dve", "tensor_scalar", r=[prt.b], w=[knw2.b], out=knw2[:], in0=prt[:, P_KNW:P_KNW + 1],
                 scalar1=float(np.sqrt(128.0)), scalar2=None, op0=ALU.mult)

            cb_order = []
            for i in (0, 1, 2, 3):
                cb_order += [i, 4 + i, 8 + i, 12 + i]
            cb_order += [16, 17, 18, 19, 20, 21]
            wseq = [(p, cb) for p in range(NP1) for cb in cb_order]
            wpos = {pc: n for n, pc in enumerate(wseq)}
            wstate = {"issued": 0}
            conv_per = -(-len(conv_jobs) // max(1, len(wseq)))

            def wprefetch(upto):
                while wstate["issued"] <= min(upto, len(wseq) - 1):
                    i = wstate["issued"]
                    s = wsl[i % NW]
                    k.dma("pool", s[:], w1t[wseq[i][1]], w=[s.b])
                    wstate["issued"] += 1
                    conv_issue(conv_per)

            def inproj(p, cb):
                i = wpos[(p, cb)]
                wprefetch(i + NW - 1)
                s = wsl[i % NW]
                bt, bb = nb()
                for kk in range(32):
                    k.op("pe", "matmul", r=[s.b] + hTb, w=[bb], sig=(kk == 31), out=bt[:, 0:512],
                         lhsT=s[:, kk * 128:(kk + 1) * 128], rhs=hT[:, kk, :], start=(kk == 0), stop=(kk == 31))
                return bt, bb

            raw = T("raw", [128, 515], F32)
            halo = [[T("halo%d_%d" % (t_, i), [128, 3], F32) for i in range(4)] for t_ in range(3)]
            for t_ in range(3):
                for i in range(4):
                    k.op("dve", "memset", w=[halo[t_][i].b], ap=halo[t_][i][:], constant=0.0)
            acc = T("acc", [128, 512], F32)
            th = T("th", [128, 512], F32)
            sqb = T("sqb", [128, 512], BF16)
            rr = T("rr", [128, 512], F32)
            qn = [T("qn%d" % i, [128, 512], F32) for i in range(2)]
            qnb = [T("qnb%d" % i, [128, 512], BF16) for i in range(2)]
            knb = [T("knb%d" % i, [128, 512], BF16) for i in range(2)]
            v2b = T("v2b", [128, 512], BF16)
            vtok = [T("vtok%d" % i, [128, 512], BF16) for i in range(2)]
            z2 = [T("z2_%d" % i, [128, 512], F32) for i in range(2)]
            oT = [T("oT%d" % i, [128, 512], F32) for i in range(2)]
            mixt = [T("mixt%d" % i, [128, 512], BF16) for i in range(2)]
            Sf = [T("Sf%d" % i, [128, 128], F32) for i in range(4)]
            Sb = [T("Sb%d" % i, [128, 128], BF16) for i in range(4)]
            for i in range(4):
                k.op("dve", "memset", w=[Sf[i].b], ap=Sf[i][:], constant=0.0)
                k.op("dve", "memset", w=[Sb[i].b], ap=Sb[i][:], constant=0.0)
            gnames = ["tA", "eA", "sp", "g", "tb", "beta", "nbeta", "gcc", "egc", "tail"]
            gt = [{n: T("g_%s%d" % (n, j), [128, 4], F32) for n in gnames} for j in range(4)]
            f32names = ["Gh", "Dt", "E", "egb", "EMi", "EMs", "Nf", "A0", "A0T", "Nof", "NofT", "P0", "P1",
                        "A1", "A1T", "XT", "Y"]
            bfnames = ["PT", "T2T", "qd", "ktl", "r2n", "vnw"]
            hb = [dict([(n, T("hb_%s%d" % (n, par), [128, 128], F32)) for n in f32names] +
                       [(n, T("hb_%s%d" % (n, par), [128, 128], BF16)) for n in bfnames]) for par in range(2)]
            sraw = T("sraw", [128, 512], F32)
            sqn = [T("sqn%d" % h, [128, 512], BF16) for h in range(4)]
            skn = T("skn", [128, 640], BF16)
            svb = T("svb", [128, 512], BF16)
            svt = T("svt", [128, 5, 128], BF16)
            smix = [T("smix%d" % i, [128, 512], BF16) for i in range(2)]
            sw = [dict(tc=T("sw_tc%d" % par, [128, 128], F32), tp=T("sw_tp%d" % par, [128, 128], F32),
                       pc=T("sw_pc%d" % par, [128, 128], BF16), pp=T("sw_pp%d" % par, [128, 128], BF16),
                       rd=T("sw_rd%d" % par, [128, 128], F32)) for par in range(2)]

            cwcol = lambda t_, i, j: prt[:, P_CW + (t_ * 4 + i) * 4 + j:P_CW + (t_ * 4 + i) * 4 + j + 1]
            hbcount = [0]
            swcount = [0]

            for p in range(NP1):
                for m in range(4):
                    xi = xin[(4 * p + m) % 2]
                    r0 = p * 512 + m * 128
                    k.dma("sp", xi[:], xb[r0:r0 + 128, :], w=[xi.b])
                    k.op("act", "activation", r=[xi.b], w=[xs.b, ssq.b], out=xs[:], in_=xi[:], func=AF.Square,
                         accum_out=ssq[:, 0:1])
                    k.op("act", "activation", r=[ssq.b], w=[rstd.b], out=rstd[:], in_=ssq[:], func=AF.Ln,
                         scale=1.0 / D, bias=EPS)
                    k.op("act", "activation", r=[rstd.b], w=[rstd.b], out=rstd[:], in_=rstd[:], func=AF.Exp,
                         scale=-0.5)
                    k.op("dve", "scalar_tensor_tensor", r=[xi.b, rstd.b, anwbc.b], w=[xs.b], out=xs[:], in0=xi[:],
                         scalar=rstd[:, 0:1], in1=anwbc[:], op0=ALU.mult, op1=ALU.mult)
                    for kq in range(8):
                        bt, bb = nb()
                        btb = bt[:].bitcast(BF16)
                        for j in range(4):
                            kk = kq * 4 + j
                            k.op("pe", "transpose", r=[xs.b, idb.b], w=[bb], sig=(j == 3),
                                 out=btb[:, j * 128:(j + 1) * 128], in_=xs[:, kk * 128:(kk + 1) * 128],
                                 identity=idb[:])
                        src = btb[:, 0:512].rearrange("p (a b) -> p a b", a=4)
                        dst = hT[:, kq * 4:(kq + 1) * 4, m * 128:(m + 1) * 128]
                        if kq % 2 == 0:
                            k.op("act", "activation", r=[bb], w=[hTb[m]], out=dst, in_=src, func=AF.Copy)
                        else:
                            k.op("dve", "tensor_copy", r=[bb], w=[hTb[m]], out=dst, in_=src)
                if p == 0 and "hT" in dbg_d:
                    k.dma("sp", dbg_d["hT"], hT[:], r=hTb)
                    dbg_bufs.extend(hTb)

                for j in range(4):
                    G = gt[j]
                    bt, bb = nb()
                    for kk in range(32):
                        k.op("pe", "matmul", r=[wabb.b] + hTb, w=[bb], sig=(kk == 31), out=bt[:, 0:8],
                             lhsT=hT[:, kk, j * 128:(j + 1) * 128], rhs=wabb[:, kk * 8:(kk + 1) * 8],
                             start=(kk == 0), stop=(kk == 31))
                    k.op("dve", "tensor_tensor", r=[bb, prt.b], w=[G["tA"].b], out=G["tA"][:], in0=bt[:, 0:4],
                         in1=prt[:, P_DTB:P_DTB + 4], op=ALU.add)
                    k.op("act", "activation", r=[bb], w=[G["tb"].b], out=G["tb"][:], in_=bt[:, 4:8], func=AF.Exp,
                         scale=-1.0)
                    k.op("act", "activation", r=[G["tb"].b], w=[G["tb"].b], out=G["tb"][:], in_=G["tb"][:],
                         func=AF.Ln, bias=1.0)
                    k.op("act", "activation", r=[G["tb"].b], w=[G["beta"].b], out=G["beta"][:], in_=G["tb"][:],
                         func=AF.Exp, scale=-1.0)
                    k.op("act", "activation", r=[G["tA"].b], w=[G["eA"].b], out=G["eA"][:], in_=G["tA"][:],
                         func=AF.Exp)
                    k.op("act", "activation", r=[G["eA"].b], w=[G["sp"].b], out=G["sp"][:], in_=G["eA"][:],
                         func=AF.Ln, bias=1.0)
                    k.op("dve", "tensor_tensor", r=[G["sp"].b, nA.b], w=[G["g"].b], out=G["g"][:], in0=G["sp"][:],
                         in1=nA[:], op=ALU.mult)
                    k.op("dve", "tensor_scalar", r=[G["beta"].b], w=[G["nbeta"].b], out=G["nbeta"][:],
                         in0=G["beta"][:], scalar1=-1.0, scalar2=None, op0=ALU.mult)
                    bt2, bb2 = nb()
                    k.op("pe", "matmul", r=[cst.b, G["g"].b], w=[bb2], out=bt2[:, 0:4],
                         lhsT=cst[:, C_UIN:C_UIN + 128], rhs=G["g"][:], start=True, stop=True)
                    k.op("act", "activation", r=[bb2], w=[G["gcc"].b], out=G["gcc"][:], in_=bt2[:, 0:4],
                         func=AF.Copy)
                    k.op("act", "activation", r=[bb2], w=[G["egc"].b], out=G["egc"][:], in_=bt2[:, 0:4],
                         func=AF.Exp)
                    bt3, bb3 = nb()
                    k.op("pe", "matmul", r=[cst.b, G["g"].b], w=[bb3], out=bt3[:, 0:4],
                         lhsT=cst[:, C_LST:C_LST + 128], rhs=G["g"][:], start=True, stop=True)
                    k.op("act", "activation", r=[bb3], w=[G["tail"].b], out=G["tail"][:], in_=bt3[:, 0:4],
                         func=AF.Exp)

                def head_pre(i):
                    par = i % 2
                    for t_ in range(3):
                        bt, bb = inproj(p, t_ * 4 + i)
                        k.op("dve", "tensor_copy", r=[halo[t_][i].b], w=[raw.b], out=raw[:, 0:3],
                             in_=halo[t_][i][:])
                        k.op("act", "activation", r=[bb], w=[raw.b], out=raw[:, 3:515], in_=bt[:, 0:512],
                             func=AF.Copy)
                        k.op("dve", "tensor_scalar", r=[raw.b, prt.b], w=[acc.b], out=acc[:], in0=raw[:, 0:512],
                             scalar1=cwcol(t_, i, 0), scalar2=None, op0=ALU.mult)
                        for j in range(1, 4):
                            k.op("dve", "scalar_tensor_tensor", r=[raw.b, prt.b, acc.b], w=[acc.b], out=acc[:],
                                 in0=raw[:, j:j + 512], scalar=cwcol(t_, i, j), in1=acc[:], op0=ALU.mult,
                                 op1=ALU.add)
                        k.op("dve", "tensor_copy", r=[raw.b], w=[halo[t_][i].b], out=halo[t_][i][:],
                             in_=raw[:, 512:515])
                        k.op("act", "activation", r=[acc.b], w=[th.b], out=th[:], in_=acc[:], func=AF.Exp,
                             scale=-1.0)
                        k.op("act", "activation", r=[th.b], w=[th.b], out=th[:], in_=th[:], func=AF.Ln, bias=1.0)
                        k.op("act", "activation", r=[th.b], w=[th.b], out=th[:], in_=th[:], func=AF.Exp, scale=-1.0)
                        k.op("dve", "tensor_tensor", r=[th.b, acc.b], w=[th.b], out=th[:], in0=th[:], in1=acc[:],
                             op=ALU.mult)
                        if t_ < 2:
                            k.op("act", "activation", r=[th.b], w=[sqb.b], out=sqb[:], in_=th[:], func=AF.Square)
                            bt2, bb2 = nb()
                            k.op("pe", "matmul", r=[oneb.b, sqb.b], w=[bb2], out=bt2[:, 0:512], lhsT=oneb[:],
                                 rhs=sqb[:], start=True, stop=True)
                            k.op("act", "activation", r=[bb2], w=[rr.b], out=rr[:], in_=bt2[:, 0:512], func=AF.Ln,
                                 bias=EPS)
                            k.op("act", "activation", r=[rr.b], w=[rr.b], out=rr[:], in_=rr[:], func=AF.Exp,
                                 scale=-0.5)
                            if t_ == 0:
                                k.op("dve", "scalar_tensor_tensor", r=[th.b, rr.b], w=[qn[par].b], out=qn[par][:],
                                     in0=th[:], scalar=float(128.0 ** -0.5), in1=rr[:], op0=ALU.mult,
                                     op1=ALU.mult)
                                k.op("act", "activation", r=[qn[par].b], w=[qnb[par].b], out=qnb[par][:],
                                     in_=qn[par][:], func=AF.Copy)
                            else:
                                k.op("dve", "tensor_tensor", r=[th.b, rr.b], w=[knb[par].b], out=knb[par][:],
                                     in0=th[:], in1=rr[:], op=ALU.mult)
                        else:
                            k.op("act", "activation", r=[th.b], w=[v2b.b], out=v2b[:], in_=th[:], func=AF.Copy)
                            bt2, bb2 = nb()
                            btb = bt2[:].bitcast(BF16)
                            for j in range(4):
                                k.op("pe", "transpose", r=[v2b.b, idb.b], w=[bb2], sig=(j == 3),
                                     out=btb[:, j * 128:(j + 1) * 128], in_=v2b[:, j * 128:(j + 1) * 128],
                                     identity=idb[:])
                            k.op("act", "activation", r=[bb2], w=[vtok[par].b], out=vtok[par][:],
                                 in_=btb[:, 0:512], func=AF.Copy)
                    bt, bb = inproj(p, 12 + i)
                    k.op("act", "activation", r=[bb], w=[th.b], out=th[:], in_=bt[:, 0:512], func=AF.Exp, scale=-1.0)
                    k.op("act", "activation", r=[th.b], w=[th.b], out=th[:], in_=th[:], func=AF.Ln, bias=1.0)
                    k.op("act", "activation", r=[th.b], w=[th.b], out=th[:], in_=th[:], func=AF.Exp, scale=-1.0)
                    k.op("dve", "tensor_tensor", r=[th.b, bb], w=[z2[par].b], out=z2[par][:], in0=th[:],
                         in1=bt[:, 0:512], op=ALU.mult)


                def gdn_block(i, j):
                    par = i % 2
                    H = hb[i % 2]
                    if True:
                        G = gt[j]
                        cs = slice(j * 128, (j + 1) * 128)
                        gcol = G["g"][:, i:i + 1]
                        k.op("dve", "tensor_scalar", r=[cst.b, G["g"].b], w=[H["Gh"].b], out=H["Gh"][:], in0=ones_f,
                             scalar1=gcol, scalar2=None, op0=ALU.mult)
                        btg, bbg = nb()
                        yield
                        k.op("pe", "matmul", r=[H["Gh"].b, cst.b], w=[bbg], out=btg[:, 0:128], lhsT=H["Gh"][:],
                             rhs=cst[:, C_UIN:C_UIN + 128], start=True, stop=True)
                        k.op("dve", "tensor_scalar", r=[bbg, G["gcc"].b], w=[H["Dt"].b], out=H["Dt"][:],
                             in0=btg[:, 0:128], scalar1=G["gcc"][:, i:i + 1], scalar2=0.0, op0=ALU.subtract,
                             op1=ALU.min)
                        k.op("act", "activation", r=[bbg], w=[H["egb"].b], out=H["egb"][:], in_=btg[:, 0:128],
                             func=AF.Exp)
                        k.op("act", "activation", r=[H["Dt"].b], w=[H["E"].b], out=H["E"][:], in_=H["Dt"][:],
                             func=AF.Exp)
                        k.op("dve", "tensor_tensor", r=[H["E"].b, cst.b], w=[H["EMi"].b], out=H["EMi"][:],
                             in0=H["E"][:], in1=cst[:, C_UIN:C_UIN + 128], op=ALU.mult)
                        k.op("dve", "tensor_tensor", r=[H["E"].b, cst.b], w=[H["EMs"].b], out=H["EMs"][:],
                             in0=H["E"][:], in1=cst[:, C_MST:C_MST + 128], op=ALU.mult)
                        k.op("dve", "tensor_tensor", r=[qn[par].b, H["egb"].b], w=[H["qd"].b], out=H["qd"][:],
                             in0=qn[par][:, cs], in1=H["egb"][:], op=ALU.mult)
                        btk, bbk = nb()
                        yield
                        k.op("pe", "matmul", r=[knb[par].b], w=[bbk], out=btk[:, 0:128], lhsT=knb[par][:, cs],
                             rhs=knb[par][:, cs], start=True, stop=True)
                        k.op("dve", "scalar_tensor_tensor", r=[bbk, G["beta"].b, H["EMs"].b], w=[H["Nf"].b],
                             out=H["Nf"][:], in0=btk[:, 0:128], scalar=G["beta"][:, i:i + 1], in1=H["EMs"][:],
                             op0=ALU.mult, op1=ALU.mult)
                        btq, bbq = nb()
                        yield
                        k.op("pe", "matmul", r=[knb[par].b, qnb[par].b], w=[bbq], out=btq[:, 0:128],
                             lhsT=knb[par][:, cs], rhs=qnb[par][:, cs], start=True, stop=True)
                        k.op("dve", "tensor_tensor", r=[bbq, H["EMi"].b], w=[H["PT"].b], out=H["PT"][:],
                             in0=btq[:, 0:128], in1=H["EMi"][:], op=ALU.mult)
                        btt, bbt = nb()
                        bttb = btt[:].bitcast(BF16)
                        yield
                        k.op("pe", "transpose", r=[knb[par].b, idb.b], w=[bbt], out=bttb[:, 0:128],
                             in_=knb[par][:, cs], identity=idb[:])
                        k.op("act", "activation", r=[bbt, G["tail"].b], w=[H["ktl"].b], out=H["ktl"][:],
                             in_=bttb[:, 0:128], func=AF.Copy, scale=G["tail"][:, i:i + 1])
                        k.op("dve", "tensor_tensor", r=[H["Nf"].b, cst.b], w=[H["A0"].b], out=H["A0"][:],
                             in0=H["Nf"][:], in1=cst[:, C_MBD:C_MBD + 128], op=ALU.mult)
                        k.op("dve", "tensor_tensor", r=[H["Nf"].b, cst.b], w=[H["Nof"].b], out=H["Nof"][:],
                             in0=H["Nf"][:], in1=cst[:, C_MNB:C_MNB + 128], op=ALU.mult)
                        btn, bbn = nb()
                        yield
                        k.op("pe", "transpose", r=[H["Nf"].b, cst.b], w=[bbn], out=btn[:, 0:128], in_=H["Nf"][:],
                             identity=ident)
                        k.op("dve", "tensor_tensor", r=[bbn, cst.b], w=[H["A0T"].b], out=H["A0T"][:],
                             in0=btn[:, 0:128], in1=cst[:, C_MBD:C_MBD + 128], op=ALU.mult)
                        k.op("dve", "tensor_tensor", r=[bbn, cst.b], w=[H["NofT"].b], out=H["NofT"][:],
                             in0=btn[:, 0:128], in1=cst[:, C_MNB:C_MNB + 128], op=ALU.mult)
                        k.op("dve", "tensor_tensor", r=[cst.b, H["A0"].b], w=[H["P0"].b], out=H["P0"][:], in0=ident,
                             in1=H["A0"][:], op=ALU.subtract)
                        A, AT, Pc = H["A0"], H["A0T"], H["P0"]
                        An, ATn, Pn = H["A1"], H["A1T"], H["P1"]
                        for lev in range(5):
                            last = (lev == 4)
                            b1, bb1 = nb()
                            yield
                            k.op("pe", "matmul", r=[A.b, AT.b], w=[bb1], out=b1[:, 0:128], lhsT=A[:], rhs=AT[:],
                                 start=True, stop=True)
                            if not last:
                                b2, bb2 = nb()
                                k.op("pe", "matmul", r=[A.b, AT.b], w=[bb2], out=b2[:, 0:128], lhsT=AT[:], rhs=A[:],
                                     start=True, stop=True)
                            k.op("act", "activation", r=[bb1], w=[ATn.b], out=ATn[:], in_=b1[:, 0:128], func=AF.Copy)
                            if not last:
                                k.op("act", "activation", r=[bb2], w=[An.b], out=An[:], in_=b2[:, 0:128],
                                     func=AF.Copy)
                            b3, bb3 = nb()
                            yield
                            k.op("pe", "matmul", r=[ATn.b, Pc.b], w=[bb3], out=b3[:, 0:128], lhsT=ATn[:], rhs=Pc[:],
                                 start=True, stop=True)
                            k.op("dve", "tensor_tensor", r=[bb3, Pc.b], w=[Pn.b], out=Pn[:], in0=b3[:, 0:128],
                                 in1=Pc[:], op=ALU.add)
                            A, An = An, A
                            AT, ATn = ATn, AT
                            Pc, Pn = Pn, Pc
                        X = Pc
                        b1, bb1 = nb()
                        yield
                        k.op("pe", "transpose", r=[X.b, cst.b], w=[bb1], out=b1[:, 0:128], in_=X[:], identity=ident)
                        k.op("act", "activation", r=[bb1], w=[H["XT"].b], out=H["XT"][:], in_=b1[:, 0:128],
                             func=AF.Copy)
                        b2, bb2 = nb()
                        yield
                        k.op("pe", "matmul", r=[H["NofT"].b, X.b], w=[bb2], out=b2[:, 0:128], lhsT=H["NofT"][:],
                             rhs=X[:], start=True, stop=True)
                        k.op("act", "activation", r=[bb2], w=[H["Y"].b], out=H["Y"][:], in_=b2[:, 0:128],
                             func=AF.Copy)
                        b3, bb3 = nb()
                        yield
                        k.op("pe", "matmul", r=[H["XT"].b, H["Y"].b], w=[bb3], out=b3[:, 0:128], lhsT=H["XT"][:],
                             rhs=H["Y"][:], start=True, stop=True)
                        k.op("dve", "tensor_tensor", r=[X.b, bb3], w=[H["T2T"].b], out=H["T2T"][:], in0=X[:],
                             in1=b3[:, 0:128], op=ALU.subtract)

                        b4, bb4 = nb()
                        yield
                        k.op("pe", "matmul", r=[knb[par].b, Sb[i].b], w=[bb4], out=b4[:, 0:128], lhsT=knb[par][:, cs],
                             rhs=Sb[i][:], start=True, stop=True)
                        k.op("dve", "scalar_tensor_tensor", r=[bb4, G["egc"].b, vtok[par].b], w=[H["r2n"].b],
                             out=H["r2n"][:], in0=b4[:, 0:128], scalar=G["egc"][:, i:i + 1], in1=vtok[par][:, cs],
                             op0=ALU.mult, op1=ALU.subtract)
                        b5, bb5 = nb()
                        yield
                        k.op("pe", "matmul", r=[H["T2T"].b, H["r2n"].b], w=[bb5], out=b5[:, 0:128], lhsT=H["T2T"][:],
                             rhs=H["r2n"][:], start=True, stop=True)
                        k.op("act", "activation", r=[bb5, G["nbeta"].b], w=[H["vnw"].b], out=H["vnw"][:],
                             in_=b5[:, 0:128], func=AF.Copy, scale=G["nbeta"][:, i:i + 1])
                        b6, bb6 = nb()
                        yield
                        k.op("pe", "matmul", r=[Sb[i].b, H["qd"].b], w=[bb6], sig=False, out=b6[:, 0:128],
                             lhsT=Sb[i][:], rhs=H["qd"][:], start=True, stop=False)
                        yield
                        k.op("pe", "matmul", r=[H["vnw"].b, H["PT"].b], w=[bb6], out=b6[:, 0:128], lhsT=H["vnw"][:],
                             rhs=H["PT"][:], start=False, stop=True)
                        k.op("act", "activation", r=[bb6], w=[oT[par].b], out=oT[par][:, cs], in_=b6[:, 0:128],
                             func=AF.Copy)
                        b7, bb7 = nb()
                        yield
                        k.op("pe", "matmul", r=[H["ktl"].b, H["vnw"].b], w=[bb7], out=b7[:, 0:128], lhsT=H["ktl"][:],
                             rhs=H["vnw"][:], start=True, stop=True)
                        k.op("dve", "scalar_tensor_tensor", r=[Sf[i].b, H["egb"].b, bb7], w=[Sf[i].b], out=Sf[i][:],
                             in0=Sf[i][:], scalar=H["egb"][:, 127:128], in1=b7[:, 0:128], op0=ALU.mult, op1=ALU.add)
                        k.op("act", "activation", r=[Sf[i].b], w=[Sb[i].b], out=Sb[i][:], in_=Sf[i][:], func=AF.Copy)


                def head_post(i):
                    par = i % 2
                    k.op("act", "activation", r=[oT[par].b], w=[sqb.b], out=sqb[:], in_=oT[par][:], func=AF.Square)
                    bt2, bb2 = nb()
                    k.op("pe", "matmul", r=[oneb.b, sqb.b], w=[bb2], out=bt2[:, 0:512], lhsT=oneb[:], rhs=sqb[:],
                         start=True, stop=True)
                    k.op("act", "activation", r=[bb2], w=[rr.b], out=rr[:], in_=bt2[:, 0:512], func=AF.Ln, bias=128.0 * EPS)
                    k.op("act", "activation", r=[rr.b], w=[rr.b], out=rr[:], in_=rr[:], func=AF.Exp, scale=-0.5)
                    k.op("dve", "tensor_tensor", r=[oT[par].b, rr.b], w=[rr.b], out=rr[:], in0=oT[par][:], in1=rr[:],
                         op=ALU.mult)
                    k.op("dve", "scalar_tensor_tensor", r=[rr.b, dnw2.b, z2[par].b], w=[mixt[par].b],
                         out=mixt[par][:], in0=rr[:], scalar=dnw2[:, 0:1], in1=z2[par][:], op0=ALU.mult,
                         op1=ALU.mult)
                    k.dma("sp", mix_in_l[p].ap()[i * 128:(i + 1) * 128, :], mixt[par][:],
                          r=[mixt[par].b], w=[b_mixin_l[p]], sem_from=mixt[par].b)


                for pair in (() if "gdn" in skip else ((0, 1), (2, 3))):
                    for i in pair:
                        head_pre(i)
                    for j in range(4):
                        gens = [gdn_block(i, j) for i in pair]
                        while gens:
                            for g_ in list(gens):
                                try:
                                    next(g_)
                                except StopIteration:
                                    gens.remove(g_)
                    for i in pair:
                        head_post(i)

                if "swa" in skip:
                    continue
                for h in range(4):
                    bt, bb = inproj(p, 16 + h)
                    k.op("act", "activation", r=[bb], w=[sraw.b], out=sraw[:], in_=bt[:, 0:512], func=AF.Copy)
                    k.op("act", "activation", r=[sraw.b], w=[sqb.b], out=sqb[:], in_=sraw[:], func=AF.Square)
                    bt2, bb2 = nb()
                    k.op("pe", "matmul", r=[oneb.b, sqb.b], w=[bb2], out=bt2[:, 0:512], lhsT=oneb[:], rhs=sqb[:],
                         start=True, stop=True)
                    k.op("act", "activation", r=[bb2], w=[rr.b], out=rr[:], in_=bt2[:, 0:512], func=AF.Ln, bias=128.0 * EPS)
                    k.op("act", "activation", r=[rr.b], w=[rr.b], out=rr[:], in_=rr[:], func=AF.Exp, scale=-0.5)
                    k.op("dve", "scalar_tensor_tensor", r=[sraw.b, prt.b, rr.b], w=[sqn[h].b], out=sqn[h][:],
                         in0=sraw[:], scalar=prt[:, P_QNW:P_QNW + 1], in1=rr[:], op0=ALU.mult, op1=ALU.mult)
                bt, bb = inproj(p, 20)
                k.op("act", "activation", r=[bb], w=[sraw.b], out=sraw[:], in_=bt[:, 0:512], func=AF.Copy)
                k.op("act", "activation", r=[sraw.b], w=[sqb.b], out=sqb[:], in_=sraw[:], func=AF.Square)
                bt2, bb2 = nb()
                k.op("pe", "matmul", r=[oneb.b, sqb.b], w=[bb2], out=bt2[:, 0:512], lhsT=oneb[:], rhs=sqb[:],
                     start=True, stop=True)
                k.op("act", "activation", r=[bb2], w=[rr.b], out=rr[:], in_=bt2[:, 0:512], func=AF.Ln, bias=128.0 * EPS)
                k.op("act", "activation", r=[rr.b], w=[rr.b], out=rr[:], in_=rr[:], func=AF.Exp, scale=-0.5)
                if p > 0:
                    k.op("dve", "tensor_copy", r=[skn.b], w=[skn.b], out=skn[:, 0:128], in_=skn[:, 512:640])
                    k.op("dve", "tensor_copy", r=[svt.b], w=[svt.b], out=svt[:, 0, :], in_=svt[:, 4, :])
                k.op("dve", "scalar_tensor_tensor", r=[sraw.b, knw2.b, rr.b], w=[skn.b], out=skn[:, 128:640],
                     in0=sraw[:], scalar=knw2[:, 0:1], in1=rr[:], op0=ALU.mult, op1=ALU.mult)
                bt, bb = inproj(p, 21)
                k.op("act", "activation", r=[bb], w=[svb.b], out=svb[:], in_=bt[:, 0:512], func=AF.Copy)
                bt2, bb2 = nb()
                btb = bt2[:].bitcast(BF16)
                for j in range(4):
                    k.op("pe", "transpose", r=[svb.b, idb.b], w=[bb2], sig=(j == 3),
                         out=btb[:, j * 128:(j + 1) * 128], in_=svb[:, j * 128:(j + 1) * 128], identity=idb[:])
                k.op("act", "activation", r=[bb2], w=[svt.b], out=svt[:, 1:5, :],
                     in_=btb[:, 0:512].rearrange("p (a b) -> p a b", a=4), func=AF.Copy)
                for h in range(4):
                    sm = smix[h % 2]
                    for j in range(4):
                        W = sw[swcount[0] % 2]
                        swcount[0] += 1
                        gblk = p * 4 + j
                        cs = slice(j * 128, (j + 1) * 128)
                        has_prev = gblk > 0
                        b1, bb1 = nb()
                        k.op("pe", "matmul", r=[skn.b, sqn[h].b], w=[bb1], out=b1[:, 0:128],
                             lhsT=skn[:, 128 + j * 128:256 + j * 128], rhs=sqn[h][:, cs], start=True, stop=True)
                        k.op("dve", "tensor_tensor", r=[bb1, cst.b], w=[W["tc"].b], out=W["tc"][:], in0=b1[:, 0:128],
                             in1=cst[:, C_AL + (2 * h) * 128:C_AL + (2 * h + 1) * 128], op=ALU.add)
                        k.op("act", "activation", r=[W["tc"].b], w=[W["pc"].b], out=W["pc"][:], in_=W["tc"][:],
                             func=AF.Exp)
                        if has_prev:
                            b2, bb2 = nb()
                            k.op("pe", "matmul", r=[skn.b, sqn[h].b], w=[bb2], out=b2[:, 0:128],
                                 lhsT=skn[:, j * 128:128 + j * 128], rhs=sqn[h][:, cs], start=True, stop=True)
                            k.op("dve", "tensor_tensor", r=[bb2, cst.b], w=[W["tp"].b], out=W["tp"][:],
                                 in0=b2[:, 0:128], in1=cst[:, C_AL + (2 * h + 1) * 128:C_AL + (2 * h + 2) * 128],
                                 op=ALU.add)
                            k.op("act", "activation", r=[W["tp"].b], w=[W["pp"].b], out=W["pp"][:], in_=W["tp"][:],
                                 func=AF.Exp)
                        b3, bb3 = nb()
                        if has_prev:
                            k.op("pe", "matmul", r=[svt.b, W["pp"].b], w=[bb3], sig=False, out=b3[:, 0:128],
                                 lhsT=svt[:, j, :], rhs=W["pp"][:], start=True, stop=False)
                        k.op("pe", "matmul", r=[svt.b, W["pc"].b], w=[bb3], out=b3[:, 0:128], lhsT=svt[:, j + 1, :],
                             rhs=W["pc"][:], start=(not has_prev), stop=True)
                        b4, bb4 = nb()
                        if has_prev:
                            k.op("pe", "matmul", r=[oneb.b, W["pp"].b], w=[bb4], sig=False, out=b4[:, 0:128],
                                 lhsT=oneb[:], rhs=W["pp"][:], start=True, stop=False)
                        k.op("pe", "matmul", r=[oneb.b, W["pc"].b], w=[bb4], out=b4[:, 0:128], lhsT=oneb[:],
                             rhs=W["pc"][:], start=(not has_prev), stop=True)
                        k.op("act", "activation", r=[bb4, esink.b], w=[W["rd"].b], out=W["rd"][:], in_=b4[:, 0:128],
                             func=AF.Ln, bias=esink[:, h:h + 1])
                        k.op("act", "activation", r=[W["rd"].b], w=[W["rd"].b], out=W["rd"][:], in_=W["rd"][:],
                             func=AF.Exp, scale=-1.0)
                        k.op("dve", "tensor_tensor", r=[bb3, W["rd"].b], w=[sm.b], out=sm[:, cs], in0=b3[:, 0:128],
                             in1=W["rd"][:], op=ALU.mult)
                    k.dma("sp", mix_in_l[p].ap()[512 + h * 128:512 + (h + 1) * 128, :], sm[:],
                          r=[sm.b], w=[b_mixin_l[p]], sem_from=sm.b)
                if "cc" not in skip:
                    k.wait_all("pool", [b_mixin_l[p]])
                    ccs = k.newsem("cc%d" % p)

                    def cc(e, sems, p=p, ccs=ccs):
                        e.collective_compute("AllGather", ALU.bypass, replica_groups=[[0, 1, 2, 3], [4, 5, 6, 7]],
                                             ins=[mix_in_l[p].ap().opt()],
                                             outs=[mix_all_l[p].ap().opt()]).then_inc(sems[ccs])
                    k.raw("pool", cc)
                    b_mixall_l[p].writes = {ccs: 1}

            conv_issue(len(conv_jobs))
            b_wconv.writes = {convsem: conv_state["cnt"]}
            allb = [t.b for t in p1_tiles] + hTb
            allb += [bb for _, bb in banks] + dbg_bufs
            for e in ("pe", "act", "dve", "pool", "sp"):
                k.wait_all(e, allb)
            p1_bufs = allb

        with contextlib.ExitStack() as st:
            T = lambda n, s, d: Tl(arena, n, s, d)
            arena.off = arena_mark
            fnwbc = T("fnwbc_s", [128, D], F32)
            selt = T("selt", [128, 4], F32)
            h2T = T("h2T", [128, 32, 512], BF16)
            h2b = [Buf("h2T%d" % m) for m in range(4)]
            big = T("big", [128, NFB * 512], BF16)
            bigb = [Buf("big%d" % f) for f in range(NFB)]
            NS = 5
            ws = [T("ws%d" % i, [128, D], BF16) for i in range(NS)]
            cand = [T("cand%d" % i, [128, 2, 512], BF16) for i in range(2)]
            xsl = [T("xsl%d" % i, [128, 512], F32) for i in range(3)]
            x2f = [T("x2f%d" % i, [128, 512], F32) for i in range(3)]
            sg = [T("sg%d" % i, [128, 512], F32) for i in range(2)]
            ssq8 = [T("ssq8_%d" % m, [128, 8], F32) for m in range(4)]
            rs2 = [T("rs2_%d" % m, [128, 1], F32) for m in range(4)]
            junk = T("junk", [128, 512], BF16)
            for t_ in [fnwbc, selt, h2T, big] + ws + cand + xsl + x2f + sg + ssq8 + rs2 + [junk]:
                t_.b.reads = {}
            k.dma("sp", fnwbc[:], fnwbc_d, w=[fnwbc.b])
            k.dma("sp", selt[:], sel_d, w=[selt.b])
            wcount = [0]
            xcount = [0]

            def wslot():
                s = ws[wcount[0] % NS]
                wcount[0] += 1
                return s

            mixT = big[:, 0:32 * 512].rearrange("p (a b) -> p a b", a=32)
            mixb = bigb[0:32]
            x2b = big[:, 32 * 512:64 * 512].rearrange("p (m c) -> p m c", m=4)
            outb = {}

            for p in range(0 if "p2" in skip else NP2):
                t0 = p * 512
                for a in range(16):
                    dst = mixT[:, a * 2:(a + 1) * 2, :]
                    dbufs = mixb[a * 2:(a + 1) * 2]
                    for j in range(4):
                        c = cand[(a * 4 + j) % 2]
                        P_ = j * NP2 + p
                        k.dma("sp", c[:], mix_all_l[P_].ap().rearrange("(a q) t -> q a t", q=128)[:, a * 2:(a + 1) * 2, :],
                              r=[b_mixall_l[P_]], w=[c.b])
                        if j == 0:
                            k.op("dve", "tensor_scalar", r=[c.b, selt.b], w=dbufs, out=dst, in0=c[:],
                                 scalar1=selt[:, 0:1], scalar2=None, op0=ALU.mult)
                        else:
                            k.op("dve", "scalar_tensor_tensor", r=[c.b, selt.b] + dbufs, w=dbufs, out=dst, in0=c[:],
                                 scalar=selt[:, j:j + 1], in1=dst, op0=ALU.mult, op1=ALU.add)
                for n in range(8):
                    bk = [nb() for _ in range(4)]
                    for a in range(4):
                        s = wslot()
                        k.dma("pool", s[:], wo_b.ap()[n][:, a * 4096:(a + 1) * 4096], r=[b_wconv], w=[s.b])
                        for m in range(4):
                            for kk in range(8):
                                kc = a * 8 + kk
                                k.op("pe", "matmul", r=[s.b, mixb[kc]], w=[bk[m][1]],
                                     sig=(kk == 7 and (a == 3 or m == 3)), out=bk[m][0][:, 0:512],
                                     lhsT=mixT[:, kc, m * 128:(m + 1) * 128], rhs=s[:, kk * 512:(kk + 1) * 512],
                                     start=(a == 0 and kk == 0), stop=(a == 3 and kk == 7))
                    for m in range(4):
                        xi = xsl[xcount[0] % 3]
                        xo = x2f[xcount[0] % 3]
                        xcount[0] += 1
                        rs = slice(t0 + m * 128, t0 + (m + 1) * 128)
                        cs = slice(n * 512, (n + 1) * 512)
                        k.dma("sp", xi[:], x2tok[rs, cs], w=[xi.b])
                        k.op("dve", "tensor_tensor", r=[bk[m][1], xi.b], w=[xo.b], out=xo[:], in0=bk[m][0][:, 0:512],
                             in1=xi[:], op=ALU.add)
                        k.op("act", "activation", r=[xo.b], w=[junk.b, ssq8[m].b], out=junk[:], in_=xo[:],
                             func=AF.Square, accum_out=ssq8[m][:, n:n + 1])
                        k.op("act", "activation", r=[xo.b], w=[bigb[32 + m * 8 + n]], out=x2b[:, m, cs], in_=xo[:], func=AF.Copy)
                        ob = Buf("o_%d_%d" % (m, n))
                        outb[(m, n)] = ob
                        k.dma("sp", out_d[rs, cs], xo[:], r=[xo.b], w=[ob], sem_from=xo.b)
                for m in range(4):
                    xb_bufs = bigb[32 + m * 8:40 + m * 8]
                    k.op("dve", "tensor_reduce", r=[ssq8[m].b], w=[rs2[m].b], out=rs2[m][:], in_=ssq8[m][:],
                         axis=mybir.AxisListType.X, op=ALU.add)
                    k.op("act", "activation", r=[rs2[m].b], w=[rs2[m].b], out=rs2[m][:], in_=rs2[m][:], func=AF.Ln,
                         scale=1.0 / D, bias=EPS)
                    k.op("act", "activation", r=[rs2[m].b], w=[rs2[m].b], out=rs2[m][:], in_=rs2[m][:], func=AF.Exp,
                         scale=-0.5)
                    k.op("dve", "scalar_tensor_tensor", r=xb_bufs + [rs2[m].b, fnwbc.b], w=xb_bufs, out=x2b[:, m, :],
                         in0=x2b[:, m, :], scalar=rs2[m][:, 0:1], in1=fnwbc[:], op0=ALU.mult, op1=ALU.mult)
                    for kq in range(8):
                        bt, bb = nb()
                        btb = bt[:].bitcast(BF16)
                        for j in range(4):
                            kk = kq * 4 + j
                            k.op("pe", "transpose", r=xb_bufs + [idb.b], w=[bb], sig=(j == 3),
                                 out=btb[:, j * 128:(j + 1) * 128], in_=x2b[:, m, kk * 128:(kk + 1) * 128],
                                 identity=idb[:])
                        src = btb[:, 0:512].rearrange("p (a b) -> p a b", a=4)
                        dst = h2T[:, kq * 4:(kq + 1) * 4, m * 128:(m + 1) * 128]
                        if kq % 2 == 0:
                            k.op("act", "activation", r=[bb], w=[h2b[m]], out=dst, in_=src, func=AF.Copy)
                        else:
                            k.op("dve", "tensor_copy", r=[bb], w=[h2b[m]], out=dst, in_=src)
                for fb in range(NFB):
                    s_g = wslot()
                    k.dma("pool", s_g[:], wg_b.ap()[fb], r=[b_wconv], w=[s_g.b])
                    s_u = wslot()
                    k.dma("pool", s_u[:], wu_b.ap()[fb], r=[b_wconv], w=[s_u.b])
                    bg, bbg = nb()
                    for kk in range(32):
                        k.op("pe", "matmul", r=[s_g.b] + h2b, w=[bbg], sig=(kk == 31), out=bg[:, 0:512],
                             lhsT=s_g[:, kk * 128:(kk + 1) * 128], rhs=h2T[:, kk, :], start=(kk == 0),
                             stop=(kk == 31))
                    bu, bbu = nb()
                    for kk in range(32):
                        k.op("pe", "matmul", r=[s_u.b] + h2b, w=[bbu], sig=(kk == 31), out=bu[:, 0:512],
                             lhsT=s_u[:, kk * 128:(kk + 1) * 128], rhs=h2T[:, kk, :], start=(kk == 0),
                             stop=(kk == 31))
                    sgt = sg[fb % 2]
                    k.op("act", "activation", r=[bbg], w=[sgt.b], out=sgt[:], in_=bg[:, 0:512], func=AF.Silu)
                    k.op("dve", "tensor_tensor", r=[sgt.b, bbu], w=[bigb[fb]], out=big[:, fb * 512:(fb + 1) * 512],
                         in0=sgt[:], in1=bu[:, 0:512], op=ALU.mult)
                for q in range(4):
                    bk = [[nb() for _ in range(2)] for _ in range(4)]
                    for fg in range((NFB + 3) // 4):
                        nf = min(4, NFB - fg * 4)
                        s = wslot()
                        k.dma("pool", s[:, 0:nf * 1024], wd_b.ap()[q][:, fg * 4096:fg * 4096 + nf * 1024], r=[b_wconv], w=[s.b])
                        for f in range(nf):
                            fb = fg * 4 + f
                            for m in range(4):
                                for n2 in range(2):
                                    k.op("pe", "matmul", r=[s.b, bigb[fb]], w=[bk[m][n2][1]],
                                         sig=(fb == NFB - 1 or (f == nf - 1 and m == 3 and n2 == 1)),
                                         out=bk[m][n2][0][:, 0:512], lhsT=big[:, fb * 512 + m * 128:fb * 512 + (m + 1) * 128],
                                         rhs=s[:, f * 1024 + n2 * 512:f * 1024 + (n2 + 1) * 512],
                                         start=(fb == 0), stop=(fb == NFB - 1))
                    for m in range(4):
                        for n2 in range(2):
                            n = q * 2 + n2
                            xi = xsl[xcount[0] % 3]
                            xo = x2f[xcount[0] % 3]
                            xcount[0] += 1
                            rs = slice(t0 + m * 128, t0 + (m + 1) * 128)
                            cs = slice(n * 512, (n + 1) * 512)
                            ob = outb[(m, n)]
                            k.dma("sp", xi[:], out_d[rs, cs], r=[ob], w=[xi.b])
                            k.op("dve", "tensor_tensor", r=[bk[m][n2][1], xi.b], w=[xo.b], out=xo[:],
                                 in0=bk[m][n2][0][:, 0:512], in1=xi[:], op=ALU.add)
                            k.dma("sp", out_d[rs, cs], xo[:], r=[xo.b], w=[ob], sem_from=xo.b)
            k.wait_all("sp", list(outb.values()) + [t_.b for t_ in x2f] + dbg_bufs)
        k.emit()
    return nc


_CACHE = {}


def _consts(g):
    c = np.zeros((128, NCONST), np.float32)
    i = np.arange(128)
    S, Cc = np.meshgrid(i, i, indexing="ij")
    c[:, C_ID:C_ID + 128] = (S == Cc)
    c[:, C_ONE:C_ONE + 128] = 1.0
    c[:, C_UIN:C_UIN + 128] = (S <= Cc)
    c[:, C_LST:C_LST + 128] = (S > Cc)
    c[:, C_MST:C_MST + 128] = (Cc > S)
    bd = (S // 64 == Cc // 64)
    c[:, C_MBD:C_MBD + 128] = bd
    c[:, C_MNB:C_MNB + 128] = ~bd
    for h in range(4):
        slope = 2.0 ** (-8.0 * (4 * g + h + 1) / 16.0)
        kk, qq = S, Cc
        cur = np.where(qq >= kk, -slope * (qq - kk), -30000.0)
        prv = np.where(kk > qq, -slope * (qq + 128 - kk), -30000.0)
        c[:, C_AL + (2 * h) * 128:C_AL + (2 * h + 1) * 128] = cur
        c[:, C_AL + (2 * h + 1) * 128:C_AL + (2 * h + 2) * 128] = prv
    return c


def _prep_shared(inp):
    w_out, w_gate, w_up, w_down = inp["w_out"][0], inp["w_gate"][0], inp["w_up"][0], inp["w_down"][0]
    perm = np.zeros(4096, np.int64)
    for r in range(4):
        for loc in range(1024):
            hh, d = (loc % 512) // 128, loc % 128
            perm[r * 1024 + loc] = (0 if loc < 512 else 2048) + (4 * r + hh) * 128 + d
    wo = w_out[perm, :]
    sh = {}
    sh["wo_t"] = np.ascontiguousarray(wo.reshape(32, 128, 8, 512).transpose(2, 1, 0, 3)).reshape(8, 128, 32 * 512)
    sh["wg_t"] = np.ascontiguousarray(w_gate.reshape(32, 128, NFB, 128).transpose(2, 1, 0, 3)).reshape(NFB, 128, D)
    sh["wu_t"] = np.ascontiguousarray(w_up.reshape(32, 128, NFB, 128).transpose(2, 1, 0, 3)).reshape(NFB, 128, D)
    sh["wd_t"] = np.ascontiguousarray(w_down.reshape(NFB, 128, 4, 1024).transpose(2, 1, 0, 3)).reshape(4, 128, NFB * 1024)
    sh["anwbc"] = np.ascontiguousarray(np.broadcast_to(inp["attn_norm_w"][0][None, :], (128, D)))
    sh["fnwbc"] = np.ascontiguousarray(np.broadcast_to(inp["ffn_norm_w"][0][None, :], (128, D)))
    return sh


def _prep_group(inp, g):
    w_in = inp["w_in"][0]
    cols = []
    for t_ in range(4):
        for i in range(4):
            cols.append(t_ * 2048 + (4 * g + i) * 128)
    for h in range(4):
        cols.append(8224 + (4 * g + h) * 128)
    cols.append(10272 + g * 128)
    cols.append(10784 + g * 128)
    w1t = np.empty((NCB, 128, D), np.float32)
    for cb, c0 in enumerate(cols):
        w1t[cb] = w_in[:, c0:c0 + 128].reshape(32, 128, 128).transpose(1, 0, 2).reshape(128, D)
    abcols = [8192 + 4 * g + i for i in range(4)] + [8208 + 4 * g + i for i in range(4)]
    wab = np.ascontiguousarray(w_in[:, abcols].reshape(32, 128, 8).transpose(1, 0, 2)).reshape(128, 256)
    prm = np.zeros((128, NPRM), np.float32)
    cw = inp["conv_w"][0]
    for t_ in range(3):
        for i in range(4):
            ch = t_ * 2048 + (4 * g + i) * 128
            for j in range(4):
                prm[:, P_CW + (t_ * 4 + i) * 4 + j] = cw[j, ch:ch + 128]
    prm[:, P_ALOG:P_ALOG + 4] = inp["a_log"][0][4 * g:4 * g + 4][None, :]
    prm[:, P_DTB:P_DTB + 4] = inp["dt_bias"][0][4 * g:4 * g + 4][None, :]
    prm[:, P_DNW] = inp["dn_norm_w"][0]
    prm[:, P_QNW] = inp["q_norm_w"][0]
    prm[:, P_KNW] = inp["k_norm_w"][0]
    prm[:, P_SNK:P_SNK + 4] = inp["sinks"][0][4 * g:4 * g + 4][None, :]
    sel = np.zeros((128, 4), np.float32)
    sel[:, g] = 1.0
    return {"w1t": w1t, "wab": wab, "prm": prm, "consts": _consts(g), "sel": sel}


def make_in_maps(inp, SEQ):
    x = inp["x"]
    sh = _prep_shared(inp)
    TOK2 = SEQ // 4
    maps = []
    grp = [_prep_group(inp, g) for g in range(4)]
    for c in range(8):
        b, g = c // 4, c % 4
        m = dict(sh)
        m.update(grp[g])
        m["xb"] = np.ascontiguousarray(x[b, :SEQ])
        m["x2tok"] = np.ascontiguousarray(x[b, g * TOK2:(g + 1) * TOK2])
        maps.append(m)
    return maps


def kernel(**inputs):
    inp = {k_: np.asarray(v) for k_, v in inputs.items()}
    SEQ = inp["x"].shape[1]
    if SEQ not in _CACHE:
        _CACHE[SEQ] = build(SEQ)
    nc = _CACHE[SEQ]
    maps = make_in_maps(inp, SEQ)
    res = run_bass_kernel_spmd(nc, maps, core_ids=list(range(8)))
    TOK2 = SEQ // 4
    out = np.empty((2, SEQ, D), np.float32)
    for c in range(8):
        b, g = c // 4, c % 4
        out[b, g * TOK2:(g + 1) * TOK2] = np.asarray(res.results[c]["out"])
    return out
```

```python
import contextlib
import numpy as np
import concourse.bass as bass
import concourse.mybir as mybir
from concourse.bass_utils import run_bass_kernel_spmd

F32 = mybir.dt.float32
BF16 = mybir.dt.bfloat16
ALU = mybir.AluOpType
AF = mybir.ActivationFunctionType
ENGS = ("pe", "act", "dve", "pool", "sp")
EPS = 1e-6
D = 4096
FF = 11008
NFB = FF // 128
NCB = 22


class Buf:
    __slots__ = ("name", "writes", "reads", "sem", "semval", "excl")

    def __init__(self, name, excl=False):
        self.name = name
        self.excl = excl
        self.writes = {}
        self.reads = {}
        self.sem = None
        self.semval = 0


class K:
    RELAY = True

    def __init__(self, nc):
        self.nc = nc
        self.q = {e: [] for e in ENGS}
        self.cnt = {e: 0 for e in ENGS}
        self.waited = {e: {} for e in ENGS}
        self.semkeys = ["E_" + e for e in ENGS]
        self.relay_sem = "S_relay"
        self.semkeys.append(self.relay_sem)
        self.relay_cnt = 0
        self.tok_out = None
        self.tok_rows = 8192
        self.tok_in = None

    def newsem(self, name):
        key = "S%d_%s" % (len(self.semkeys), name)
        self.semkeys.append(key)
        return key

    def _wait(self, eng, evs):
        if eng == "pool" and K.RELAY:
            comp = {sk: v for sk, v in evs.items()
                    if sk.startswith("E_") and v > 0 and self.waited["pool"].get(sk, 0) < v}
            if comp:
                for sk, v in comp.items():
                    self.waited["pool"][sk] = v
                self._wait("sp", comp)
                self.relay_cnt += 16
                ti = self.relay_cnt // 16 - 1
                assert ti < self.tok_rows
                self.q["sp"].append(("dma", (self.tok_out[ti:ti + 1, :], self.tok_in, self.relay_sem)))
                self.q["pool"].append(("wait", (self.relay_sem, self.relay_cnt)))
            evs = {sk: v for sk, v in evs.items() if not sk.startswith("E_")}
        for sk, v in evs.items():
            if v <= 0 or (eng == "pe" and sk == "E_pe"):
                continue
            if self.waited[eng].get(sk, 0) >= v:
                continue
            self.waited[eng][sk] = v
            self.q[eng].append(("wait", (sk, v)))

    def _deps(self, eng, r, w):
        evs = {}
        for b in r:
            for sk, v in b.writes.items():
                if evs.get(sk, 0) < v:
                    evs[sk] = v
        for b in w:
            for d in (b.writes, b.reads):
                for sk, v in d.items():
                    if evs.get(sk, 0) < v:
                        evs[sk] = v
        self._wait(eng, evs)

    def op(self, eng, meth, r=(), w=(), sig=True, **kw):
        if eng != "pe":
            ex = [b for b in r if b.excl]
            if ex:
                r = [b for b in r if not b.excl]
                w = list(w) + ex
        self._deps(eng, r, w)
        sk = "E_" + eng
        if eng == "pe" and not sig:
            ev = self.cnt[eng] + 1
            inc = False
        else:
            self.cnt[eng] += 1
            ev = self.cnt[eng]
            inc = True
        self.q[eng].append(("op", (meth, kw, sk if inc else None)))
        for b in r:
            if b.reads.get(sk, 0) < ev:
                b.reads[sk] = ev
        for b in w:
            b.writes = {sk: ev}
            b.reads = {}

    def dma(self, eng, out, in_, r=(), w=(), sem_from=None):
        self._deps(eng, r, w)
        tgt = sem_from if sem_from is not None else (w[0] if len(w) else r[0])
        if tgt.sem is None:
            tgt.sem = self.newsem(tgt.name)
        tgt.semval += 16
        sk, v = tgt.sem, tgt.semval
        self.q[eng].append(("dma", (out, in_, sk)))
        for b in r:
            if b.reads.get(sk, 0) < v:
                b.reads[sk] = v
        for b in w:
            b.writes = {sk: v}
            b.reads = {}

    def wait_all(self, eng, bufs):
        evs = {}
        for b in bufs:
            for d in (b.writes, b.reads):
                for sk, v in d.items():
                    if evs.get(sk, 0) < v:
                        evs[sk] = v
        self._wait(eng, evs)

    def raw(self, eng, fn):
        self.q[eng].append(("raw", fn))

    def emit(self):
        nc = self.nc
        with contextlib.ExitStack() as st:
            sems = {}
            for sk in self.semkeys:
                sems[sk] = st.enter_context(nc.semaphore(sk))
            block = st.enter_context(nc.Block())
            q = self.q

            def run(engname, e):
                for kind, p in q[engname]:
                    if kind == "wait":
                        e.wait_ge(sems[p[0]], p[1])
                    elif kind == "op":
                        meth, kw, sk = p
                        ins = getattr(e, meth)(**kw)
                        if sk is not None:
                            ins.then_inc(sems[sk], 1)
                    elif kind == "dma":
                        out, in_, sk = p
                        e.dma_start(out=out, in_=in_).then_inc(sems[sk], 16)
                    else:
                        p(e, sems)

            @block.tensor
            def _(e):
                run("pe", e)

            @block.scalar
            def _(e):
                run("act", e)

            @block.vector
            def _(e):
                run("dve", e)

            @block.gpsimd
            def _(e):
                run("pool", e)

            @block.sync
            def _(e):
                run("sp", e)


class Arena:
    def __init__(self, nc, st, nbytes):
        self.n = nbytes // 2
        self.t = st.enter_context(nc.sbuf_tensor("arena", [128, self.n], BF16))
        self.off = 0

    def alloc(self, nbytes):
        nb_ = (nbytes + 31) // 32 * 32
        o = self.off
        self.off += nb_ // 2
        assert self.off <= self.n, "SBUF arena overflow: %d > %d" % (self.off * 2, self.n * 2)
        return o


class Tl:
    def __init__(self, arena, name, shape, dt):
        nel = 1
        for s in shape[1:]:
            nel *= s
        esz = 4 if dt == F32 else 2
        o = arena.alloc(nel * esz)
        ap = arena.t[:, o:o + nel * esz // 2]
        if dt != BF16:
            ap = ap.bitcast(dt)
        if len(shape) == 3:
            ap = ap.rearrange("p (a b) -> p a b", a=shape[1])
        self.ap = ap
        self.b = Buf(name)

    def __getitem__(self, idx):
        return self.ap[idx]


C_ID, C_ONE, C_UIN, C_LST, C_MST, C_MBD, C_MNB, C_AL = [i * 128 for i in range(8)]
NCONST = C_AL + 8 * 128
P_CW, P_ALOG, P_DTB, P_DNW, P_QNW, P_KNW, P_SNK = 0, 48, 52, 56, 57, 58, 59
NPRM = 64


def build(SEQ, dbg=None, skip=(), np1=None):
    nc = bass.Bass("TRN2", target_bir_lowering=False)
    NP1 = SEQ // 512 if np1 is None else np1
    TOK2 = SEQ // 4
    NP2 = TOK2 // 512
    dram = lambda n, s, d, kind=None: (nc.dram_tensor(n, s, d, kind=kind) if kind else nc.dram_tensor(n, s, d))
    xb = dram("xb", [SEQ, D], F32, "ExternalInput").ap()
    consts = dram("consts", [128, NCONST], F32, "ExternalInput").ap()
    prm = dram("prm", [128, NPRM], F32, "ExternalInput").ap()
    anwbc_d = dram("anwbc", [128, D], F32, "ExternalInput").ap()
    fnwbc_d = dram("fnwbc", [128, D], F32, "ExternalInput").ap()
    w1t = dram("w1t", [NCB, 128, D], F32, "ExternalInput").ap()
    wab_d = dram("wab", [128, 256], F32, "ExternalInput").ap()
    wo_t = dram("wo_t", [8, 128, 32 * 512], F32, "ExternalInput").ap()
    wg_t = dram("wg_t", [NFB, 128, D], F32, "ExternalInput").ap()
    wu_t = dram("wu_t", [NFB, 128, D], F32, "ExternalInput").ap()
    wd_t = dram("wd_t", [4, 128, NFB * 1024], F32, "ExternalInput").ap()
    out_d = dram("out", [TOK2, D], F32, "ExternalOutput").ap()
    x2tok = dram("x2tok", [TOK2, D], F32, "ExternalInput").ap()
    sel_d = dram("sel", [128, 4], F32, "ExternalInput").ap()
    NPT = SEQ // 512
    wo_b = dram("wo_b", [8, 128, 32 * 512], BF16)
    wg_b = dram("wg_b", [NFB, 128, D], BF16)
    wu_b = dram("wu_b", [NFB, 128, D], BF16)
    wd_b = dram("wd_b", [4, 128, NFB * 1024], BF16)
    mix_in_l = [dram("mix_in%d" % p_, [1024, 512], BF16) for p_ in range(NPT)]
    mix_all_l = [dram("mix_all%d" % p_, [4096, 512], BF16) for p_ in range(NPT)]
    dbg_d = {}
    if dbg:
        for name, shape, dt in dbg:
            dbg_d[name] = dram("dbg_" + name, shape, dt, "ExternalOutput").ap()

    k = K(nc)
    tok_d = dram("tok_d", [8192, 16], F32)
    k.tok_out = tok_d.ap()
    k.tok_in = consts[0:1, 0:16]
    b_mixin_l = [Buf("mixin%d" % p_) for p_ in range(NPT)]
    b_wconv = Buf("wconv")
    convsem = k.newsem("conv")
    conv_jobs = []
    for n_ in range(8):
        for a_ in range(4):
            conv_jobs.append((wo_b.ap()[n_][:, a_ * 4096:(a_ + 1) * 4096], wo_t[n_][:, a_ * 4096:(a_ + 1) * 4096]))
    for fb_ in range(NFB):
        conv_jobs.append((wg_b.ap()[fb_], wg_t[fb_]))
        conv_jobs.append((wu_b.ap()[fb_], wu_t[fb_]))
    for q_ in range(4):
        for fg_ in range((NFB + 3) // 4):
            nf_ = min(4, NFB - fg_ * 4)
            conv_jobs.append((wd_b.ap()[q_][:, fg_ * 4096:fg_ * 4096 + nf_ * 1024],
                              wd_t[q_][:, fg_ * 4096:fg_ * 4096 + nf_ * 1024]))
    conv_state = {"i": 0, "cnt": 0}

    def conv_issue(n):
        while n > 0 and conv_state["i"] < len(conv_jobs):
            o_, i_ = conv_jobs[conv_state["i"]]
            k.q["pool"].append(("dma", (o_, i_, convsem)))
            conv_state["i"] += 1
            conv_state["cnt"] += 16
            n -= 1
    b_mixall_l = [Buf("mixall%d" % p_) for p_ in range(NPT)]
    b_out = Buf("outd")

    with contextlib.ExitStack() as st0:
        banks = []
        for i in range(8):
            t = st0.enter_context(nc.psum_tensor("bank%d" % i, [128, 512], F32))
            banks.append((t, Buf("bank%d" % i, excl=True)))
        rot = [0]

        def nb():
            i = rot[0] % 8
            rot[0] += 1
            return banks[i]

        arena = Arena(nc, st0, 206 * 1024)
        cst = Tl(arena, "cst", [128, NCONST], F32)
        prt = Tl(arena, "prt", [128, NPRM], F32)
        idb = Tl(arena, "idb", [128, 128], BF16)
        oneb = Tl(arena, "oneb", [128, 128], BF16)
        arena_mark = arena.off
        k.dma("sp", cst[:], consts, w=[cst.b])
        k.dma("sp", prt[:], prm, w=[prt.b])
        k.op("dve", "tensor_copy", r=[cst.b], w=[idb.b], out=idb[:], in_=cst[:, C_ID:C_ID + 128])
        k.op("dve", "tensor_copy", r=[cst.b], w=[oneb.b], out=oneb[:], in_=cst[:, C_ONE:C_ONE + 128])
        ident = cst[:, C_ID:C_ID + 128]
        ones_f = cst[:, C_ONE:C_ONE + 128]

        def dbg_store(name, ap_dram_idx, src_ap, srcbuf):
            if name in dbg_d:
                k.dma("sp", dbg_d[name][ap_dram_idx], src_ap, r=[srcbuf])
                dbg_bufs.append(srcbuf)
        dbg_bufs = []

        with contextlib.ExitStack() as st:
            p1_tiles = []

            def T(n, s, d):
                t = Tl(arena, n, s, d)
                p1_tiles.append(t)
                return t
            xin = [T("xin%d" % i, [128, D], F32) for i in range(2)]
            xs = T("xs", [128, D], BF16)
            anwbc = T("anwbc_s", [128, D], F32)
            hT = T("hT", [128, 32, 512], BF16)
            hTb = [Buf("hT%d" % m) for m in range(4)]
            NW = 3
            wsl = [T("wsl%d" % i, [128, D], BF16) for i in range(NW)]
            wabf = T("wabf", [128, 256], F32)
            wabb = T("wabb", [128, 256], BF16)
            ssq = T("ssq", [128, 1], F32)
            rstd = T("rstd", [128, 1], F32)
            nA = T("nA", [128, 4], F32)
            dnw2 = T("dnw2", [128, 1], F32)
            knw2 = T("knw2", [128, 1], F32)
            esink = T("esink", [128, 4], F32)
            k.dma("sp", anwbc[:], anwbc_d, w=[anwbc.b])
            k.dma("sp", wabf[:], wab_d, w=[wabf.b])
            k.op("dve", "tensor_copy", r=[wabf.b], w=[wabb.b], out=wabb[:], in_=wabf[:])
            k.op("act", "activation", r=[prt.b], w=[nA.b], out=nA[:], in_=prt[:, P_ALOG:P_ALOG + 4], func=AF.Exp)
            k.op("dve", "tensor_scalar", r=[nA.b], w=[nA.b], out=nA[:], in0=nA[:], scalar1=-1.0, scalar2=None,
                 op0=ALU.mult)
            k.op("act", "activation", r=[prt.b], w=[esink.b], out=esink[:], in_=prt[:, P_SNK:P_SNK + 4], func=AF.Exp)
            k.op("dve", "tensor_scalar", r=[prt.b], w=[dnw2.b], out=dnw2[:], in0=prt[:, P_DNW:P_DNW + 1],
                 scalar1=float(np.sqrt(128.0)), scalar2=None, op0=ALU.mult)
            k.op("dve", "tensor_scalar", r=[prt.b], w=[knw2.b], out=knw2[:], in0=prt[:, P_KNW:P_KNW + 1],
                 scalar1=float(np.sqrt(128.0)), scalar2=None, op0=ALU.mult)

            cb_order = []
            cb_order += [16, 17, 18, 19, 20, 21]
            for i in (0, 1, 2, 3):
                cb_order += [i, 4 + i, 8 + i, 12 + i]
            wseq = [(p, cb) for p in range(NP1) for cb in cb_order]
            wpos = {pc: n for n, pc in enumerate(wseq)}
            wstate = {"issued": 0}
            conv_per = -(-len(conv_jobs) // max(1, len(wseq)))

            def wprefetch(upto):
                while wstate["issued"] <= min(upto, len(wseq) - 1):
                    i = wstate["issued"]
                    s = wsl[i % NW]
                    k.dma("pool", s[:], w1t[wseq[i][1]], w=[s.b])
                    wstate["issued"] += 1
                    conv_issue(conv_per)

            def inproj(p, cb):
                i = wpos[(p, cb)]
                wprefetch(i + NW - 1)
                s = wsl[i % NW]
                bt, bb = nb()
                for kk in range(32):
                    k.op("pe", "matmul", r=[s.b] + hTb, w=[bb], sig=(kk == 31), out=bt[:, 0:512],
                         lhsT=s[:, kk * 128:(kk + 1) * 128], rhs=hT[:, kk, :], start=(kk == 0), stop=(kk == 31))
                return bt, bb

            raw = T("raw", [128, 515], F32)
            halo = [[T("halo%d_%d" % (t_, i), [128, 3], F32) for i in range(4)] for t_ in range(3)]
            for t_ in range(3):
                for i in range(4):
                    k.op("dve", "memset", w=[halo[t_][i].b], ap=halo[t_][i][:], constant=0.0)
            acc = T("acc", [128, 512], F32)
            th = T("th", [128, 512], F32)
            sqb = T("sqb", [128, 512], BF16)
            rr = T("rr", [128, 512], F32)
            qn = [T("qn%d" % i, [128, 512], F32) for i in range(2)]
            qnb = [T("qnb%d" % i, [128, 512], BF16) for i in range(2)]
            knb = [T("knb%d" % i, [128, 512], BF16) for i in range(2)]
            v2b = T("v2b", [128, 512], BF16)
            vtok = [T("vtok%d" % i, [128, 512], BF16) for i in range(2)]
            z2 = [T("z2_%d" % i, [128, 512], F32) for i in range(2)]
            oT = [T("oT%d" % i, [128, 512], F32) for i in range(2)]
            mixt = [T("mixt%d" % i, [128, 512], BF16) for i in range(2)]
            Sf = [T("Sf%d" % i, [128, 128], F32) for i in range(4)]
            Sb = [T("Sb%d" % i, [128, 128], BF16) for i in range(4)]
            for i in range(4):
                k.op("dve", "memset", w=[Sf[i].b], ap=Sf[i][:], constant=0.0)
                k.op("dve", "memset", w=[Sb[i].b], ap=Sb[i][:], constant=0.0)
            gnames = ["tA", "eA", "sp", "g", "tb", "beta", "nbeta", "gcc", "egc", "tail"]
            gt = [{n: T("g_%s%d" % (n, j), [128, 4], F32) for n in gnames} for j in range(4)]
            f32names = ["Gh", "Dt", "E", "egb", "EMi", "EMs", "Nf", "A0", "A0T", "Nof", "NofT", "P0", "P1",
                        "A1", "A1T", "XT", "Y"]
            bfnames = ["PT", "T2T", "qd", "ktl", "r2n", "vnw"]
            hb = [dict([(n, T("hb_%s%d" % (n, par), [128, 128], F32)) for n in f32names] +
                       [(n, T("hb_%s%d" % (n, par), [128, 128], BF16)) for n in bfnames]) for par in range(2)]
            sraw = T("sraw", [128, 512], F32)
            sqn = [T("sqn%d" % h, [128, 512], BF16) for h in range(4)]
            skn = T("skn", [128, 640], BF16)
            svb = T("svb", [128, 512], BF16)
            svt = T("svt", [128, 5, 128], BF16)
            smix = [T("smix%d" % i, [128, 512], BF16) for i in range(2)]
            sw = [dict(tc=T("sw_tc%d" % par, [128, 128], F32), tp=T("sw_tp%d" % par, [128, 128], F32),
                       pc=T("sw_pc%d" % par, [128, 128], BF16), pp=T("sw_pp%d" % par, [128, 128], BF16),
                       rd=T("sw_rd%d" % par, [128, 128], F32)) for par in range(2)]

            cwcol = lambda t_, i, j: prt[:, P_CW + (t_ * 4 + i) * 4 + j:P_CW + (t_ * 4 + i) * 4 + j + 1]
            hbcount = [0]
            swcount = [0]

            for p in range(NP1):
                for m in range(4):
                    xi = xin[(4 * p + m) % 2]
                    r0 = p * 512 + m * 128
                    k.dma("sp", xi[:], xb[r0:r0 + 128, :], w=[xi.b])
                    k.op("act", "activation", r=[xi.b], w=[xs.b, ssq.b], out=xs[:], in_=xi[:], func=AF.Square,
                         accum_out=ssq[:, 0:1])
                    k.op("act", "activation", r=[ssq.b], w=[rstd.b], out=rstd[:], in_=ssq[:], func=AF.Ln,
                         scale=1.0 / D, bias=EPS)
                    k.op("act", "activation", r=[rstd.b], w=[rstd.b], out=rstd[:], in_=rstd[:], func=AF.Exp,
                         scale=-0.5)
                    k.op("dve", "scalar_tensor_tensor", r=[xi.b, rstd.b, anwbc.b], w=[xs.b], out=xs[:], in0=xi[:],
                         scalar=rstd[:, 0:1], in1=anwbc[:], op0=ALU.mult, op1=ALU.mult)
                    for kq in range(8):
                        bt, bb = nb()
                        btb = bt[:].bitcast(BF16)
                        for j in range(4):
                            kk = kq * 4 + j
                            k.op("pe", "transpose", r=[xs.b, idb.b], w=[bb], sig=(j == 3),
                                 out=btb[:, j * 128:(j + 1) * 128], in_=xs[:, kk * 128:(kk + 1) * 128],
                                 identity=idb[:])
                        src = btb[:, 0:512].rearrange("p (a b) -> p a b", a=4)
                        dst = hT[:, kq * 4:(kq + 1) * 4, m * 128:(m + 1) * 128]
                        if kq % 2 == 0:
                            k.op("act", "activation", r=[bb], w=[hTb[m]], out=dst, in_=src, func=AF.Copy)
                        else:
                            k.op("dve", "tensor_copy", r=[bb], w=[hTb[m]], out=dst, in_=src)
                if p == 0 and "hT" in dbg_d:
                    k.dma("sp", dbg_d["hT"], hT[:], r=hTb)
                    dbg_bufs.extend(hTb)

                for j in range(4):
                    G = gt[j]
                    bt, bb = nb()
                    for kk in range(32):
                        k.op("pe", "matmul", r=[wabb.b] + hTb, w=[bb], sig=(kk == 31), out=bt[:, 0:8],
                             lhsT=hT[:, kk, j * 128:(j + 1) * 128], rhs=wabb[:, kk * 8:(kk + 1) * 8],
                             start=(kk == 0), stop=(kk == 31))
                    k.op("dve", "tensor_tensor", r=[bb, prt.b], w=[G["tA"].b], out=G["tA"][:], in0=bt[:, 0:4],
                         in1=prt[:, P_DTB:P_DTB + 4], op=ALU.add)
                    k.op("act", "activation", r=[bb], w=[G["tb"].b], out=G["tb"][:], in_=bt[:, 4:8], func=AF.Exp,
                         scale=-1.0)
                    k.op("act", "activation", r=[G["tb"].b], w=[G["tb"].b], out=G["tb"][:], in_=G["tb"][:],
                         func=AF.Ln, bias=1.0)
                    k.op("act", "activation", r=[G["tb"].b], w=[G["beta"].b], out=G["beta"][:], in_=G["tb"][:],
                         func=AF.Exp, scale=-1.0)
                    k.op("act", "activation", r=[G["tA"].b], w=[G["eA"].b], out=G["eA"][:], in_=G["tA"][:],
                         func=AF.Exp)
                    k.op("act", "activation", r=[G["eA"].b], w=[G["sp"].b], out=G["sp"][:], in_=G["eA"][:],
                         func=AF.Ln, bias=1.0)
                    k.op("dve", "tensor_tensor", r=[G["sp"].b, nA.b], w=[G["g"].b], out=G["g"][:], in0=G["sp"][:],
                         in1=nA[:], op=ALU.mult)
                    k.op("dve", "tensor_scalar", r=[G["beta"].b], w=[G["nbeta"].b], out=G["nbeta"][:],
                         in0=G["beta"][:], scalar1=-1.0, scalar2=None, op0=ALU.mult)
                    bt2, bb2 = nb()
                    k.op("pe", "matmul", r=[cst.b, G["g"].b], w=[bb2], out=bt2[:, 0:4],
                         lhsT=cst[:, C_UIN:C_UIN + 128], rhs=G["g"][:], start=True, stop=True)
                    k.op("act", "activation", r=[bb2], w=[G["gcc"].b], out=G["gcc"][:], in_=bt2[:, 0:4],
                         func=AF.Copy)
                    k.op("act", "activation", r=[bb2], w=[G["egc"].b], out=G["egc"][:], in_=bt2[:, 0:4],
                         func=AF.Exp)
                    bt3, bb3 = nb()
                    k.op("pe", "matmul", r=[cst.b, G["g"].b], w=[bb3], out=bt3[:, 0:4],
                         lhsT=cst[:, C_LST:C_LST + 128], rhs=G["g"][:], start=True, stop=True)
                    k.op("act", "activation", r=[bb3], w=[G["tail"].b], out=G["tail"][:], in_=bt3[:, 0:4],
                         func=AF.Exp)

                def head_pre(i):
                    par = i % 2
                    for t_ in range(3):
                        bt, bb = inproj(p, t_ * 4 + i)
                        k.op("dve", "tensor_copy", r=[halo[t_][i].b], w=[raw.b], out=raw[:, 0:3],
                             in_=halo[t_][i][:])
                        k.op("act", "activation", r=[bb], w=[raw.b], out=raw[:, 3:515], in_=bt[:, 0:512],
                             func=AF.Copy)
                        k.op("dve", "tensor_scalar", r=[raw.b, prt.b], w=[acc.b], out=acc[:], in0=raw[:, 0:512],
                             scalar1=cwcol(t_, i, 0), scalar2=None, op0=ALU.mult)
                        for j in range(1, 4):
                            k.op("dve", "scalar_tensor_tensor", r=[raw.b, prt.b, acc.b], w=[acc.b], out=acc[:],
                                 in0=raw[:, j:j + 512], scalar=cwcol(t_, i, j), in1=acc[:], op0=ALU.mult,
                                 op1=ALU.add)
                        k.op("dve", "tensor_copy", r=[raw.b], w=[halo[t_][i].b], out=halo[t_][i][:],
                             in_=raw[:, 512:515])
                        k.op("act", "activation", r=[acc.b], w=[th.b], out=th[:], in_=acc[:], func=AF.Exp,
                             scale=-1.0)
                        k.op("act", "activation", r=[th.b], w=[th.b], out=th[:], in_=th[:], func=AF.Ln, bias=1.0)
                        k.op("act", "activation", r=[th.b], w=[th.b], out=th[:], in_=th[:], func=AF.Exp, scale=-1.0)
                        k.op("dve", "tensor_tensor", r=[th.b, acc.b], w=[th.b], out=th[:], in0=th[:], in1=acc[:],
                             op=ALU.mult)
                        if t_ < 2:
                            k.op("act", "activation", r=[th.b], w=[sqb.b], out=sqb[:], in_=th[:], func=AF.Square)
                            bt2, bb2 = nb()
                            k.op("pe", "matmul", r=[oneb.b, sqb.b], w=[bb2], out=bt2[:, 0:512], lhsT=oneb[:],
                                 rhs=sqb[:], start=True, stop=True)
                            k.op("act", "activation", r=[bb2], w=[rr.b], out=rr[:], in_=bt2[:, 0:512], func=AF.Ln,
                                 bias=EPS)
                            k.op("act", "activation", r=[rr.b], w=[rr.b], out=rr[:], in_=rr[:], func=AF.Exp,
                                 scale=-0.5)
                            if t_ == 0:
                                k.op("dve", "scalar_tensor_tensor", r=[th.b, rr.b], w=[qn[par].b], out=qn[par][:],
                                     in0=th[:], scalar=float(128.0 ** -0.5), in1=rr[:], op0=ALU.mult,
                                     op1=ALU.mult)
                                k.op("act", "activation", r=[qn[par].b], w=[qnb[par].b], out=qnb[par][:],
                                     in_=qn[par][:], func=AF.Copy)
                            else:
                                k.op("dve", "tensor_tensor", r=[th.b, rr.b], w=[knb[par].b], out=knb[par][:],
                                     in0=th[:], in1=rr[:], op=ALU.mult)
                        else:
                            k.op("act", "activation", r=[th.b], w=[v2b.b], out=v2b[:], in_=th[:], func=AF.Copy)
                            bt2, bb2 = nb()
                            btb = bt2[:].bitcast(BF16)
                            for j in range(4):
                                k.op("pe", "transpose", r=[v2b.b, idb.b], w=[bb2], sig=(j == 3),
                                     out=btb[:, j * 128:(j + 1) * 128], in_=v2b[:, j * 128:(j + 1) * 128],
                                     identity=idb[:])
                            k.op("act", "activation", r=[bb2], w=[vtok[par].b], out=vtok[par][:],
                                 in_=btb[:, 0:512], func=AF.Copy)
                    bt, bb = inproj(p, 12 + i)
                    k.op("act", "activation", r=[bb], w=[th.b], out=th[:], in_=bt[:, 0:512], func=AF.Exp, scale=-1.0)
                    k.op("act", "activation", r=[th.b], w=[th.b], out=th[:], in_=th[:], func=AF.Ln, bias=1.0)
                    k.op("act", "activation", r=[th.b], w=[th.b], out=th[:], in_=th[:], func=AF.Exp, scale=-1.0)
                    k.op("dve", "tensor_tensor", r=[th.b, bb], w=[z2[par].b], out=z2[par][:], in0=th[:],
                         in1=bt[:, 0:512], op=ALU.mult)


                def gdn_block(i, j):
                    par = i % 2
                    H = hb[i % 2]
                    if True:
                        G = gt[j]
                        cs = slice(j * 128, (j + 1) * 128)
                        gcol = G["g"][:, i:i + 1]
                        k.op("dve", "tensor_scalar", r=[cst.b, G["g"].b], w=[H["Gh"].b], out=H["Gh"][:], in0=ones_f,
                             scalar1=gcol, scalar2=None, op0=ALU.mult)
                        btg, bbg = nb()
                        yield
                        k.op("pe", "matmul", r=[H["Gh"].b, cst.b], w=[bbg], out=btg[:, 0:128], lhsT=H["Gh"][:],
                             rhs=cst[:, C_UIN:C_UIN + 128], start=True, stop=True)
                        k.op("dve", "tensor_scalar", r=[bbg, G["gcc"].b], w=[H["Dt"].b], out=H["Dt"][:],
                             in0=btg[:, 0:128], scalar1=G["gcc"][:, i:i + 1], scalar2=0.0, op0=ALU.subtract,
                             op1=ALU.min)
                        k.op("act", "activation", r=[bbg], w=[H["egb"].b], out=H["egb"][:], in_=btg[:, 0:128],
                             func=AF.Exp)
                        k.op("act", "activation", r=[H["Dt"].b], w=[H["E"].b], out=H["E"][:], in_=H["Dt"][:],
                             func=AF.Exp)
                        k.op("dve", "tensor_tensor", r=[H["E"].b, cst.b], w=[H["EMi"].b], out=H["EMi"][:],
                             in0=H["E"][:], in1=cst[:, C_UIN:C_UIN + 128], op=ALU.mult)
                        k.op("dve", "tensor_tensor", r=[H["E"].b, cst.b], w=[H["EMs"].b], out=H["EMs"][:],
                             in0=H["E"][:], in1=cst[:, C_MST:C_MST + 128], op=ALU.mult)
                        k.op("dve", "tensor_tensor", r=[qn[par].b, H["egb"].b], w=[H["qd"].b], out=H["qd"][:],
                             in0=qn[par][:, cs], in1=H["egb"][:], op=ALU.mult)
                        btk, bbk = nb()
                        yield
                        k.op("pe", "matmul", r=[knb[par].b], w=[bbk], out=btk[:, 0:128], lhsT=knb[par][:, cs],
                             rhs=knb[par][:, cs], start=True, stop=True)
                        k.op("dve", "scalar_tensor_tensor", r=[bbk, G["beta"].b, H["EMs"].b], w=[H["Nf"].b],
                             out=H["Nf"][:], in0=btk[:, 0:128], scalar=G["beta"][:, i:i + 1], in1=H["EMs"][:],
                             op0=ALU.mult, op1=ALU.mult)
                        btq, bbq = nb()
                        yield
                        k.op("pe", "matmul", r=[knb[par].b, qnb[par].b], w=[bbq], out=btq[:, 0:128],
                             lhsT=knb[par][:, cs], rhs=qnb[par][:, cs], start=True, stop=True)
                        k.op("dve", "tensor_tensor", r=[bbq, H["EMi"].b], w=[H["PT"].b], out=H["PT"][:],
                             in0=btq[:, 0:128], in1=H["EMi"][:], op=ALU.mult)
                        btt, bbt = nb()
                        bttb = btt[:].bitcast(BF16)
                        yield
                        k.op("pe", "transpose", r=[knb[par].b, idb.b], w=[bbt], out=bttb[:, 0:128],
                             in_=knb[par][:, cs], identity=idb[:])
                        k.op("act", "activation", r=[bbt, G["tail"].b], w=[H["ktl"].b], out=H["ktl"][:],
                             in_=bttb[:, 0:128], func=AF.Copy, scale=G["tail"][:, i:i + 1])
                        k.op("dve", "tensor_tensor", r=[H["Nf"].b, cst.b], w=[H["A0"].b], out=H["A0"][:],
                             in0=H["Nf"][:], in1=cst[:, C_MBD:C_MBD + 128], op=ALU.mult)
                        k.op("dve", "tensor_tensor", r=[H["Nf"].b, cst.b], w=[H["Nof"].b], out=H["Nof"][:],
                             in0=H["Nf"][:], in1=cst[:, C_MNB:C_MNB + 128], op=ALU.mult)
                        btn, bbn = nb()
                        yield
                        k.op("pe", "transpose", r=[H["Nf"].b, cst.b], w=[bbn], out=btn[:, 0:128], in_=H["Nf"][:],
                             identity=ident)
                        k.op("dve", "tensor_tensor", r=[bbn, cst.b], w=[H["A0T"].b], out=H["A0T"][:],
                             in0=btn[:, 0:128], in1=cst[:, C_MBD:C_MBD + 128], op=ALU.mult)
                        k.op("dve", "tensor_tensor", r=[bbn, cst.b], w=[H["NofT"].b], out=H["NofT"][:],
                             in0=btn[:, 0:128], in1=cst[:, C_MNB:C_MNB + 128], op=ALU.mult)
                        k.op("dve", "tensor_tensor", r=[cst.b, H["A0"].b], w=[H["P0"].b], out=H["P0"][:], in0=ident,
                             in1=H["A0"][:], op=ALU.subtract)
                        A, AT, Pc = H["A0"], H["A0T"], H["P0"]
                        An, ATn, Pn = H["A1"], H["A1T"], H["P1"]
                        for lev in range(5):
                            last = (lev == 4)
                            b1, bb1 = nb()
                            yield
                            k.op("pe", "matmul", r=[A.b, AT.b], w=[bb1], out=b1[:, 0:128], lhsT=A[:], rhs=AT[:],
                                 start=True, stop=True)
                            if not last:
                                b2, bb2 = nb()
                                k.op("pe", "matmul", r=[A.b, AT.b], w=[bb2], out=b2[:, 0:128], lhsT=AT[:], rhs=A[:],
                                     start=True, stop=True)
                            k.op("act", "activation", r=[bb1], w=[ATn.b], out=ATn[:], in_=b1[:, 0:128], func=AF.Copy)
                            if not last:
                                k.op("act", "activation", r=[bb2], w=[An.b], out=An[:], in_=b2[:, 0:128],
                                     func=AF.Copy)
                            b3, bb3 = nb()
                            yield
                            k.op("pe", "matmul", r=[ATn.b, Pc.b], w=[bb3], out=b3[:, 0:128], lhsT=ATn[:], rhs=Pc[:],
                                 start=True, stop=True)
                            k.op("dve", "tensor_tensor", r=[bb3, Pc.b], w=[Pn.b], out=Pn[:], in0=b3[:, 0:128],
                                 in1=Pc[:], op=ALU.add)
                            A, An = An, A
                            AT, ATn = ATn, AT
                            Pc, Pn = Pn, Pc
                        X = Pc
                        b1, bb1 = nb()
                        yield
                        k.op("pe", "transpose", r=[X.b, cst.b], w=[bb1], out=b1[:, 0:128], in_=X[:], identity=ident)
                        k.op("act", "activation", r=[bb1], w=[H["XT"].b], out=H["XT"][:], in_=b1[:, 0:128],
                             func=AF.Copy)
                        b2, bb2 = nb()
                        yield
                        k.op("pe", "matmul", r=[H["NofT"].b, X.b], w=[bb2], out=b2[:, 0:128], lhsT=H["NofT"][:],
                             rhs=X[:], start=True, stop=True)
                        k.op("act", "activation", r=[bb2], w=[H["Y"].b], out=H["Y"][:], in_=b2[:, 0:128],
                             func=AF.Copy)
                        b3, bb3 = nb()
                        yield
                        k.op("pe", "matmul", r=[H["XT"].b, H["Y"].b], w=[bb3], out=b3[:, 0:128], lhsT=H["XT"][:],
                             rhs=H["Y"][:], start=True, stop=True)
                        k.op("dve", "tensor_tensor", r=[X.b, bb3], w=[H["T2T"].b], out=H["T2T"][:], in0=X[:],
                             in1=b3[:, 0:128], op=ALU.subtract)

                        b4, bb4 = nb()
                        yield
                        k.op("pe", "matmul", r=[knb[par].b, Sb[i].b], w=[bb4], out=b4[:, 0:128], lhsT=knb[par][:, cs],
                             rhs=Sb[i][:], start=True, stop=True)
                        k.op("dve", "scalar_tensor_tensor", r=[bb4, G["egc"].b, vtok[par].b], w=[H["r2n"].b],
                             out=H["r2n"][:], in0=b4[:, 0:128], scalar=G["egc"][:, i:i + 1], in1=vtok[par][:, cs],
                             op0=ALU.mult, op1=ALU.subtract)
                        b5, bb5 = nb()
                        yield
                        k.op("pe", "matmul", r=[H["T2T"].b, H["r2n"].b], w=[bb5], out=b5[:, 0:128], lhsT=H["T2T"][:],
                             rhs=H["r2n"][:], start=True, stop=True)
                        k.op("act", "activation", r=[bb5, G["nbeta"].b], w=[H["vnw"].b], out=H["vnw"][:],
                             in_=b5[:, 0:128], func=AF.Copy, scale=G["nbeta"][:, i:i + 1])
                        b6, bb6 = nb()
                        yield
                        k.op("pe", "matmul", r=[Sb[i].b, H["qd"].b], w=[bb6], sig=False, out=b6[:, 0:128],
                             lhsT=Sb[i][:], rhs=H["qd"][:], start=True, stop=False)
                        k.op("pe", "matmul", r=[H["vnw"].b, H["PT"].b], w=[bb6], out=b6[:, 0:128], lhsT=H["vnw"][:],
                             rhs=H["PT"][:], start=False, stop=True)
                        k.op("act", "activation", r=[bb6], w=[oT[par].b], out=oT[par][:, cs], in_=b6[:, 0:128],
                             func=AF.Copy)
                        b7, bb7 = nb()
                        yield
                        k.op("pe", "matmul", r=[H["ktl"].b, H["vnw"].b], w=[bb7], out=b7[:, 0:128], lhsT=H["ktl"][:],
                             rhs=H["vnw"][:], start=True, stop=True)
                        k.op("dve", "scalar_tensor_tensor", r=[Sf[i].b, H["egb"].b, bb7], w=[Sf[i].b], out=Sf[i][:],
                             in0=Sf[i][:], scalar=H["egb"][:, 127:128], in1=b7[:, 0:128], op0=ALU.mult, op1=ALU.add)
                        k.op("act", "activation", r=[Sf[i].b], w=[Sb[i].b], out=Sb[i][:], in_=Sf[i][:], func=AF.Copy)


                def head_post(i):
                    par = i % 2
                    k.op("act", "activation", r=[oT[par].b], w=[sqb.b], out=sqb[:], in_=oT[par][:], func=AF.Square)
                    bt2, bb2 = nb()
                    k.op("pe", "matmul", r=[oneb.b, sqb.b], w=[bb2], out=bt2[:, 0:512], lhsT=oneb[:], rhs=sqb[:],
                         start=True, stop=True)
                    k.op("act", "activation", r=[bb2], w=[rr.b], out=rr[:], in_=bt2[:, 0:512], func=AF.Ln, bias=128.0 * EPS)
                    k.op("act", "activation", r=[rr.b], w=[rr.b], out=rr[:], in_=rr[:], func=AF.Exp, scale=-0.5)
                    k.op("dve", "tensor_tensor", r=[oT[par].b, rr.b], w=[rr.b], out=rr[:], in0=oT[par][:], in1=rr[:],
                         op=ALU.mult)
                    k.op("dve", "scalar_tensor_tensor", r=[rr.b, dnw2.b, z2[par].b], w=[mixt[par].b],
                         out=mixt[par][:], in0=rr[:], scalar=dnw2[:, 0:1], in1=z2[par][:], op0=ALU.mult,
                         op1=ALU.mult)
                    k.dma("sp", mix_in_l[p].ap()[i * 128:(i + 1) * 128, :], mixt[par][:],
                          r=[mixt[par].b], w=[b_mixin_l[p]], sem_from=mixt[par].b)


                def swa_pre():
                    for h in range(4):
                        bt, bb = inproj(p, 16 + h)
                        k.op("act", "activation", r=[bb], w=[sraw.b], out=sraw[:], in_=bt[:, 0:512], func=AF.Copy)
                        k.op("act", "activation", r=[sraw.b], w=[sqb.b], out=sqb[:], in_=sraw[:], func=AF.Square)
                        bt2, bb2 = nb()
                        k.op("pe", "matmul", r=[oneb.b, sqb.b], w=[bb2], out=bt2[:, 0:512], lhsT=oneb[:], rhs=sqb[:],
                             start=True, stop=True)
                        k.op("act", "activation", r=[bb2], w=[rr.b], out=rr[:], in_=bt2[:, 0:512], func=AF.Ln, bias=128.0 * EPS)
                        k.op("act", "activation", r=[rr.b], w=[rr.b], out=rr[:], in_=rr[:], func=AF.Exp, scale=-0.5)
                        k.op("dve", "scalar_tensor_tensor", r=[sraw.b, prt.b, rr.b], w=[sqn[h].b], out=sqn[h][:],
                             in0=sraw[:], scalar=prt[:, P_QNW:P_QNW + 1], in1=rr[:], op0=ALU.mult, op1=ALU.mult)
                    bt, bb = inproj(p, 20)
                    k.op("act", "activation", r=[bb], w=[sraw.b], out=sraw[:], in_=bt[:, 0:512], func=AF.Copy)
                    k.op("act", "activation", r=[sraw.b], w=[sqb.b], out=sqb[:], in_=sraw[:], func=AF.Square)
                    bt2, bb2 = nb()
                    k.op("pe", "matmul", r=[oneb.b, sqb.b], w=[bb2], out=bt2[:, 0:512], lhsT=oneb[:], rhs=sqb[:],
                         start=True, stop=True)
                    k.op("act", "activation", r=[bb2], w=[rr.b], out=rr[:], in_=bt2[:, 0:512], func=AF.Ln, bias=128.0 * EPS)
                    k.op("act", "activation", r=[rr.b], w=[rr.b], out=rr[:], in_=rr[:], func=AF.Exp, scale=-0.5)
                    if p > 0:
                        k.op("dve", "tensor_copy", r=[skn.b], w=[skn.b], out=skn[:, 0:128], in_=skn[:, 512:640])
                        k.op("dve", "tensor_copy", r=[svt.b], w=[svt.b], out=svt[:, 0, :], in_=svt[:, 4, :])
                    k.op("dve", "scalar_tensor_tensor", r=[sraw.b, knw2.b, rr.b], w=[skn.b], out=skn[:, 128:640],
                         in0=sraw[:], scalar=knw2[:, 0:1], in1=rr[:], op0=ALU.mult, op1=ALU.mult)
                    bt, bb = inproj(p, 21)
                    k.op("act", "activation", r=[bb], w=[svb.b], out=svb[:], in_=bt[:, 0:512], func=AF.Copy)
                    bt2, bb2 = nb()
                    btb = bt2[:].bitcast(BF16)
                    for j in range(4):
                        k.op("pe", "transpose", r=[svb.b, idb.b], w=[bb2], sig=(j == 3),
                             out=btb[:, j * 128:(j + 1) * 128], in_=svb[:, j * 128:(j + 1) * 128], identity=idb[:])
                    k.op("act", "activation", r=[bb2], w=[svt.b], out=svt[:, 1:5, :],
                         in_=btb[:, 0:512].rearrange("p (a b) -> p a b", a=4), func=AF.Copy)

                def swa_blocks():
                    for h in range(4):
                        sm = smix[h % 2]
                        for j in range(4):
                            W = sw[swcount[0] % 2]
                            swcount[0] += 1
                            gblk = p * 4 + j
                            cs = slice(j * 128, (j + 1) * 128)
                            has_prev = gblk > 0
                            b1, bb1 = nb()
                            yield
                            k.op("pe", "matmul", r=[skn.b, sqn[h].b], w=[bb1], out=b1[:, 0:128],
                                 lhsT=skn[:, 128 + j * 128:256 + j * 128], rhs=sqn[h][:, cs], start=True, stop=True)
                            k.op("dve", "tensor_tensor", r=[bb1, cst.b], w=[W["tc"].b], out=W["tc"][:], in0=b1[:, 0:128],
                                 in1=cst[:, C_AL + (2 * h) * 128:C_AL + (2 * h + 1) * 128], op=ALU.add)
                            k.op("act", "activation", r=[W["tc"].b], w=[W["pc"].b], out=W["pc"][:], in_=W["tc"][:],
                                 func=AF.Exp)
                            if has_prev:
                                b2, bb2 = nb()
                                yield
                                k.op("pe", "matmul", r=[skn.b, sqn[h].b], w=[bb2], out=b2[:, 0:128],
                                     lhsT=skn[:, j * 128:128 + j * 128], rhs=sqn[h][:, cs], start=True, stop=True)
                                k.op("dve", "tensor_tensor", r=[bb2, cst.b], w=[W["tp"].b], out=W["tp"][:],
                                     in0=b2[:, 0:128], in1=cst[:, C_AL + (2 * h + 1) * 128:C_AL + (2 * h + 2) * 128],
                                     op=ALU.add)
                                k.op("act", "activation", r=[W["tp"].b], w=[W["pp"].b], out=W["pp"][:], in_=W["tp"][:],
                                     func=AF.Exp)
                            b3, bb3 = nb()
                            if has_prev:
                                yield
                                k.op("pe", "matmul", r=[svt.b, W["pp"].b], w=[bb3], sig=False, out=b3[:, 0:128],
                                     lhsT=svt[:, j, :], rhs=W["pp"][:], start=True, stop=False)
                            k.op("pe", "matmul", r=[svt.b, W["pc"].b], w=[bb3], out=b3[:, 0:128], lhsT=svt[:, j + 1, :],
                                 rhs=W["pc"][:], start=(not has_prev), stop=True)
                            b4, bb4 = nb()
                            if has_prev:
                                yield
                                k.op("pe", "matmul", r=[oneb.b, W["pp"].b], w=[bb4], sig=False, out=b4[:, 0:128],
                                     lhsT=oneb[:], rhs=W["pp"][:], start=True, stop=False)
                            k.op("pe", "matmul", r=[oneb.b, W["pc"].b], w=[bb4], out=b4[:, 0:128], lhsT=oneb[:],
                                 rhs=W["pc"][:], start=(not has_prev), stop=True)
                            k.op("act", "activation", r=[bb4, esink.b], w=[W["rd"].b], out=W["rd"][:], in_=b4[:, 0:128],
                                 func=AF.Ln, bias=esink[:, h:h + 1])
                            k.op("act", "activation", r=[W["rd"].b], w=[W["rd"].b], out=W["rd"][:], in_=W["rd"][:],
                                 func=AF.Exp, scale=-1.0)
                            k.op("dve", "tensor_tensor", r=[bb3, W["rd"].b], w=[sm.b], out=sm[:, cs], in0=b3[:, 0:128],
                                 in1=W["rd"][:], op=ALU.mult)
                        k.dma("sp", mix_in_l[p].ap()[512 + h * 128:512 + (h + 1) * 128, :], sm[:],
                              r=[sm.b], w=[b_mixin_l[p]], sem_from=sm.b)

                sgen = None
                if "swa" not in skip:
                    swa_pre()
                    sgen = swa_blocks()

                def swa_step():
                    nonlocal_s = sg_state
                    if nonlocal_s[0] is not None:
                        try:
                            next(nonlocal_s[0])
                        except StopIteration:
                            nonlocal_s[0] = None
                sg_state = [sgen]
                for pair in (() if "gdn" in skip else ((0, 1), (2, 3))):
                    for i in pair:
                        head_pre(i)
                    for j in range(4):
                        gens = [gdn_block(i, j) for i in pair]
                        while gens:
                            for g_ in list(gens):
                                try:
                                    next(g_)
                                except StopIteration:
                                    gens.remove(g_)
                            swa_step()
                    for i in pair:
                        head_post(i)
                while sg_state[0] is not None:
                    swa_step()
                if "cc" not in skip:
                    k.wait_all("pool", [b_mixin_l[p]])
                    ccs = k.newsem("cc%d" % p)

                    def cc(e, sems, p=p, ccs=ccs):
                        e.collective_compute("AllGather", ALU.bypass, replica_groups=[[0, 1, 2, 3], [4, 5, 6, 7]],
                                             ins=[mix_in_l[p].ap().opt()],
                                             outs=[mix_all_l[p].ap().opt()]).then_inc(sems[ccs])
                    k.raw("pool", cc)
                    b_mixall_l[p].writes = {ccs: 1}

            conv_issue(len(conv_jobs))
            b_wconv.writes = {convsem: conv_state["cnt"]}
            allb = [t.b for t in p1_tiles] + hTb
            allb += [bb for _, bb in banks] + dbg_bufs
            for e in ("pe", "act", "dve", "pool", "sp"):
                k.wait_all(e, allb)
            p1_bufs = allb

        with contextlib.ExitStack() as st:
            T = lambda n, s, d: Tl(arena, n, s, d)
            arena.off = arena_mark
            fnwbc = T("fnwbc_s", [128, D], F32)
            selt = T("selt", [128, 4], F32)
            h2T = T("h2T", [128, 32, 512], BF16)
            h2b = [Buf("h2T%d" % m) for m in range(4)]
            big = T("big", [128, NFB * 512], BF16)
            bigb = [Buf("big%d" % f) for f in range(NFB)]
            NS = 5
            ws = [T("ws%d" % i, [128, D], BF16) for i in range(NS)]
            cand = [T("cand%d" % i, [128, 2, 512], BF16) for i in range(2)]
            xsl = [T("xsl%d" % i, [128, 512], F32) for i in range(3)]
            x2f = [T("x2f%d" % i, [128, 512], F32) for i in range(3)]
            sg = [T("sg%d" % i, [128, 512], F32) for i in range(2)]
            ssq8 = [T("ssq8_%d" % m, [128, 8], F32) for m in range(4)]
            rs2 = [T("rs2_%d" % m, [128, 1], F32) for m in range(4)]
            junk = T("junk", [128, 512], BF16)
            for t_ in [fnwbc, selt, h2T, big] + ws + cand + xsl + x2f + sg + ssq8 + rs2 + [junk]:
                t_.b.reads = {}
            k.dma("sp", fnwbc[:], fnwbc_d, w=[fnwbc.b])
            k.dma("sp", selt[:], sel_d, w=[selt.b])
            wcount = [0]
            xcount = [0]

            def wslot():
                s = ws[wcount[0] % NS]
                wcount[0] += 1
                return s

            mixT = big[:, 0:32 * 512].rearrange("p (a b) -> p a b", a=32)
            mixb = bigb[0:32]
            x2b = big[:, 32 * 512:64 * 512].rearrange("p (m c) -> p m c", m=4)
            outb = {}

            for p in range(0 if "p2" in skip else NP2):
                t0 = p * 512
                for a in range(16):
                    dst = mixT[:, a * 2:(a + 1) * 2, :]
                    dbufs = mixb[a * 2:(a + 1) * 2]
                    for j in range(4):
                        c = cand[(a * 4 + j) % 2]
                        P_ = j * NP2 + p
                        k.dma("sp", c[:], mix_all_l[P_].ap().rearrange("(a q) t -> q a t", q=128)[:, a * 2:(a + 1) * 2, :],
                              r=[b_mixall_l[P_]], w=[c.b])
                        if j == 0:
                            k.op("dve", "tensor_scalar", r=[c.b, selt.b], w=dbufs, out=dst, in0=c[:],
                                 scalar1=selt[:, 0:1], scalar2=None, op0=ALU.mult)
                        else:
                            k.op("dve", "scalar_tensor_tensor", r=[c.b, selt.b] + dbufs, w=dbufs, out=dst, in0=c[:],
                                 scalar=selt[:, j:j + 1], in1=dst, op0=ALU.mult, op1=ALU.add)
                for n in range(8):
                    bk = [nb() for _ in range(4)]
                    for a in range(4):
                        s = wslot()
                        k.dma("pool", s[:], wo_b.ap()[n][:, a * 4096:(a + 1) * 4096], r=[b_wconv], w=[s.b])
                        for m in range(4):
                            for kk in range(8):
                                kc = a * 8 + kk
                                k.op("pe", "matmul", r=[s.b, mixb[kc]], w=[bk[m][1]],
                                     sig=(kk == 7 and (a == 3 or m == 3)), out=bk[m][0][:, 0:512],
                                     lhsT=mixT[:, kc, m * 128:(m + 1) * 128], rhs=s[:, kk * 512:(kk + 1) * 512],
                                     start=(a == 0 and kk == 0), stop=(a == 3 and kk == 7))
                    for m in range(4):
                        xi = xsl[xcount[0] % 3]
                        xo = x2f[xcount[0] % 3]
                        xcount[0] += 1
                        rs = slice(t0 + m * 128, t0 + (m + 1) * 128)
                        cs = slice(n * 512, (n + 1) * 512)
                        k.dma("sp", xi[:], x2tok[rs, cs], w=[xi.b])
                        k.op("dve", "tensor_tensor", r=[bk[m][1], xi.b], w=[xo.b], out=xo[:], in0=bk[m][0][:, 0:512],
                             in1=xi[:], op=ALU.add)
                        k.op("act", "activation", r=[xo.b], w=[junk.b, ssq8[m].b], out=junk[:], in_=xo[:],
                             func=AF.Square, accum_out=ssq8[m][:, n:n + 1])
                        k.op("act", "activation", r=[xo.b], w=[bigb[32 + m * 8 + n]], out=x2b[:, m, cs], in_=xo[:], func=AF.Copy)
                        ob = Buf("o_%d_%d" % (m, n))
                        outb[(m, n)] = ob
                        k.dma("sp", out_d[rs, cs], xo[:], r=[xo.b], w=[ob], sem_from=xo.b)
                for m in range(4):
                    xb_bufs = bigb[32 + m * 8:40 + m * 8]
                    k.op("dve", "tensor_reduce", r=[ssq8[m].b], w=[rs2[m].b], out=rs2[m][:], in_=ssq8[m][:],
                         axis=mybir.AxisListType.X, op=ALU.add)
                    k.op("act", "activation", r=[rs2[m].b], w=[rs2[m].b], out=rs2[m][:], in_=rs2[m][:], func=AF.Ln,
                         scale=1.0 / D, bias=EPS)
                    k.op("act", "activation", r=[rs2[m].b], w=[rs2[m].b], out=rs2[m][:], in_=rs2[m][:], func=AF.Exp,
                         scale=-0.5)
                    k.op("dve", "scalar_tensor_tensor", r=xb_bufs + [rs2[m].b, fnwbc.b], w=xb_bufs, out=x2b[:, m, :],
                         in0=x2b[:, m, :], scalar=rs2[m][:, 0:1], in1=fnwbc[:], op0=ALU.mult, op1=ALU.mult)
                    for kq in range(8):
                        bt, bb = nb()
                        btb = bt[:].bitcast(BF16)
                        for j in range(4):
                            kk = kq * 4 + j
                            k.op("pe", "transpose", r=xb_bufs + [idb.b], w=[bb], sig=(j == 3),
                                 out=btb[:, j * 128:(j + 1) * 128], in_=x2b[:, m, kk * 128:(kk + 1) * 128],
                                 identity=idb[:])
                        src = btb[:, 0:512].rearrange("p (a b) -> p a b", a=4)
                        dst = h2T[:, kq * 4:(kq + 1) * 4, m * 128:(m + 1) * 128]
                        if kq % 2 == 0:
                            k.op("act", "activation", r=[bb], w=[h2b[m]], out=dst, in_=src, func=AF.Copy)
                        else:
                            k.op("dve", "tensor_copy", r=[bb], w=[h2b[m]], out=dst, in_=src)
                for fb in range(NFB):
                    s_g = wslot()
                    k.dma("pool", s_g[:], wg_b.ap()[fb], r=[b_wconv], w=[s_g.b])
                    s_u = wslot()
                    k.dma("pool", s_u[:], wu_b.ap()[fb], r=[b_wconv], w=[s_u.b])
                    bg, bbg = nb()
                    for kk in range(32):
                        k.op("pe", "matmul", r=[s_g.b] + h2b, w=[bbg], sig=(kk == 31), out=bg[:, 0:512],
                             lhsT=s_g[:, kk * 128:(kk + 1) * 128], rhs=h2T[:, kk, :], start=(kk == 0),
                             stop=(kk == 31))
                    bu, bbu = nb()
                    for kk in range(32):
                        k.op("pe", "matmul", r=[s_u.b] + h2b, w=[bbu], sig=(kk == 31), out=bu[:, 0:512],
                             lhsT=s_u[:, kk * 128:(kk + 1) * 128], rhs=h2T[:, kk, :], start=(kk == 0),
                             stop=(kk == 31))
                    sgt = sg[fb % 2]
                    k.op("act", "activation", r=[bbg], w=[sgt.b], out=sgt[:], in_=bg[:, 0:512], func=AF.Silu)
                    k.op("dve", "tensor_tensor", r=[sgt.b, bbu], w=[bigb[fb]], out=big[:, fb * 512:(fb + 1) * 512],
                         in0=sgt[:], in1=bu[:, 0:512], op=ALU.mult)
                for q in range(4):
                    bk = [[nb() for _ in range(2)] for _ in range(4)]
                    for fg in range((NFB + 3) // 4):
                        nf = min(4, NFB - fg * 4)
                        s = wslot()
                        k.dma("pool", s[:, 0:nf * 1024], wd_b.ap()[q][:, fg * 4096:fg * 4096 + nf * 1024], r=[b_wconv], w=[s.b])
                        for f in range(nf):
                            fb = fg * 4 + f
                            for m in range(4):
                                for n2 in range(2):
                                    k.op("pe", "matmul", r=[s.b, bigb[fb]], w=[bk[m][n2][1]],
                                         sig=(fb == NFB - 1 or (f == nf - 1 and m == 3 and n2 == 1)),
                                         out=bk[m][n2][0][:, 0:512], lhsT=big[:, fb * 512 + m * 128:fb * 512 + (m + 1) * 128],
                                         rhs=s[:, f * 1024 + n2 * 512:f * 1024 + (n2 + 1) * 512],
                                         start=(fb == 0), stop=(fb == NFB - 1))
                    for m in range(4):
                        for n2 in range(2):
                            n = q * 2 + n2
                            xi = xsl[xcount[0] % 3]
                            xo = x2f[xcount[0] % 3]
                            xcount[0] += 1
                            rs = slice(t0 + m * 128, t0 + (m + 1) * 128)
                            cs = slice(n * 512, (n + 1) * 512)
                            ob = outb[(m, n)]
                            k.dma("sp", xi[:], out_d[rs, cs], r=[ob], w=[xi.b])
                            k.op("dve", "tensor_tensor", r=[bk[m][n2][1], xi.b], w=[xo.b], out=xo[:],
                                 in0=bk[m][n2][0][:, 0:512], in1=xi[:], op=ALU.add)
                            k.dma("sp", out_d[rs, cs], xo[:], r=[xo.b], w=[ob], sem_from=xo.b)
            k.wait_all("sp", list(outb.values()) + [t_.b for t_ in x2f] + dbg_bufs)
        k.emit()
    return nc


_CACHE = {}


def _consts(g):
    c = np.zeros((128, NCONST), np.float32)
    i = np.arange(128)
    S, Cc = np.meshgrid(i, i, indexing="ij")
    c[:, C_ID:C_ID + 128] = (S == Cc)
    c[:, C_ONE:C_ONE + 128] = 1.0
    c[:, C_UIN:C_UIN + 128] = (S <= Cc)
    c[:, C_LST:C_LST + 128] = (S > Cc)
    c[:, C_MST:C_MST + 128] = (Cc > S)
    bd = (S // 64 == Cc // 64)
    c[:, C_MBD:C_MBD + 128] = bd
    c[:, C_MNB:C_MNB + 128] = ~bd
    for h in range(4):
        slope = 2.0 ** (-8.0 * (4 * g + h + 1) / 16.0)
        kk, qq = S, Cc
        cur = np.where(qq >= kk, -slope * (qq - kk), -30000.0)
        prv = np.where(kk > qq, -slope * (qq + 128 - kk), -30000.0)
        c[:, C_AL + (2 * h) * 128:C_AL + (2 * h + 1) * 128] = cur
        c[:, C_AL + (2 * h + 1) * 128:C_AL + (2 * h + 2) * 128] = prv
    return c


def _prep_shared(inp):
    w_out, w_gate, w_up, w_down = inp["w_out"][0], inp["w_gate"][0], inp["w_up"][0], inp["w_down"][0]
    perm = np.zeros(4096, np.int64)
    for r in range(4):
        for loc in range(1024):
            hh, d = (loc % 512) // 128, loc % 128
            perm[r * 1024 + loc] = (0 if loc < 512 else 2048) + (4 * r + hh) * 128 + d
    wo = w_out[perm, :]
    sh = {}
    sh["wo_t"] = np.ascontiguousarray(wo.reshape(32, 128, 8, 512).transpose(2, 1, 0, 3)).reshape(8, 128, 32 * 512)
    sh["wg_t"] = np.ascontiguousarray(w_gate.reshape(32, 128, NFB, 128).transpose(2, 1, 0, 3)).reshape(NFB, 128, D)
    sh["wu_t"] = np.ascontiguousarray(w_up.reshape(32, 128, NFB, 128).transpose(2, 1, 0, 3)).reshape(NFB, 128, D)
    sh["wd_t"] = np.ascontiguousarray(w_down.reshape(NFB, 128, 4, 1024).transpose(2, 1, 0, 3)).reshape(4, 128, NFB * 1024)
    sh["anwbc"] = np.ascontiguousarray(np.broadcast_to(inp["attn_norm_w"][0][None, :], (128, D)))
    sh["fnwbc"] = np.ascontiguousarray(np.broadcast_to(inp["ffn_norm_w"][0][None, :], (128, D)))
    return sh


def _prep_group(inp, g):
    w_in = inp["w_in"][0]
    cols = []
    for t_ in range(4):
        for i in range(4):
            cols.append(t_ * 2048 + (4 * g + i) * 128)
    for h in range(4):
        cols.append(8224 + (4 * g + h) * 128)
    cols.append(10272 + g * 128)
    cols.append(10784 + g * 128)
    w1t = np.empty((NCB, 128, D), np.float32)
    for cb, c0 in enumerate(cols):
        w1t[cb] = w_in[:, c0:c0 + 128].reshape(32, 128, 128).transpose(1, 0, 2).reshape(128, D)
    abcols = [8192 + 4 * g + i for i in range(4)] + [8208 + 4 * g + i for i in range(4)]
    wab = np.ascontiguousarray(w_in[:, abcols].reshape(32, 128, 8).transpose(1, 0, 2)).reshape(128, 256)
    prm = np.zeros((128, NPRM), np.float32)
    cw = inp["conv_w"][0]
    for t_ in range(3):
        for i in range(4):
            ch = t_ * 2048 + (4 * g + i) * 128
            for j in range(4):
                prm[:, P_CW + (t_ * 4 + i) * 4 + j] = cw[j, ch:ch + 128]
    prm[:, P_ALOG:P_ALOG + 4] = inp["a_log"][0][4 * g:4 * g + 4][None, :]
    prm[:, P_DTB:P_DTB + 4] = inp["dt_bias"][0][4 * g:4 * g + 4][None, :]
    prm[:, P_DNW] = inp["dn_norm_w"][0]
    prm[:, P_QNW] = inp["q_norm_w"][0]
    prm[:, P_KNW] = inp["k_norm_w"][0]
    prm[:, P_SNK:P_SNK + 4] = inp["sinks"][0][4 * g:4 * g + 4][None, :]
    sel = np.zeros((128, 4), np.float32)
    sel[:, g] = 1.0
    return {"w1t": w1t, "wab": wab, "prm": prm, "consts": _consts(g), "sel": sel}


def make_in_maps(inp, SEQ):
    x = inp["x"]
    sh = _prep_shared(inp)
    TOK2 = SEQ // 4
    maps = []
    grp = [_prep_group(inp, g) for g in range(4)]
    for c in range(8):
        b, g = c // 4, c % 4
        m = dict(sh)
        m.update(grp[g])
        m["xb"] = np.ascontiguousarray(x[b, :SEQ])
        m["x2tok"] = np.ascontiguousarray(x[b, g * TOK2:(g + 1) * TOK2])
        maps.append(m)
    return maps


def kernel(**inputs):
    inp = {k_: np.asarray(v) for k_, v in inputs.items()}
    SEQ = inp["x"].shape[1]
    if SEQ not in _CACHE:
        _CACHE[SEQ] = build(SEQ)
    nc = _CACHE[SEQ]
    maps = make_in_maps(inp, SEQ)
    res = run_bass_kernel_spmd(nc, maps, core_ids=list(range(8)))
    TOK2 = SEQ // 4
    out = np.empty((2, SEQ, D), np.float32)
    for c in range(8):
        b, g = c // 4, c % 4
        out[b, g * TOK2:(g + 1) * TOK2] = np.asarray(res.results[c]["out"])
    return out
```

```python
import contextlib
import numpy as np
import concourse.bass as bass
import concourse.mybir as mybir
from concourse.bass_utils import run_bass_kernel_spmd

F32 = mybir.dt.float32
BF16 = mybir.dt.bfloat16
ALU = mybir.AluOpType
AF = mybir.ActivationFunctionType
ENGS = ("pe", "act", "dve", "pool", "sp")
EPS = 1e-6
D = 4096
FF = 11008
NFB = FF // 128
NCB = 22


class Buf:
    __slots__ = ("name", "writes", "reads", "sem", "semval", "excl")

    def __init__(self, name, excl=False):
        self.name = name
        self.excl = excl
        self.writes = {}
        self.reads = {}
        self.sem = None
        self.semval = 0


class K:
    RELAY = True

    def __init__(self, nc):
        self.nc = nc
        self.q = {e: [] for e in ENGS}
        self.cnt = {e: 0 for e in ENGS}
        self.waited = {e: {} for e in ENGS}
        self.semkeys = ["E_" + e for e in ENGS]
        self.relay_sem = "S_relay"
        self.semkeys.append(self.relay_sem)
        self.relay_cnt = 0
        self.tok_out = None
        self.tok_rows = 8192
        self.tok_in = None

    def newsem(self, name):
        key = "S%d_%s" % (len(self.semkeys), name)
        self.semkeys.append(key)
        return key

    def _wait(self, eng, evs):
        if eng == "pool" and K.RELAY:
            comp = {sk: v for sk, v in evs.items()
                    if sk.startswith("E_") and v > 0 and self.waited["pool"].get(sk, 0) < v}
            if comp:
                for sk, v in comp.items():
                    self.waited["pool"][sk] = v
                self._wait("sp", comp)
                self.relay_cnt += 16
                ti = self.relay_cnt // 16 - 1
                assert ti < self.tok_rows
                self.q["sp"].append(("dma", (self.tok_out[ti:ti + 1, :], self.tok_in, self.relay_sem)))
                self.q["pool"].append(("wait", (self.relay_sem, self.relay_cnt)))
            evs = {sk: v for sk, v in evs.items() if not sk.startswith("E_")}
        for sk, v in evs.items():
            if v <= 0 or (eng == "pe" and sk == "E_pe"):
                continue
            if self.waited[eng].get(sk, 0) >= v:
                continue
            self.waited[eng][sk] = v
            self.q[eng].append(("wait", (sk, v)))

    def _deps(self, eng, r, w):
        evs = {}
        for b in r:
            for sk, v in b.writes.items():
                if evs.get(sk, 0) < v:
                    evs[sk] = v
        for b in w:
            for d in (b.writes, b.reads):
                for sk, v in d.items():
                    if evs.get(sk, 0) < v:
                        evs[sk] = v
        self._wait(eng, evs)

    def op(self, eng, meth, r=(), w=(), sig=True, **kw):
        if eng != "pe":
            ex = [b for b in r if b.excl]
            if ex:
                r = [b for b in r if not b.excl]
                w = list(w) + ex
        self._deps(eng, r, w)
        sk = "E_" + eng
        if eng == "pe" and not sig:
            ev = self.cnt[eng] + 1
            inc = False
        else:
            self.cnt[eng] += 1
            ev = self.cnt[eng]
            inc = True
        self.q[eng].append(("op", (meth, kw, sk if inc else None)))
        for b in r:
            if b.reads.get(sk, 0) < ev:
                b.reads[sk] = ev
        for b in w:
            b.writes = {sk: ev}
            b.reads = {}

    def dma(self, eng, out, in_, r=(), w=(), sem_from=None):
        self._deps(eng, r, w)
        tgt = sem_from if sem_from is not None else (w[0] if len(w) else r[0])
        if tgt.sem is None:
            tgt.sem = self.newsem(tgt.name)
        tgt.semval += 16
        sk, v = tgt.sem, tgt.semval
        self.q[eng].append(("dma", (out, in_, sk)))
        for b in r:
            if b.reads.get(sk, 0) < v:
                b.reads[sk] = v
        for b in w:
            b.writes = {sk: v}
            b.reads = {}

    def wait_all(self, eng, bufs):
        evs = {}
        for b in bufs:
            for d in (b.writes, b.reads):
                for sk, v in d.items():
                    if evs.get(sk, 0) < v:
                        evs[sk] = v
        self._wait(eng, evs)

    def raw(self, eng, fn):
        self.q[eng].append(("raw", fn))

    def emit(self):
        nc = self.nc
        with contextlib.ExitStack() as st:
            sems = {}
            for sk in self.semkeys:
                sems[sk] = st.enter_context(nc.semaphore(sk))
            block = st.enter_context(nc.Block())
            q = self.q

            def run(engname, e):
                for kind, p in q[engname]:
                    if kind == "wait":
                        e.wait_ge(sems[p[0]], p[1])
                    elif kind == "op":
                        meth, kw, sk = p
                        ins = getattr(e, meth)(**kw)
                        if sk is not None:
                            ins.then_inc(sems[sk], 1)
                    elif kind == "dma":
                        out, in_, sk = p
                        e.dma_start(out=out, in_=in_).then_inc(sems[sk], 16)
                    else:
                        p(e, sems)

            @block.tensor
            def _(e):
                run("pe", e)

            @block.scalar
            def _(e):
                run("act", e)

            @block.vector
            def _(e):
                run("dve", e)

            @block.gpsimd
            def _(e):
                run("pool", e)

            @block.sync
            def _(e):
                run("sp", e)


class Arena:
    def __init__(self, nc, st, nbytes):
        self.n = nbytes // 2
        self.t = st.enter_context(nc.sbuf_tensor("arena", [128, self.n], BF16))
        self.off = 0

    def alloc(self, nbytes):
        nb_ = (nbytes + 31) // 32 * 32
        o = self.off
        self.off += nb_ // 2
        assert self.off <= self.n, "SBUF arena overflow: %d > %d" % (self.off * 2, self.n * 2)
        return o


class Tl:
    def __init__(self, arena, name, shape, dt):
        nel = 1
        for s in shape[1:]:
            nel *= s
        esz = 4 if dt == F32 else 2
        o = arena.alloc(nel * esz)
        ap = arena.t[:, o:o + nel * esz // 2]
        if dt != BF16:
            ap = ap.bitcast(dt)
        if len(shape) == 3:
            ap = ap.rearrange("p (a b) -> p a b", a=shape[1])
        self.ap = ap
        self.b = Buf(name)

    def __getitem__(self, idx):
        return self.ap[idx]


C_ID, C_ONE, C_UIN, C_LST, C_MST, C_MBD, C_MNB, C_AL = [i * 128 for i in range(8)]
NCONST = C_AL + 8 * 128
P_CW, P_ALOG, P_DTB, P_DNW, P_QNW, P_KNW, P_SNK = 0, 48, 52, 56, 57, 58, 59
NPRM = 64


def build(SEQ, dbg=None, skip=(), np1=None):
    nc = bass.Bass("TRN2", target_bir_lowering=False)
    NP1 = SEQ // 512 if np1 is None else np1
    TOK2 = SEQ // 4
    NP2 = TOK2 // 512
    dram = lambda n, s, d, kind=None: (nc.dram_tensor(n, s, d, kind=kind) if kind else nc.dram_tensor(n, s, d))
    xb = dram("xb", [SEQ, D], F32, "ExternalInput").ap()
    consts = dram("consts", [128, NCONST], F32, "ExternalInput").ap()
    prm = dram("prm", [128, NPRM], F32, "ExternalInput").ap()
    anwbc_d = dram("anwbc", [128, D], F32, "ExternalInput").ap()
    fnwbc_d = dram("fnwbc", [128, D], F32, "ExternalInput").ap()
    w1t = dram("w1t", [NCB, 128, D], F32, "ExternalInput").ap()
    wab_d = dram("wab", [128, 256], F32, "ExternalInput").ap()
    wo_t = dram("wo_t", [8, 128, 32 * 512], F32, "ExternalInput").ap()
    wg_t = dram("wg_t", [NFB, 128, D], F32, "ExternalInput").ap()
    wu_t = dram("wu_t", [NFB, 128, D], F32, "ExternalInput").ap()
    wd_t = dram("wd_t", [4, 128, NFB * 1024], F32, "ExternalInput").ap()
    out_d = dram("out", [TOK2, D], F32, "ExternalOutput").ap()
    x2tok = dram("x2tok", [TOK2, D], F32, "ExternalInput").ap()
    sel_d = dram("sel", [128, 4], F32, "ExternalInput").ap()
    NPT = SEQ // 512
    wo_b = dram("wo_b", [8, 128, 32 * 512], BF16)
    wg_b = dram("wg_b", [NFB, 128, D], BF16)
    wu_b = dram("wu_b", [NFB, 128, D], BF16)
    wd_b = dram("wd_b", [4, 128, NFB * 1024], BF16)
    mix_in_l = [dram("mix_in%d" % p_, [1024, 512], BF16) for p_ in range(NPT)]
    mix_all_l = [dram("mix_all%d" % p_, [4096, 512], BF16) for p_ in range(NPT)]
    dbg_d = {}
    if dbg:
        for name, shape, dt in dbg:
            dbg_d[name] = dram("dbg_" + name, shape, dt, "ExternalOutput").ap()

    k = K(nc)
    tok_d = dram("tok_d", [8192, 16], F32)
    k.tok_out = tok_d.ap()
    k.tok_in = consts[0:1, 0:16]
    b_mixin_l = [Buf("mixin%d" % p_) for p_ in range(NPT)]
    b_wconv = Buf("wconv")
    convsem = k.newsem("conv")
    conv_jobs = []
    for n_ in range(8):
        for a_ in range(4):
            conv_jobs.append((wo_b.ap()[n_][:, a_ * 4096:(a_ + 1) * 4096], wo_t[n_][:, a_ * 4096:(a_ + 1) * 4096]))
    for fb_ in range(NFB):
        conv_jobs.append((wg_b.ap()[fb_], wg_t[fb_]))
        conv_jobs.append((wu_b.ap()[fb_], wu_t[fb_]))
    for q_ in range(4):
        for fg_ in range((NFB + 3) // 4):
            nf_ = min(4, NFB - fg_ * 4)
            conv_jobs.append((wd_b.ap()[q_][:, fg_ * 4096:fg_ * 4096 + nf_ * 1024],
                              wd_t[q_][:, fg_ * 4096:fg_ * 4096 + nf_ * 1024]))
    conv_state = {"i": 0, "cnt": 0}

    def conv_issue(n):
        while n > 0 and conv_state["i"] < len(conv_jobs):
            o_, i_ = conv_jobs[conv_state["i"]]
            k.q["pool"].append(("dma", (o_, i_, convsem)))
            conv_state["i"] += 1
            conv_state["cnt"] += 16
            n -= 1
    b_mixall_l = [Buf("mixall%d" % p_) for p_ in range(NPT)]
    b_out = Buf("outd")

    with contextlib.ExitStack() as st0:
        banks = []
        for i in range(8):
            t = st0.enter_context(nc.psum_tensor("bank%d" % i, [128, 512], F32))
            banks.append((t, Buf("bank%d" % i, excl=True)))
        rot = [0]

        def nb():
            i = rot[0] % 8
            rot[0] += 1
            return banks[i]

        arena = Arena(nc, st0, 206 * 1024)
        cst = Tl(arena, "cst", [128, NCONST], F32)
        prt = Tl(arena, "prt", [128, NPRM], F32)
        idb = Tl(arena, "idb", [128, 128], BF16)
        oneb = Tl(arena, "oneb", [128, 128], BF16)
        arena_mark = arena.off
        k.dma("sp", cst[:], consts, w=[cst.b])
        k.dma("sp", prt[:], prm, w=[prt.b])
        k.op("dve", "tensor_copy", r=[cst.b], w=[idb.b], out=idb[:], in_=cst[:, C_ID:C_ID + 128])
        k.op("dve", "tensor_copy", r=[cst.b], w=[oneb.b], out=oneb[:], in_=cst[:, C_ONE:C_ONE + 128])
        ident = cst[:, C_ID:C_ID + 128]
        ones_f = cst[:, C_ONE:C_ONE + 128]

        def dbg_store(name, ap_dram_idx, src_ap, srcbuf):
            if name in dbg_d:
                k.dma("sp", dbg_d[name][ap_dram_idx], src_ap, r=[srcbuf])
                dbg_bufs.append(srcbuf)
        dbg_bufs = []

        with contextlib.ExitStack() as st:
            p1_tiles = []

            def T(n, s, d):
                t = Tl(arena, n, s, d)
                p1_tiles.append(t)
                return t
            xin = [T("xin%d" % i, [128, D], F32) for i in range(2)]
            xs = T("xs", [128, D], BF16)
            anwbc = T("anwbc_s", [128, D], F32)
            hT = T("hT", [128, 32, 512], BF16)
            hTb = [Buf("hT%d" % m) for m in range(4)]
            NW = 3
            wsl = [T("wsl%d" % i, [128, D], BF16) for i in range(NW)]
            wabf = T("wabf", [128, 256], F32)
            wabb = T("wabb", [128, 256], BF16)
            ssq = T("ssq", [128, 1], F32)
            rstd = T("rstd", [128, 1], F32)
            nA = T("nA", [128, 4], F32)
            dnw2 = T("dnw2", [128, 1], F32)
            knw2 = T("knw2", [128, 1], F32)
            esink = T("esink", [128, 4], F32)
            k.dma("sp", anwbc[:], anwbc_d, w=[anwbc.b])
            k.dma("sp", wabf[:], wab_d, w=[wabf.b])
            k.op("dve", "tensor_copy", r=[wabf.b], w=[wabb.b], out=wabb[:], in_=wabf[:])
            k.op("act", "activation", r=[prt.b], w=[nA.b], out=nA[:], in_=prt[:, P_ALOG:P_ALOG + 4], func=AF.Exp)
            k.op("dve", "tensor_scalar", r=[nA.b], w=[nA.b], out=nA[:], in0=nA[:], scalar1=-1.0, scalar2=None,
                 op0=ALU.mult)
            k.op("act", "activation", r=[prt.b], w=[esink.b], out=esink[:], in_=prt[:, P_SNK:P_SNK + 4], func=AF.Exp)
            k.op("dve", "tensor_scalar", r=[prt.b], w=[dnw2.b], out=dnw2[:], in0=prt[:, P_DNW:P_DNW + 1],
                 scalar1=float(np.sqrt(128.0)), scalar2=None, op0=ALU.mult)
            k.op("dve", "tensor_scalar", r=[prt.b], w=[knw2.b], out=knw2[:], in0=prt[:, P_KNW:P_KNW + 1],
                 scalar1=float(np.sqrt(128.0)), scalar2=None, op0=ALU.mult)

            cb_order = []
            for i in (0, 1, 2, 3):
                cb_order += [i, 4 + i, 8 + i, 12 + i]
            cb_order += [16, 17, 18, 19, 20, 21]
            wseq = [(p, cb) for p in range(NP1) for cb in cb_order]
            wpos = {pc: n for n, pc in enumerate(wseq)}
            wstate = {"issued": 0}
            conv_per = -(-len(conv_jobs) // max(1, len(wseq)))

            def wprefetch(upto):
                while wstate["issued"] <= min(upto, len(wseq) - 1):
                    i = wstate["issued"]
                    s = wsl[i % NW]
                    k.dma("pool", s[:], w1t[wseq[i][1]], w=[s.b])
                    wstate["issued"] += 1
                    conv_issue(conv_per)

            def inproj(p, cb):
                i = wpos[(p, cb)]
                wprefetch(i + NW - 1)
                s = wsl[i % NW]
                bt, bb = nb()
                for kk in range(32):
                    k.op("pe", "matmul", r=[s.b] + hTb, w=[bb], sig=(kk == 31), out=bt[:, 0:512],
                         lhsT=s[:, kk * 128:(kk + 1) * 128], rhs=hT[:, kk, :], start=(kk == 0), stop=(kk == 31))
                return bt, bb

            raw = T("raw", [128, 515], F32)
            halo = [[T("halo%d_%d" % (t_, i), [128, 3], F32) for i in range(4)] for t_ in range(3)]
            for t_ in range(3):
                for i in range(4):
                    k.op("dve", "memset", w=[halo[t_][i].b], ap=halo[t_][i][:], constant=0.0)
            acc = T("acc", [128, 512], F32)
            th = T("th", [128, 512], F32)
            sqb = T("sqb", [128, 512], BF16)
            rr = T("rr", [128, 512], F32)
            qn = [T("qn%d" % i, [128, 512], F32) for i in range(2)]
            qnb = [T("qnb%d" % i, [128, 512], BF16) for i in range(2)]
            knb = [T("knb%d" % i, [128, 512], BF16) for i in range(2)]
            v2b = T("v2b", [128, 512], BF16)
            vtok = [T("vtok%d" % i, [128, 512], BF16) for i in range(2)]
            z2 = [T("z2_%d" % i, [128, 512], F32) for i in range(2)]
            oT = [T("oT%d" % i, [128, 512], F32) for i in range(2)]
            mixt = [T("mixt%d" % i, [128, 512], BF16) for i in range(2)]
            Sf = [T("Sf%d" % i, [128, 128], F32) for i in range(4)]
            Sb = [T("Sb%d" % i, [128, 128], BF16) for i in range(4)]
            for i in range(4):
                k.op("dve", "memset", w=[Sf[i].b], ap=Sf[i][:], constant=0.0)
                k.op("dve", "memset", w=[Sb[i].b], ap=Sb[i][:], constant=0.0)
            gnames = ["tA", "eA", "sp", "g", "tb", "beta", "nbeta", "gcc", "egc", "tail"]
            gt = [{n: T("g_%s%d" % (n, j), [128, 4], F32) for n in gnames} for j in range(4)]
            f32names = ["Gh", "Dt", "E", "egb", "EMi", "EMs", "Nf", "A0", "A0T", "Nof", "NofT", "P0", "P1",
                        "A1", "A1T", "XT", "Y"]
            bfnames = ["PT", "T2T", "qd", "ktl", "r2n", "vnw"]
            hb = [dict([(n, T("hb_%s%d" % (n, par), [128, 128], F32)) for n in f32names] +
                       [(n, T("hb_%s%d" % (n, par), [128, 128], BF16)) for n in bfnames]) for par in range(2)]
            sraw = T("sraw", [128, 512], F32)
            sqn = [T("sqn%d" % h, [128, 512], BF16) for h in range(4)]
            skn = T("skn", [128, 640], BF16)
            svb = T("svb", [128, 512], BF16)
            svt = T("svt", [128, 5, 128], BF16)
            smix = [T("smix%d" % i, [128, 512], BF16) for i in range(2)]
            sw = [dict(tc=T("sw_tc%d" % par, [128, 128], F32), tp=T("sw_tp%d" % par, [128, 128], F32),
                       pc=T("sw_pc%d" % par, [128, 128], BF16), pp=T("sw_pp%d" % par, [128, 128], BF16),
                       rd=T("sw_rd%d" % par, [128, 128], F32)) for par in range(2)]

            cwcol = lambda t_, i, j: prt[:, P_CW + (t_ * 4 + i) * 4 + j:P_CW + (t_ * 4 + i) * 4 + j + 1]
            hbcount = [0]
            swcount = [0]

            for p in range(NP1):
                for m in range(4):
                    xi = xin[(4 * p + m) % 2]
                    r0 = p * 512 + m * 128
                    k.dma("sp", xi[:], xb[r0:r0 + 128, :], w=[xi.b])
                    k.op("act", "activation", r=[xi.b], w=[xs.b, ssq.b], out=xs[:], in_=xi[:], func=AF.Square,
                         accum_out=ssq[:, 0:1])
                    k.op("act", "activation", r=[ssq.b], w=[rstd.b], out=rstd[:], in_=ssq[:], func=AF.Ln,
                         scale=1.0 / D, bias=EPS)
                    k.op("act", "activation", r=[rstd.b], w=[rstd.b], out=rstd[:], in_=rstd[:], func=AF.Exp,
                         scale=-0.5)
                    k.op("dve", "scalar_tensor_tensor", r=[xi.b, rstd.b, anwbc.b], w=[xs.b], out=xs[:], in0=xi[:],
                         scalar=rstd[:, 0:1], in1=anwbc[:], op0=ALU.mult, op1=ALU.mult)
                    for kq in range(8):
                        bt, bb = nb()
                        btb = bt[:].bitcast(BF16)
                        for j in range(4):
                            kk = kq * 4 + j
                            k.op("pe", "transpose", r=[xs.b, idb.b], w=[bb], sig=(j == 3),
                                 out=btb[:, j * 128:(j + 1) * 128], in_=xs[:, kk * 128:(kk + 1) * 128],
                                 identity=idb[:])
                        src = btb[:, 0:512].rearrange("p (a b) -> p a b", a=4)
                        dst = hT[:, kq * 4:(kq + 1) * 4, m * 128:(m + 1) * 128]
                        if kq % 2 == 0:
                            k.op("act", "activation", r=[bb], w=[hTb[m]], out=dst, in_=src, func=AF.Copy)
                        else:
                            k.op("dve", "tensor_copy", r=[bb], w=[hTb[m]], out=dst, in_=src)
                if p == 0 and "hT" in dbg_d:
                    k.dma("sp", dbg_d["hT"], hT[:], r=hTb)
                    dbg_bufs.extend(hTb)

                for j in range(4):
                    G = gt[j]
                    bt, bb = nb()
                    for kk in range(32):
                        k.op("pe", "matmul", r=[wabb.b] + hTb, w=[bb], sig=(kk == 31), out=bt[:, 0:8],
                             lhsT=hT[:, kk, j * 128:(j + 1) * 128], rhs=wabb[:, kk * 8:(kk + 1) * 8],
                             start=(kk == 0), stop=(kk == 31))
                    k.op("dve", "tensor_tensor", r=[bb, prt.b], w=[G["tA"].b], out=G["tA"][:], in0=bt[:, 0:4],
                         in1=prt[:, P_DTB:P_DTB + 4], op=ALU.add)
                    k.op("act", "activation", r=[bb], w=[G["tb"].b], out=G["tb"][:], in_=bt[:, 4:8], func=AF.Exp,
                         scale=-1.0)
                    k.op("act", "activation", r=[G["tb"].b], w=[G["tb"].b], out=G["tb"][:], in_=G["tb"][:],
                         func=AF.Ln, bias=1.0)
                    k.op("act", "activation", r=[G["tb"].b], w=[G["beta"].b], out=G["beta"][:], in_=G["tb"][:],
                         func=AF.Exp, scale=-1.0)
                    k.op("act", "activation", r=[G["tA"].b], w=[G["eA"].b], out=G["eA"][:], in_=G["tA"][:],
                         func=AF.Exp)
                    k.op("act", "activation", r=[G["eA"].b], w=[G["sp"].b], out=G["sp"][:], in_=G["eA"][:],
                         func=AF.Ln, bias=1.0)
                    k.op("dve", "tensor_tensor", r=[G["sp"].b, nA.b], w=[G["g"].b], out=G["g"][:], in0=G["sp"][:],
                         in1=nA[:], op=ALU.mult)
                    k.op("dve", "tensor_scalar", r=[G["beta"].b], w=[G["nbeta"].b], out=G["nbeta"][:],
                         in0=G["beta"][:], scalar1=-1.0, scalar2=None, op0=ALU.mult)
                    bt2, bb2 = nb()
                    k.op("pe", "matmul", r=[cst.b, G["g"].b], w=[bb2], out=bt2[:, 0:4],
                         lhsT=cst[:, C_UIN:C_UIN + 128], rhs=G["g"][:], start=True, stop=True)
                    k.op("act", "activation", r=[bb2], w=[G["gcc"].b], out=G["gcc"][:], in_=bt2[:, 0:4],
                         func=AF.Copy)
                    k.op("act", "activation", r=[bb2], w=[G["egc"].b], out=G["egc"][:], in_=bt2[:, 0:4],
                         func=AF.Exp)
                    bt3, bb3 = nb()
                    k.op("pe", "matmul", r=[cst.b, G["g"].b], w=[bb3], out=bt3[:, 0:4],
                         lhsT=cst[:, C_LST:C_LST + 128], rhs=G["g"][:], start=True, stop=True)
                    k.op("act", "activation", r=[bb3], w=[G["tail"].b], out=G["tail"][:], in_=bt3[:, 0:4],
                         func=AF.Exp)

                def head_pre(i):
                    par = i % 2
                    for t_ in range(3):
                        bt, bb = inproj(p, t_ * 4 + i)
                        k.op("dve", "tensor_copy", r=[halo[t_][i].b], w=[raw.b], out=raw[:, 0:3],
                             in_=halo[t_][i][:])
                        k.op("act", "activation", r=[bb], w=[raw.b], out=raw[:, 3:515], in_=bt[:, 0:512],
                             func=AF.Copy)
                        k.op("dve", "tensor_scalar", r=[raw.b, prt.b], w=[acc.b], out=acc[:], in0=raw[:, 0:512],
                             scalar1=cwcol(t_, i, 0), scalar2=None, op0=ALU.mult)
                        for j in range(1, 4):
                            k.op("dve", "scalar_tensor_tensor", r=[raw.b, prt.b, acc.b], w=[acc.b], out=acc[:],
                                 in0=raw[:, j:j + 512], scalar=cwcol(t_, i, j), in1=acc[:], op0=ALU.mult,
                                 op1=ALU.add)
                        k.op("dve", "tensor_copy", r=[raw.b], w=[halo[t_][i].b], out=halo[t_][i][:],
                             in_=raw[:, 512:515])
                        k.op("act", "activation", r=[acc.b], w=[th.b], out=th[:], in_=acc[:], func=AF.Exp,
                             scale=-1.0)
                        k.op("act", "activation", r=[th.b], w=[th.b], out=th[:], in_=th[:], func=AF.Ln, bias=1.0)
                        k.op("act", "activation", r=[th.b], w=[th.b], out=th[:], in_=th[:], func=AF.Exp, scale=-1.0)
                        k.op("dve", "tensor_tensor", r=[th.b, acc.b], w=[th.b], out=th[:], in0=th[:], in1=acc[:],
                             op=ALU.mult)
                        if t_ < 2:
                            k.op("act", "activation", r=[th.b], w=[sqb.b], out=sqb[:], in_=th[:], func=AF.Square)
                            bt2, bb2 = nb()
                            k.op("pe", "matmul", r=[oneb.b, sqb.b], w=[bb2], out=bt2[:, 0:512], lhsT=oneb[:],
                                 rhs=sqb[:], start=True, stop=True)
                            k.op("act", "activation", r=[bb2], w=[rr.b], out=rr[:], in_=bt2[:, 0:512], func=AF.Ln,
                                 bias=EPS)
                            k.op("act", "activation", r=[rr.b], w=[rr.b], out=rr[:], in_=rr[:], func=AF.Exp,
                                 scale=-0.5)
                            if t_ == 0:
                                k.op("dve", "scalar_tensor_tensor", r=[th.b, rr.b], w=[qn[par].b], out=qn[par][:],
                                     in0=th[:], scalar=float(128.0 ** -0.5), in1=rr[:], op0=ALU.mult,
                                     op1=ALU.mult)
                                k.op("act", "activation", r=[qn[par].b], w=[qnb[par].b], out=qnb[par][:],
                                     in_=qn[par][:], func=AF.Copy)
                            else:
                                k.op("dve", "tensor_tensor", r=[th.b, rr.b], w=[knb[par].b], out=knb[par][:],
                                     in0=th[:], in1=rr[:], op=ALU.mult)
                        else:
                            k.op("act", "activation", r=[th.b], w=[v2b.b], out=v2b[:], in_=th[:], func=AF.Copy)
                            bt2, bb2 = nb()
                            btb = bt2[:].bitcast(BF16)
                            for j in range(4):
                                k.op("pe", "transpose", r=[v2b.b, idb.b], w=[bb2], sig=(j == 3),
                                     out=btb[:, j * 128:(j + 1) * 128], in_=v2b[:, j * 128:(j + 1) * 128],
                                     identity=idb[:])
                            k.op("act", "activation", r=[bb2], w=[vtok[par].b], out=vtok[par][:],
                                 in_=btb[:, 0:512], func=AF.Copy)
                    bt, bb = inproj(p, 12 + i)
                    k.op("act", "activation", r=[bb], w=[th.b], out=th[:], in_=bt[:, 0:512], func=AF.Exp, scale=-1.0)
                    k.op("act", "activation", r=[th.b], w=[th.b], out=th[:], in_=th[:], func=AF.Ln, bias=1.0)
                    k.op("act", "activation", r=[th.b], w=[th.b], out=th[:], in_=th[:], func=AF.Exp, scale=-1.0)
                    k.op("dve", "tensor_tensor", r=[th.b, bb], w=[z2[par].b], out=z2[par][:], in0=th[:],
                         in1=bt[:, 0:512], op=ALU.mult)


                def gdn_block(i, j):
                    par = i % 2
                    H = hb[i % 2]
                    if True:
                        G = gt[j]
                        cs = slice(j * 128, (j + 1) * 128)
                        gcol = G["g"][:, i:i + 1]
                        k.op("dve", "tensor_scalar", r=[cst.b, G["g"].b], w=[H["Gh"].b], out=H["Gh"][:], in0=ones_f,
                             scalar1=gcol, scalar2=None, op0=ALU.mult)
                        btg, bbg = nb()
                        yield
                        k.op("pe", "matmul", r=[H["Gh"].b, cst.b], w=[bbg], out=btg[:, 0:128], lhsT=H["Gh"][:],
                             rhs=cst[:, C_UIN:C_UIN + 128], start=True, stop=True)
                        k.op("dve", "tensor_scalar", r=[bbg, G["gcc"].b], w=[H["Dt"].b], out=H["Dt"][:],
                             in0=btg[:, 0:128], scalar1=G["gcc"][:, i:i + 1], scalar2=0.0, op0=ALU.subtract,
                             op1=ALU.min)
                        k.op("act", "activation", r=[bbg], w=[H["egb"].b], out=H["egb"][:], in_=btg[:, 0:128],
                             func=AF.Exp)
                        k.op("act", "activation", r=[H["Dt"].b], w=[H["E"].b], out=H["E"][:], in_=H["Dt"][:],
                             func=AF.Exp)
                        k.op("dve", "tensor_tensor", r=[H["E"].b, cst.b], w=[H["EMi"].b], out=H["EMi"][:],
                             in0=H["E"][:], in1=cst[:, C_UIN:C_UIN + 128], op=ALU.mult)
                        k.op("dve", "tensor_tensor", r=[H["E"].b, cst.b], w=[H["EMs"].b], out=H["EMs"][:],
                             in0=H["E"][:], in1=cst[:, C_MST:C_MST + 128], op=ALU.mult)
                        k.op("dve", "tensor_tensor", r=[qn[par].b, H["egb"].b], w=[H["qd"].b], out=H["qd"][:],
                             in0=qn[par][:, cs], in1=H["egb"][:], op=ALU.mult)
                        btk, bbk = nb()
                        yield
                        k.op("pe", "matmul", r=[knb[par].b], w=[bbk], out=btk[:, 0:128], lhsT=knb[par][:, cs],
                             rhs=knb[par][:, cs], start=True, stop=True)
                        k.op("dve", "scalar_tensor_tensor", r=[bbk, G["beta"].b, H["EMs"].b], w=[H["Nf"].b],
                             out=H["Nf"][:], in0=btk[:, 0:128], scalar=G["beta"][:, i:i + 1], in1=H["EMs"][:],
                             op0=ALU.mult, op1=ALU.mult)
                        btq, bbq = nb()
                        yield
                        k.op("pe", "matmul", r=[knb[par].b, qnb[par].b], w=[bbq], out=btq[:, 0:128],
                             lhsT=knb[par][:, cs], rhs=qnb[par][:, cs], start=True, stop=True)
                        k.op("dve", "tensor_tensor", r=[bbq, H["EMi"].b], w=[H["PT"].b], out=H["PT"][:],
                             in0=btq[:, 0:128], in1=H["EMi"][:], op=ALU.mult)
                        btt, bbt = nb()
                        bttb = btt[:].bitcast(BF16)
                        yield
                        k.op("pe", "transpose", r=[knb[par].b, idb.b], w=[bbt], out=bttb[:, 0:128],
                             in_=knb[par][:, cs], identity=idb[:])
                        k.op("act", "activation", r=[bbt, G["tail"].b], w=[H["ktl"].b], out=H["ktl"][:],
                             in_=bttb[:, 0:128], func=AF.Copy, scale=G["tail"][:, i:i + 1])
                        k.op("dve", "tensor_tensor", r=[H["Nf"].b, cst.b], w=[H["A0"].b], out=H["A0"][:],
                             in0=H["Nf"][:], in1=cst[:, C_MBD:C_MBD + 128], op=ALU.mult)
                        k.op("dve", "tensor_tensor", r=[H["Nf"].b, cst.b], w=[H["Nof"].b], out=H["Nof"][:],
                             in0=H["Nf"][:], in1=cst[:, C_MNB:C_MNB + 128], op=ALU.mult)
                        btn, bbn = nb()
                        yield
                        k.op("pe", "transpose", r=[H["Nf"].b, cst.b], w=[bbn], out=btn[:, 0:128], in_=H["Nf"][:],
                             identity=ident)
                        k.op("dve", "tensor_tensor", r=[bbn, cst.b], w=[H["A0T"].b], out=H["A0T"][:],
                             in0=btn[:, 0:128], in1=cst[:, C_MBD:C_MBD + 128], op=ALU.mult)
                        k.op("dve", "tensor_tensor", r=[bbn, cst.b], w=[H["NofT"].b], out=H["NofT"][:],
                             in0=btn[:, 0:128], in1=cst[:, C_MNB:C_MNB + 128], op=ALU.mult)
                        k.op("dve", "tensor_tensor", r=[cst.b, H["A0"].b], w=[H["P0"].b], out=H["P0"][:], in0=ident,
                             in1=H["A0"][:], op=ALU.subtract)
                        A, AT, Pc = H["A0"], H["A0T"], H["P0"]
                        An, ATn, Pn = H["A1"], H["A1T"], H["P1"]
                        for lev in range(5):
                            last = (lev == 4)
                            b1, bb1 = nb()
                            yield
                            k.op("pe", "matmul", r=[A.b, AT.b], w=[bb1], out=b1[:, 0:128], lhsT=A[:], rhs=AT[:],
                                 start=True, stop=True)
                            if not last:
                                b2, bb2 = nb()
                                k.op("pe", "matmul", r=[A.b, AT.b], w=[bb2], out=b2[:, 0:128], lhsT=AT[:], rhs=A[:],
                                     start=True, stop=True)
                            k.op("act", "activation", r=[bb1], w=[ATn.b], out=ATn[:], in_=b1[:, 0:128], func=AF.Copy)
                            if not last:
                                k.op("act", "activation", r=[bb2], w=[An.b], out=An[:], in_=b2[:, 0:128],
                                     func=AF.Copy)
                            b3, bb3 = nb()
                            yield
                            k.op("pe", "matmul", r=[ATn.b, Pc.b], w=[bb3], out=b3[:, 0:128], lhsT=ATn[:], rhs=Pc[:],
                                 start=True, stop=True)
                            k.op("dve", "tensor_tensor", r=[bb3, Pc.b], w=[Pn.b], out=Pn[:], in0=b3[:, 0:128],
                                 in1=Pc[:], op=ALU.add)
                            A, An = An, A
                            AT, ATn = ATn, AT
                            Pc, Pn = Pn, Pc
                        X = Pc
                        b1, bb1 = nb()
                        yield
                        k.op("pe", "transpose", r=[X.b, cst.b], w=[bb1], out=b1[:, 0:128], in_=X[:], identity=ident)
                        k.op("act", "activation", r=[bb1], w=[H["XT"].b], out=H["XT"][:], in_=b1[:, 0:128],
                             func=AF.Copy)
                        b2, bb2 = nb()
                        yield
                        k.op("pe", "matmul", r=[H["NofT"].b, X.b], w=[bb2], out=b2[:, 0:128], lhsT=H["NofT"][:],
                             rhs=X[:], start=True, stop=True)
                        k.op("act", "activation", r=[bb2], w=[H["Y"].b], out=H["Y"][:], in_=b2[:, 0:128],
                             func=AF.Copy)
                        b3, bb3 = nb()
                        yield
                        k.op("pe", "matmul", r=[H["XT"].b, H["Y"].b], w=[bb3], out=b3[:, 0:128], lhsT=H["XT"][:],
                             rhs=H["Y"][:], start=True, stop=True)
                        k.op("dve", "tensor_tensor", r=[X.b, bb3], w=[H["T2T"].b], out=H["T2T"][:], in0=X[:],
                             in1=b3[:, 0:128], op=ALU.subtract)

                        b4, bb4 = nb()
                        yield
                        k.op("pe", "matmul", r=[knb[par].b, Sb[i].b], w=[bb4], out=b4[:, 0:128], lhsT=knb[par][:, cs],
                             rhs=Sb[i][:], start=True, stop=True)
                        k.op("dve", "scalar_tensor_tensor", r=[bb4, G["egc"].b, vtok[par].b], w=[H["r2n"].b],
                             out=H["r2n"][:], in0=b4[:, 0:128], scalar=G["egc"][:, i:i + 1], in1=vtok[par][:, cs],
                             op0=ALU.mult, op1=ALU.subtract)
                        b5, bb5 = nb()
                        yield
                        k.op("pe", "matmul", r=[H["T2T"].b, H["r2n"].b], w=[bb5], out=b5[:, 0:128], lhsT=H["T2T"][:],
                             rhs=H["r2n"][:], start=True, stop=True)
                        k.op("act", "activation", r=[bb5, G["nbeta"].b], w=[H["vnw"].b], out=H["vnw"][:],
                             in_=b5[:, 0:128], func=AF.Copy, scale=G["nbeta"][:, i:i + 1])
                        b6, bb6 = nb()
                        yield
                        k.op("pe", "matmul", r=[Sb[i].b, H["qd"].b], w=[bb6], sig=False, out=b6[:, 0:128],
                             lhsT=Sb[i][:], rhs=H["qd"][:], start=True, stop=False)
                        yield
                        k.op("pe", "matmul", r=[H["vnw"].b, H["PT"].b], w=[bb6], out=b6[:, 0:128], lhsT=H["vnw"][:],
                             rhs=H["PT"][:], start=False, stop=True)
                        k.op("act", "activation", r=[bb6], w=[oT[par].b], out=oT[par][:, cs], in_=b6[:, 0:128],
                             func=AF.Copy)
                        b7, bb7 = nb()
                        yield
                        k.op("pe", "matmul", r=[H["ktl"].b, H["vnw"].b], w=[bb7], out=b7[:, 0:128], lhsT=H["ktl"][:],
                             rhs=H["vnw"][:], start=True, stop=True)
                        k.op("dve", "scalar_tensor_tensor", r=[Sf[i].b, H["egb"].b, bb7], w=[Sf[i].b], out=Sf[i][:],
                             in0=Sf[i][:], scalar=H["egb"][:, 127:128], in1=b7[:, 0:128], op0=ALU.mult, op1=ALU.add)
                        k.op("act", "activation", r=[Sf[i].b], w=[Sb[i].b], out=Sb[i][:], in_=Sf[i][:], func=AF.Copy)


                def head_post(i):
                    par = i % 2
                    k.op("act", "activation", r=[oT[par].b], w=[sqb.b], out=sqb[:], in_=oT[par][:], func=AF.Square)
                    bt2, bb2 = nb()
                    k.op("pe", "matmul", r=[oneb.b, sqb.b], w=[bb2], out=bt2[:, 0:512], lhsT=oneb[:], rhs=sqb[:],
                         start=True, stop=True)
                    k.op("act", "activation", r=[bb2], w=[rr.b], out=rr[:], in_=bt2[:, 0:512], func=AF.Ln, bias=128.0 * EPS)
                    k.op("act", "activation", r=[rr.b], w=[rr.b], out=rr[:], in_=rr[:], func=AF.Exp, scale=-0.5)
                    k.op("dve", "tensor_tensor", r=[oT[par].b, rr.b], w=[rr.b], out=rr[:], in0=oT[par][:], in1=rr[:],
                         op=ALU.mult)
                    k.op("dve", "scalar_tensor_tensor", r=[rr.b, dnw2.b, z2[par].b], w=[mixt[par].b],
                         out=mixt[par][:], in0=rr[:], scalar=dnw2[:, 0:1], in1=z2[par][:], op0=ALU.mult,
                         op1=ALU.mult)
                    k.dma("sp", mix_in_l[p].ap()[i * 128:(i + 1) * 128, :], mixt[par][:],
                          r=[mixt[par].b], w=[b_mixin_l[p]], sem_from=mixt[par].b)


                for pair in (() if "gdn" in skip else ((0, 1), (2, 3))):
                    for i in pair:
                        head_pre(i)
                    for j in range(4):
                        gens = [gdn_block(i, j) for i in pair]
                        while gens:
                            for g_ in list(gens):
                                try:
                                    next(g_)
                                except StopIteration:
                                    gens.remove(g_)
                    for i in pair:
                        head_post(i)

                if "swa" in skip:
                    continue
                for h in range(4):
                    bt, bb = inproj(p, 16 + h)
                    k.op("act", "activation", r=[bb], w=[sraw.b], out=sraw[:], in_=bt[:, 0:512], func=AF.Copy)
                    k.op("act", "activation", r=[sraw.b], w=[sqb.b], out=sqb[:], in_=sraw[:], func=AF.Square)
                    bt2, bb2 = nb()
                    k.op("pe", "matmul", r=[oneb.b, sqb.b], w=[bb2], out=bt2[:, 0:512], lhsT=oneb[:], rhs=sqb[:],
                         start=True, stop=True)
                    k.op("act", "activation", r=[bb2], w=[rr.b], out=rr[:], in_=bt2[:, 0:512], func=AF.Ln, bias=128.0 * EPS)
                    k.op("act", "activation", r=[rr.b], w=[rr.b], out=rr[:], in_=rr[:], func=AF.Exp, scale=-0.5)
                    k.op("dve", "scalar_tensor_tensor", r=[sraw.b, prt.b, rr.b], w=[sqn[h].b], out=sqn[h][:],
                         in0=sraw[:], scalar=prt[:, P_QNW:P_QNW + 1], in1=rr[:], op0=ALU.mult, op1=ALU.mult)
                bt, bb = inproj(p, 20)
                k.op("act", "activation", r=[bb], w=[sraw.b], out=sraw[:], in_=bt[:, 0:512], func=AF.Copy)
                k.op("act", "activation", r=[sraw.b], w=[sqb.b], out=sqb[:], in_=sraw[:], func=AF.Square)
                bt2, bb2 = nb()
                k.op("pe", "matmul", r=[oneb.b, sqb.b], w=[bb2], out=bt2[:, 0:512], lhsT=oneb[:], rhs=sqb[:],
                     start=True, stop=True)
                k.op("act", "activation", r=[bb2], w=[rr.b], out=rr[:], in_=bt2[:, 0:512], func=AF.Ln, bias=128.0 * EPS)
                k.op("act", "activation", r=[rr.b], w=[rr.b], out=rr[:], in_=rr[:], func=AF.Exp, scale=-0.5)
                if p > 0:
                    k.op("dve", "tensor_copy", r=[skn.b], w=[skn.b], out=skn[:, 0:128], in_=skn[:, 512:640])
                    k.op("dve", "tensor_copy", r=[svt.b], w=[svt.b], out=svt[:, 0, :], in_=svt[:, 4, :])
                k.op("dve", "scalar_tensor_tensor", r=[sraw.b, knw2.b, rr.b], w=[skn.b], out=skn[:, 128:640],
                     in0=sraw[:], scalar=knw2[:, 0:1], in1=rr[:], op0=ALU.mult, op1=ALU.mult)
                bt, bb = inproj(p, 21)
                k.op("act", "activation", r=[bb], w=[svb.b], out=svb[:], in_=bt[:, 0:512], func=AF.Copy)
                bt2, bb2 = nb()
                btb = bt2[:].bitcast(BF16)
                for j in range(4):
                    k.op("pe", "transpose", r=[svb.b, idb.b], w=[bb2], sig=(j == 3),
                         out=btb[:, j * 128:(j + 1) * 128], in_=svb[:, j * 128:(j + 1) * 128], identity=idb[:])
                k.op("act", "activation", r=[bb2], w=[svt.b], out=svt[:, 1:5, :],
                     in_=btb[:, 0:512].rearrange("p (a b) -> p a b", a=4), func=AF.Copy)
                for h in range(4):
                    sm = smix[h % 2]
                    for j in range(4):
                        W = sw[swcount[0] % 2]
                        swcount[0] += 1
                        gblk = p * 4 + j
                        cs = slice(j * 128, (j + 1) * 128)
                        has_prev = gblk > 0
                        b1, bb1 = nb()
                        k.op("pe", "matmul", r=[skn.b, sqn[h].b], w=[bb1], out=b1[:, 0:128],
                             lhsT=skn[:, 128 + j * 128:256 + j * 128], rhs=sqn[h][:, cs], start=True, stop=True)
                        k.op("dve", "tensor_tensor", r=[bb1, cst.b], w=[W["tc"].b], out=W["tc"][:], in0=b1[:, 0:128],
                             in1=cst[:, C_AL + (2 * h) * 128:C_AL + (2 * h + 1) * 128], op=ALU.add)
                        k.op("act", "activation", r=[W["tc"].b], w=[W["pc"].b], out=W["pc"][:], in_=W["tc"][:],
                             func=AF.Exp)
                        if has_prev:
                            b2, bb2 = nb()
                            k.op("pe", "matmul", r=[skn.b, sqn[h].b], w=[bb2], out=b2[:, 0:128],
                                 lhsT=skn[:, j * 128:128 + j * 128], rhs=sqn[h][:, cs], start=True, stop=True)
                            k.op("dve", "tensor_tensor", r=[bb2, cst.b], w=[W["tp"].b], out=W["tp"][:],
                                 in0=b2[:, 0:128], in1=cst[:, C_AL + (2 * h + 1) * 128:C_AL + (2 * h + 2) * 128],
                                 op=ALU.add)
                            k.op("act", "activation", r=[W["tp"].b], w=[W["pp"].b], out=W["pp"][:], in_=W["tp"][:],
                                 func=AF.Exp)
                        b3, bb3 = nb()
                        if has_prev:
                            k.op("pe", "matmul", r=[svt.b, W["pp"].b], w=[bb3], sig=False, out=b3[:, 0:128],
                                 lhsT=svt[:, j, :], rhs=W["pp"][:], start=True, stop=False)
                        k.op("pe", "matmul", r=[svt.b, W["pc"].b], w=[bb3], out=b3[:, 0:128], lhsT=svt[:, j + 1, :],
                             rhs=W["pc"][:], start=(not has_prev), stop=True)
                        b4, bb4 = nb()
                        if has_prev:
                            k.op("pe", "matmul", r=[oneb.b, W["pp"].b], w=[bb4], sig=False, out=b4[:, 0:128],
                                 lhsT=oneb[:], rhs=W["pp"][:], start=True, stop=False)
                        k.op("pe", "matmul", r=[oneb.b, W["pc"].b], w=[bb4], out=b4[:, 0:128], lhsT=oneb[:],
                             rhs=W["pc"][:], start=(not has_prev), stop=True)
                        k.op("act", "activation", r=[bb4, esink.b], w=[W["rd"].b], out=W["rd"][:], in_=b4[:, 0:128],
                             func=AF.Ln, bias=esink[:, h:h + 1])
                        k.op("act", "activation", r=[W["rd"].b], w=[W["rd"].b], out=W["rd"][:], in_=W["rd"][:],
                             func=AF.Exp, scale=-1.0)
                        k.op("dve", "tensor_tensor", r=[bb3, W["rd"].b], w=[sm.b], out=sm[:, cs], in0=b3[:, 0:128],
                             in1=W["rd"][:], op=ALU.mult)
                    k.dma("sp", mix_in_l[p].ap()[512 + h * 128:512 + (h + 1) * 128, :], sm[:],
                          r=[sm.b], w=[b_mixin_l[p]], sem_from=sm.b)
                if "cc" not in skip:
                    k.wait_all("pool", [b_mixin_l[p]])
                    ccs = k.newsem("cc%d" % p)

                    def cc(e, sems, p=p, ccs=ccs):
                        e.collective_compute("AllGather", ALU.bypass, replica_groups=[[0, 1, 2, 3], [4, 5, 6, 7]],
                                             ins=[mix_in_l[p].ap().opt()],
                                             outs=[mix_all_l[p].ap().opt()]).then_inc(sems[ccs])
                    k.raw("pool", cc)
                    b_mixall_l[p].writes = {ccs: 1}

            conv_issue(len(conv_jobs))
            b_wconv.writes = {convsem: conv_state["cnt"]}
            allb = [t.b for t in p1_tiles] + hTb
            allb += [bb for _, bb in banks] + dbg_bufs
            for e in ("pe", "act", "dve", "pool", "sp"):
                k.wait_all(e, allb)
            p1_bufs = allb

        with contextlib.ExitStack() as st:
            T = lambda n, s, d: Tl(arena, n, s, d)
            arena.off = arena_mark
            fnwbc = T("fnwbc_s", [128, D], F32)
            selt = T("selt", [128, 4], F32)
            h2T = T("h2T", [128, 32, 512], BF16)
            h2b = [Buf("h2T%d" % m) for m in range(4)]
            big = T("big", [128, NFB * 512], BF16)
            bigb = [Buf("big%d" % f) for f in range(NFB)]
            NS = 5
            ws = [T("ws%d" % i, [128, D], BF16) for i in range(NS)]
            cand = [T("cand%d" % i, [128, 2, 512], BF16) for i in range(2)]
            xsl = [T("xsl%d" % i, [128, 512], F32) for i in range(3)]
            x2f = [T("x2f%d" % i, [128, 512], F32) for i in range(3)]
            sg = [T("sg%d" % i, [128, 512], F32) for i in range(2)]
            ssq8 = [T("ssq8_%d" % m, [128, 8], F32) for m in range(4)]
            rs2 = [T("rs2_%d" % m, [128, 1], F32) for m in range(4)]
            junk = T("junk", [128, 512], BF16)
            for t_ in [fnwbc, selt, h2T, big] + ws + cand + xsl + x2f + sg + ssq8 + rs2 + [junk]:
                t_.b.reads = {}
            k.dma("sp", fnwbc[:], fnwbc_d, w=[fnwbc.b])
            k.dma("sp", selt[:], sel_d, w=[selt.b])
            wcount = [0]
            xcount = [0]

            NDG = (NFB + 7) // 8
            WL = []
            for n_ in range(8):
                for a_ in range(4):
                    WL.append(("full", wo_b.ap()[n_][:, a_ * 4096:(a_ + 1) * 4096]))
            for fb_ in range(NFB):
                WL.append(("full", wg_b.ap()[fb_]))
                WL.append(("full", wu_b.ap()[fb_]))
            for q_ in range(4):
                for n2_ in range(2):
                    for fg_ in range(NDG):
                        WL.append(("dn", (q_, n2_, fg_, min(8, NFB - fg_ * 8))))
            I_GU = 32
            I_DN = 32 + 2 * NFB
            wst2 = {"issued": 0, "base": 0}

            def wensure(upto):
                while wst2["issued"] <= min(upto, len(WL) - 1):
                    i_ = wst2["issued"]
                    s_ = ws[(wst2["base"] + i_) % NS]
                    kind, arg = WL[i_]
                    if kind == "full":
                        k.dma("pool", s_[:], arg, r=[b_wconv], w=[s_.b])
                    else:
                        q_, n2_, fg_, nf_ = arg
                        src_ = wd_b.ap()[q_][:, fg_ * 8192:fg_ * 8192 + nf_ * 1024].rearrange(
                            "p (f c) -> p f c", c=1024)[:, :, n2_ * 512:(n2_ + 1) * 512]
                        dst_ = s_[:, 0:nf_ * 512].rearrange("p (f c) -> p f c", c=512)
                        k.dma("pool", dst_, src_, r=[b_wconv], w=[s_.b])
                    wst2["issued"] += 1

            def wget(i_):
                wensure(i_ + NS - 1)
                return ws[(wst2["base"] + i_) % NS]

            mixT = big[:, 0:32 * 512].rearrange("p (a b) -> p a b", a=32)
            mixb = bigb[0:32]
            x2b = big[:, 32 * 512:64 * 512].rearrange("p (m c) -> p m c", m=4)
            outb = {}

            for p in range(0 if "p2" in skip else NP2):
                t0 = p * 512
                wst2["base"] = p * len(WL)
                wst2["issued"] = 0
                for a in range(16):
                    dst = mixT[:, a * 2:(a + 1) * 2, :]
                    dbufs = mixb[a * 2:(a + 1) * 2]
                    for j in range(4):
                        c = cand[(a * 4 + j) % 2]
                        P_ = j * NP2 + p
                        k.dma("sp", c[:], mix_all_l[P_].ap().rearrange("(a q) t -> q a t", q=128)[:, a * 2:(a + 1) * 2, :],
                              r=[b_mixall_l[P_]], w=[c.b])
                        if j == 0:
                            k.op("dve", "tensor_scalar", r=[c.b, selt.b], w=dbufs, out=dst, in0=c[:],
                                 scalar1=selt[:, 0:1], scalar2=None, op0=ALU.mult)
                        else:
                            k.op("dve", "scalar_tensor_tensor", r=[c.b, selt.b] + dbufs, w=dbufs, out=dst, in0=c[:],
                                 scalar=selt[:, j:j + 1], in1=dst, op0=ALU.mult, op1=ALU.add)
                for n in range(8):
                    bk = [nb() for _ in range(4)]
                    for a in range(4):
                        s = wget(n * 4 + a)
                        for m in range(4):
                            for kk in range(8):
                                kc = a * 8 + kk
                                k.op("pe", "matmul", r=[s.b, mixb[kc]], w=[bk[m][1]],
                                     sig=(kk == 7 and (a == 3 or m == 3)), out=bk[m][0][:, 0:512],
                                     lhsT=mixT[:, kc, m * 128:(m + 1) * 128], rhs=s[:, kk * 512:(kk + 1) * 512],
                                     start=(a == 0 and kk == 0), stop=(a == 3 and kk == 7))
                    for m in range(4):
                        xi = xsl[xcount[0] % 3]
                        xo = x2f[xcount[0] % 3]
                        xcount[0] += 1
                        rs = slice(t0 + m * 128, t0 + (m + 1) * 128)
                        cs = slice(n * 512, (n + 1) * 512)
                        k.dma("sp", xi[:], x2tok[rs, cs], w=[xi.b])
                        k.op("dve", "tensor_tensor", r=[bk[m][1], xi.b], w=[xo.b], out=xo[:], in0=bk[m][0][:, 0:512],
                             in1=xi[:], op=ALU.add)
                        k.op("act", "activation", r=[xo.b], w=[junk.b, ssq8[m].b], out=junk[:], in_=xo[:],
                             func=AF.Square, accum_out=ssq8[m][:, n:n + 1])
                        k.op("act", "activation", r=[xo.b], w=[bigb[32 + m * 8 + n]], out=x2b[:, m, cs], in_=xo[:], func=AF.Copy)
                        ob = Buf("o_%d_%d" % (m, n))
                        outb[(m, n)] = ob
                        k.dma("sp", out_d[rs, cs], xo[:], r=[xo.b], w=[ob], sem_from=xo.b)
                for m in range(4):
                    xb_bufs = bigb[32 + m * 8:40 + m * 8]
                    k.op("dve", "tensor_reduce", r=[ssq8[m].b], w=[rs2[m].b], out=rs2[m][:], in_=ssq8[m][:],
                         axis=mybir.AxisListType.X, op=ALU.add)
                    k.op("act", "activation", r=[rs2[m].b], w=[rs2[m].b], out=rs2[m][:], in_=rs2[m][:], func=AF.Ln,
                         scale=1.0 / D, bias=EPS)
                    k.op("act", "activation", r=[rs2[m].b], w=[rs2[m].b], out=rs2[m][:], in_=rs2[m][:], func=AF.Exp,
                         scale=-0.5)
                    k.op("dve", "scalar_tensor_tensor", r=xb_bufs + [rs2[m].b, fnwbc.b], w=xb_bufs, out=x2b[:, m, :],
                         in0=x2b[:, m, :], scalar=rs2[m][:, 0:1], in1=fnwbc[:], op0=ALU.mult, op1=ALU.mult)
                    for kq in range(8):
                        bt, bb = nb()
                        btb = bt[:].bitcast(BF16)
                        for j in range(4):
                            kk = kq * 4 + j
                            k.op("pe", "transpose", r=xb_bufs + [idb.b], w=[bb], sig=(j == 3),
                                 out=btb[:, j * 128:(j + 1) * 128], in_=x2b[:, m, kk * 128:(kk + 1) * 128],
                                 identity=idb[:])
                        src = btb[:, 0:512].rearrange("p (a b) -> p a b", a=4)
                        dst = h2T[:, kq * 4:(kq + 1) * 4, m * 128:(m + 1) * 128]
                        if kq % 2 == 0:
                            k.op("act", "activation", r=[bb], w=[h2b[m]], out=dst, in_=src, func=AF.Copy)
                        else:
                            k.op("dve", "tensor_copy", r=[bb], w=[h2b[m]], out=dst, in_=src)
                for fb in range(NFB):
                    s_g = wget(I_GU + 2 * fb)
                    bg, bbg = nb()
                    for kk in range(32):
                        k.op("pe", "matmul", r=[s_g.b] + h2b, w=[bbg], sig=(kk == 31), out=bg[:, 0:512],
                             lhsT=s_g[:, kk * 128:(kk + 1) * 128], rhs=h2T[:, kk, :], start=(kk == 0),
                             stop=(kk == 31))
                    s_u = wget(I_GU + 2 * fb + 1)
                    bu, bbu = nb()
                    for kk in range(32):
                        k.op("pe", "matmul", r=[s_u.b] + h2b, w=[bbu], sig=(kk == 31), out=bu[:, 0:512],
                             lhsT=s_u[:, kk * 128:(kk + 1) * 128], rhs=h2T[:, kk, :], start=(kk == 0),
                             stop=(kk == 31))
                    sgt = sg[fb % 2]
                    k.op("act", "activation", r=[bbg], w=[sgt.b], out=sgt[:], in_=bg[:, 0:512], func=AF.Silu)
                    k.op("dve", "tensor_tensor", r=[sgt.b, bbu], w=[bigb[fb]], out=big[:, fb * 512:(fb + 1) * 512],
                         in0=sgt[:], in1=bu[:, 0:512], op=ALU.mult)
                for q in range(4):
                    for n2 in range(2):
                        u = q * 2 + n2
                        n = u
                        bk = [nb() for _ in range(4)]
                        for fg in range(NDG):
                            nf = min(8, NFB - fg * 8)
                            s = wget(I_DN + u * NDG + fg)
                            for f in range(nf):
                                fb = fg * 8 + f
                                for m in range(4):
                                    k.op("pe", "matmul", r=[s.b, bigb[fb]], w=[bk[m][1]],
                                         sig=(fb == NFB - 1 or (f == nf - 1 and m == 3)),
                                         out=bk[m][0][:, 0:512],
                                         lhsT=big[:, fb * 512 + m * 128:fb * 512 + (m + 1) * 128],
                                         rhs=s[:, f * 512:(f + 1) * 512], start=(fb == 0), stop=(fb == NFB - 1))
                        for m in range(4):
                            xi = xsl[xcount[0] % 3]
                            xo = x2f[xcount[0] % 3]
                            xcount[0] += 1
                            rs = slice(t0 + m * 128, t0 + (m + 1) * 128)
                            cs = slice(n * 512, (n + 1) * 512)
                            ob = outb[(m, n)]
                            k.dma("sp", xi[:], out_d[rs, cs], r=[ob], w=[xi.b])
                            k.op("dve", "tensor_tensor", r=[bk[m][1], xi.b], w=[xo.b], out=xo[:],
                                 in0=bk[m][0][:, 0:512], in1=xi[:], op=ALU.add)
                            k.dma("sp", out_d[rs, cs], xo[:], r=[xo.b], w=[ob], sem_from=xo.b)
            k.wait_all("sp", list(outb.values()) + [t_.b for t_ in x2f] + dbg_bufs)
        k.emit()
    return nc


_CACHE = {}


def _consts(g):
    c = np.zeros((128, NCONST), np.float32)
    i = np.arange(128)
    S, Cc = np.meshgrid(i, i, indexing="ij")
    c[:, C_ID:C_ID + 128] = (S == Cc)
    c[:, C_ONE:C_ONE + 128] = 1.0
    c[:, C_UIN:C_UIN + 128] = (S <= Cc)
    c[:, C_LST:C_LST + 128] = (S > Cc)
    c[:, C_MST:C_MST + 128] = (Cc > S)
    bd = (S // 64 == Cc // 64)
    c[:, C_MBD:C_MBD + 128] = bd
    c[:, C_MNB:C_MNB + 128] = ~bd
    for h in range(4):
        slope = 2.0 ** (-8.0 * (4 * g + h + 1) / 16.0)
        kk, qq = S, Cc
        cur = np.where(qq >= kk, -slope * (qq - kk), -30000.0)
        prv = np.where(kk > qq, -slope * (qq + 128 - kk), -30000.0)
        c[:, C_AL + (2 * h) * 128:C_AL + (2 * h + 1) * 128] = cur
        c[:, C_AL + (2 * h + 1) * 128:C_AL + (2 * h + 2) * 128] = prv
    return c


def _prep_shared(inp):
    w_out, w_gate, w_up, w_down = inp["w_out"][0], inp["w_gate"][0], inp["w_up"][0], inp["w_down"][0]
    perm = np.zeros(4096, np.int64)
    for r in range(4):
        for loc in range(1024):
            hh, d = (loc % 512) // 128, loc % 128
            perm[r * 1024 + loc] = (0 if loc < 512 else 2048) + (4 * r + hh) * 128 + d
    wo = w_out[perm, :]
    sh = {}
    sh["wo_t"] = np.ascontiguousarray(wo.reshape(32, 128, 8, 512).transpose(2, 1, 0, 3)).reshape(8, 128, 32 * 512)
    sh["wg_t"] = np.ascontiguousarray(w_gate.reshape(32, 128, NFB, 128).transpose(2, 1, 0, 3)).reshape(NFB, 128, D)
    sh["wu_t"] = np.ascontiguousarray(w_up.reshape(32, 128, NFB, 128).transpose(2, 1, 0, 3)).reshape(NFB, 128, D)
    sh["wd_t"] = np.ascontiguousarray(w_down.reshape(NFB, 128, 4, 1024).transpose(2, 1, 0, 3)).reshape(4, 128, NFB * 1024)
    sh["anwbc"] = np.ascontiguousarray(np.broadcast_to(inp["attn_norm_w"][0][None, :], (128, D)))
    sh["fnwbc"] = np.ascontiguousarray(np.broadcast_to(inp["ffn_norm_w"][0][None, :], (128, D)))
    return sh


def _prep_group(inp, g):
    w_in = inp["w_in"][0]
    cols = []
    for t_ in range(4):
        for i in range(4):
            cols.append(t_ * 2048 + (4 * g + i) * 128)
    for h in range(4):
        cols.append(8224 + (4 * g + h) * 128)
    cols.append(10272 + g * 128)
    cols.append(10784 + g * 128)
    w1t = np.empty((NCB, 128, D), np.float32)
    for cb, c0 in enumerate(cols):
        w1t[cb] = w_in[:, c0:c0 + 128].reshape(32, 128, 128).transpose(1, 0, 2).reshape(128, D)
    abcols = [8192 + 4 * g + i for i in range(4)] + [8208 + 4 * g + i for i in range(4)]
    wab = np.ascontiguousarray(w_in[:, abcols].reshape(32, 128, 8).transpose(1, 0, 2)).reshape(128, 256)
    prm = np.zeros((128, NPRM), np.float32)
    cw = inp["conv_w"][0]
    for t_ in range(3):
        for i in range(4):
            ch = t_ * 2048 + (4 * g + i) * 128
            for j in range(4):
                prm[:, P_CW + (t_ * 4 + i) * 4 + j] = cw[j, ch:ch + 128]
    prm[:, P_ALOG:P_ALOG + 4] = inp["a_log"][0][4 * g:4 * g + 4][None, :]
    prm[:, P_DTB:P_DTB + 4] = inp["dt_bias"][0][4 * g:4 * g + 4][None, :]
    prm[:, P_DNW] = inp["dn_norm_w"][0]
    prm[:, P_QNW] = inp["q_norm_w"][0]
    prm[:, P_KNW] = inp["k_norm_w"][0]
    prm[:, P_SNK:P_SNK + 4] = inp["sinks"][0][4 * g:4 * g + 4][None, :]
    sel = np.zeros((128, 4), np.float32)
    sel[:, g] = 1.0
    return {"w1t": w1t, "wab": wab, "prm": prm, "consts": _consts(g), "sel": sel}


def make_in_maps(inp, SEQ):
    x = inp["x"]
    sh = _prep_shared(inp)
    TOK2 = SEQ // 4
    maps = []
    grp = [_prep_group(inp, g) for g in range(4)]
    for c in range(8):
        b, g = c // 4, c % 4
        m = dict(sh)
        m.update(grp[g])
        m["xb"] = np.ascontiguousarray(x[b, :SEQ])
        m["x2tok"] = np.ascontiguousarray(x[b, g * TOK2:(g + 1) * TOK2])
        maps.append(m)
    return maps


def kernel(**inputs):
    inp = {k_: np.asarray(v) for k_, v in inputs.items()}
    SEQ = inp["x"].shape[1]
    if SEQ not in _CACHE:
        _CACHE[SEQ] = build(SEQ)
    nc = _CACHE[SEQ]
    maps = make_in_maps(inp, SEQ)
    res = run_bass_kernel_spmd(nc, maps, core_ids=list(range(8)))
    TOK2 = SEQ // 4
    out = np.empty((2, SEQ, D), np.float32)
    for c in range(8):
        b, g = c // 4, c % 4
        out[b, g * TOK2:(g + 1) * TOK2] = np.asarray(res.results[c]["out"])
    return out
```
